# Optimizing a Trainium2 kernel written in Bass

```python
import jax, jax.numpy as jnp
from jax import lax
import numpy as np

D_MODEL = 1024
BATCH = 4
SEQ = 8192
DEPTH = 1

GLA_HEADS = 4
GLA_DK = D_MODEL // 16
GLA_DV = D_MODEL // 8
GLA_WIDTH = GLA_HEADS * GLA_DV
GLA_GATE_RANK = 16
GLA_GATE_TAU = 16.0
GLA_CHUNK = 64
SWA_Q_HEADS = 8
SWA_KV_HEADS = 2
SWA_HEAD_DIM = D_MODEL // 16
SWA_WIDTH = SWA_Q_HEADS * SWA_HEAD_DIM
WINDOW = 128
SWA_BLOCK = 128
ROPE_THETA = 10000.0
RMS_EPS = 1e-6
D_MIX = GLA_WIDTH + SWA_WIDTH
COL_SIZES = (
    GLA_HEADS * GLA_DK,
    GLA_HEADS * GLA_DK,
    GLA_WIDTH,
    GLA_GATE_RANK,
    GLA_WIDTH,
    SWA_WIDTH,
    SWA_KV_HEADS * SWA_HEAD_DIM,
    SWA_KV_HEADS * SWA_HEAD_DIM,
    SWA_WIDTH,
)
D_IN = (4 * GLA_HEADS * GLA_DK // 2) + 2 * GLA_WIDTH + GLA_GATE_RANK + 2 * SWA_WIDTH + 2 * SWA_KV_HEADS * SWA_HEAD_DIM

kernel_name = "hymba_gla_swa_sink_adaln"


def rmsnorm(x, g):
    xf = x.astype(jnp.float32)
    y = xf * lax.rsqrt(jnp.mean(xf * xf, axis=-1, keepdims=True) + RMS_EPS)
    return (y * g.astype(jnp.float32)).astype(x.dtype)


def split_cols(t, sizes):
    outs = []
    start = 0
    for s in sizes:
        outs.append(t[..., start:start + s])
        start += s
    return outs


def rope(t, positions):
    hd = t.shape[-1]
    inv_freq = 1.0 / (ROPE_THETA ** (jnp.arange(0, hd, 2, dtype=jnp.float32) / hd))
    ang = positions.astype(jnp.float32)[..., None] * inv_freq
    cos = jnp.cos(ang)[:, :, None, :]
    sin = jnp.sin(ang)[:, :, None, :]
    tf = t.astype(jnp.float32)
    t1, t2 = tf[..., : hd // 2], tf[..., hd // 2:]
    return jnp.concatenate([t1 * cos - t2 * sin, t2 * cos + t1 * sin], axis=-1).astype(t.dtype)


def gla_chunked(q, k, v, log_a):
    B, S, H, dk = q.shape
    dv = v.shape[-1]
    C = GLA_CHUNK
    N = S // C

    def chunks(t):
        return t.reshape(B, N, C, H, t.shape[-1]).transpose(0, 3, 1, 2, 4).astype(jnp.float32)

    qc = chunks(q) * (dk ** -0.5)
    kc = chunks(k)
    vc = chunks(v)
    b = jnp.cumsum(chunks(log_a), axis=3)
    b_last = b[:, :, :, -1:, :]
    q_d = qc * jnp.exp(b)
    k_d = kc * jnp.exp(-b)
    k_tail = kc * jnp.exp(b_last - b)
    causal = jnp.tril(jnp.ones((C, C), dtype=bool))
    scores = jnp.einsum('bhnid,bhnjd->bhnij', q_d, k_d)
    scores = jnp.where(causal, scores, 0.0)
    o_intra = jnp.einsum('bhnij,bhnjv->bhniv', scores, vc)
    u = jnp.einsum('bhncd,bhncv->bhndv', k_tail, vc)
    decay = jnp.exp(b_last[:, :, :, 0, :])

    def step(state, inp):
        dec, un = inp
        return dec[..., None] * state + un, state

    state0 = jnp.zeros((B, H, dk, dv), jnp.float32)
    _, s_prev = lax.scan(step, state0, (decay.transpose(2, 0, 1, 3), u.transpose(2, 0, 1, 3, 4)))
    s_prev = s_prev.transpose(1, 2, 0, 3, 4)
    o = o_intra + jnp.einsum('bhncd,bhndv->bhncv', q_d, s_prev)
    return o.transpose(0, 2, 3, 1, 4).reshape(B, S, H, dv)


def sliding_window_sink_attention(q, k, v, sinks):
    B, S, Hq, hd = q.shape
    Hkv = k.shape[2]
    G = Hq // Hkv
    L = SWA_BLOCK
    N = S // L
    qb = q.reshape(B, N, L, Hkv, G, hd)

    def band(t):
        tb = t.reshape(B, N, L, Hkv, hd)
        prev = jnp.concatenate([jnp.zeros_like(tb[:, :1]), tb[:, :-1]], axis=1)
        return jnp.concatenate([prev, tb], axis=2)

    kb = band(k)
    vb = band(v)
    scores = jnp.einsum('bnqhgd,bnkhd->bhgnqk', qb, kb).astype(jnp.float32) * (hd ** -0.5)
    qi = jnp.arange(L)[None, :, None]
    kj = jnp.arange(2 * L)[None, None, :]
    blk = jnp.arange(N)[:, None, None]
    dist = qi + L - kj
    valid = (dist >= 0) & (dist < WINDOW) & (blk * L + kj - L >= 0)
    scores = jnp.where(valid, scores, -jnp.inf)
    sink = jnp.broadcast_to(sinks.astype(jnp.float32).reshape(Hkv, G, 1, 1, 1), scores.shape[:-1] + (1,))
    probs = jax.nn.softmax(jnp.concatenate([scores, sink], axis=-1), axis=-1)[..., :-1]
    o = jnp.einsum('bhgnqk,bnkhd->bnqhgd', probs.astype(v.dtype), vb)
    return o.reshape(B, S, Hq * hd)


def setup_inputs(seed: int = 0) -> dict:
    key = jax.random.key(seed)
    ks = jax.random.split(key, 16)
    f32 = jnp.float32
    x = jax.random.normal(ks[0], (BATCH, SEQ, D_MODEL), f32)
    c = jax.random.normal(ks[1], (BATCH, D_MODEL), f32)
    offsets = jax.random.randint(ks[2], (BATCH, 1), 0, 4096, dtype=jnp.int32)
    positions = jnp.arange(SEQ, dtype=jnp.int32)[None, :] + offsets
    w_ada = jax.random.normal(ks[3], (DEPTH, D_MODEL, 3 * D_MODEL), f32) * (0.5 * D_MODEL ** -0.5)
    gate_bias = jnp.concatenate([jnp.zeros((2 * D_MODEL,), f32), jnp.ones((D_MODEL,), f32)])
    b_ada = gate_bias[None, :] + 0.02 * jax.random.normal(ks[4], (DEPTH, 3 * D_MODEL), f32)
    g_norm = 1.0 + 0.02 * jax.random.normal(ks[5], (DEPTH, D_MODEL), f32)
    w_in = jax.random.normal(ks[6], (DEPTH, D_MODEL, D_IN), f32) * (D_MODEL ** -0.5)
    w_decay = jax.random.normal(ks[7], (DEPTH, GLA_GATE_RANK, GLA_HEADS * GLA_DK), f32) * (GLA_GATE_RANK ** -0.5)
    b_decay = 0.1 * jax.random.normal(ks[8], (DEPTH, GLA_HEADS * GLA_DK), f32)
    g_gla_head = 1.0 + 0.02 * jax.random.normal(ks[9], (DEPTH, GLA_WIDTH), f32)
    sinks = 0.5 * jax.random.normal(ks[10], (DEPTH, SWA_Q_HEADS), f32)
    w_out = jax.random.normal(ks[11], (DEPTH, D_MIX, D_MODEL), f32) * (D_MIX ** -0.5)
    g_final = 1.0 + 0.02 * jax.random.normal(ks[12], (D_MODEL,), f32)
    return {"x": x, "c": c, "positions": positions, "w_ada": w_ada, "b_ada": b_ada,
            "g_norm": g_norm, "w_in": w_in, "w_decay": w_decay, "b_decay": b_decay,
            "g_gla_head": g_gla_head, "sinks": sinks, "w_out": w_out, "g_final": g_final}


def reference(x, c, positions, w_ada, b_ada, g_norm, w_in, w_decay, b_decay, g_gla_head, sinks, w_out, g_final):
    B, S, _ = x.shape
    for l in range(DEPTH):
        mod = jnp.dot(jax.nn.silu(c.astype(jnp.float32)), w_ada[l].astype(jnp.float32)) + b_ada[l].astype(jnp.float32)
        shift, scale, gate = jnp.split(mod, 3, axis=-1)
        h = (rmsnorm(x, g_norm[l]).astype(jnp.float32) * (1.0 + scale[:, None, :]) + shift[:, None, :]).astype(x.dtype)
        proj = jnp.einsum('bsd,de->bse', h, w_in[l])
        gq, gk, gv, ga, gz, sq, sk, sv, sz = split_cols(proj, COL_SIZES)
        z = jnp.einsum('bsr,rk->bsk', ga, w_decay[l]) + b_decay[l]
        log_a = jax.nn.log_sigmoid(z.astype(jnp.float32)) / GLA_GATE_TAU
        o_gla = gla_chunked(gq.reshape(B, S, GLA_HEADS, GLA_DK), gk.reshape(B, S, GLA_HEADS, GLA_DK),
                            gv.reshape(B, S, GLA_HEADS, GLA_DV), log_a.reshape(B, S, GLA_HEADS, GLA_DK))
        o_gla = rmsnorm(o_gla, g_gla_head[l].reshape(GLA_HEADS, GLA_DV)).reshape(B, S, GLA_WIDTH)
        o_gla = (o_gla * jax.nn.silu(gz.astype(jnp.float32))).astype(x.dtype)
        q = rope(sq.reshape(B, S, SWA_Q_HEADS, SWA_HEAD_DIM), positions)
        k = rope(sk.reshape(B, S, SWA_KV_HEADS, SWA_HEAD_DIM), positions)
        v = sv.reshape(B, S, SWA_KV_HEADS, SWA_HEAD_DIM)
        o_swa = sliding_window_sink_attention(q, k, v, sinks[l])
        o_swa = (o_swa.astype(jnp.float32) * jax.nn.silu(sz.astype(jnp.float32))).astype(x.dtype)
        y = jnp.einsum('bse,ed->bsd', jnp.concatenate([o_gla, o_swa], axis=-1), w_out[l])
        x = (x.astype(jnp.float32) + gate[:, None, :] * y.astype(jnp.float32)).astype(x.dtype)
    return rmsnorm(x, g_final)
```

```python
import math
import os as _os
from contextlib import ExitStack

import numpy as np
import concourse.bass as bass
import concourse.mybir as mybir
from concourse.bass_utils import run_bass_kernel_spmd

F32 = mybir.dt.float32
BF16 = mybir.dt.bfloat16
I32 = mybir.dt.int32
AF = mybir.ActivationFunctionType
ALU = mybir.AluOpType

D = 1024
KC = 8
DIN = 2832
EPS = 1e-6
NEG = -30000.0
SIN_S = 0.999999

O_GQ, O_GK, O_GV, O_GA, O_SK, O_SV, O_GZ, O_SQ, O_SZ = 0, 256, 512, 1024, 1040, 1168, 1296, 1808, 2320
PERM = np.concatenate([
    np.arange(0, 256), np.arange(256, 512), np.arange(512, 1024), np.arange(1024, 1040),
    np.arange(2064, 2192), np.arange(2192, 2320), np.arange(1040, 1552), np.arange(1552, 2064),
    np.arange(2320, 2832)])


class Buf:
    __slots__ = ("name", "t", "w", "r", "excl", "wread")

    def __init__(self, name, t, excl=False):
        self.name = name
        self.t = t
        self.w = None
        self.r = []
        self.excl = excl
        self.wread = False

    def __getitem__(self, k):
        return self.t[k]


class _Probe:
    def __init__(self):
        self.name = None
        self.args = ()
        self.kw = {}

    def __getattr__(self, name):
        def f(*args, **kw):
            self.name, self.args, self.kw = name, args, kw
            return self
        return f


def _est_cost(eng, fn):
    try:
        p = _Probe()
        fn(p)
        out = p.kw.get("out", p.args[0] if p.args else None)
        shp = list(out.shape)
        n = 1
        for d in shp[1:]:
            n *= int(d)
        if eng == "pe":
            if p.name == "transpose":
                return 0.08
            lhsT = p.kw.get("lhsT", p.args[1] if len(p.args) > 1 else None)
            f32 = lhsT is not None and lhsT.dtype == F32
            c = 0.012 + n / 1950.0
            return c * (4.5 if f32 else 1.0)
        if eng == "act":
            return 0.22 + n / 1150.0
        if eng == "dve":
            c = 0.12 + n / 950.0
            if p.name == "reciprocal":
                c = 0.12 + n / 160.0
            if p.name == "scalar_tensor_tensor":
                c = 0.15 + n / 850.0
            return c
        if eng == "pool":
            return 0.2 + n / 550.0
    except Exception:
        pass
    return None


class Sched:
    ENG = ("pe", "act", "dve", "pool", "sp")

    def __init__(self, self_sync=True):
        self.ops = {e: [] for e in self.ENG}
        self.cnt = {}
        self.seen = {e: {} for e in self.ENG}
        self.self_sync = self_sync
        self.off = False
        self.capture = None

    def _tickets(self, eng, reads, writes, xreads=()):
        tk = []
        for b in xreads:
            if b.w is not None and not (b.wread and b.w[0] == eng):
                tk.append(b.w)
            tk.extend(b.r)
        for b in reads:
            if b.excl:
                continue
            if b.w is not None:
                tk.append(b.w)
        for b in writes:
            if b.w is not None:
                tk.append(b.w)
            tk.extend(b.r)
        need = {}
        for (s, v) in tk:
            if s == eng and (eng == "pe" or not self.self_sync):
                continue
            if self.seen[eng].get(s, 0) < v:
                need[s] = max(need.get(s, 0), v)
        for s, v in need.items():
            self.seen[eng][s] = v
        return list(need.items())

    def replay(self, lst):
        cap, self.capture = self.capture, None
        for a in lst:
            if a[0] == "dma":
                self.dma(*a[1], **a[2])
            else:
                self.op(*a[:5])
        self.capture = cap

    def op(self, eng, fn, reads=(), writes=(), inc=True, cost=None):
        if self.off:
            return None
        if self.capture is not None:
            self.capture.append((eng, fn, list(reads), list(writes), inc, cost))
            return None
        xr = [b for b in reads if b.excl]
        waits = self._tickets(eng, reads, list(writes), xr)
        ticket = (eng, self.cnt.get(eng, 0) + 1)
        if inc:
            self.cnt[eng] = ticket[1]
        self.ops[eng].append((waits, fn, (eng, 1) if inc else None))
        for b in reads:
            b.r.append(ticket)
        for b in xr:
            b.w = ticket
            b.r = []
            b.wread = True
        for b in writes:
            b.w = ticket
            b.r = []
            b.wread = False
        return ticket

    def dma(self, q, semkey, out_ap, in_ap, reads=(), writes=(), **kw):
        if self.off:
            return None
        if self.capture is not None:
            self.capture.append(("dma", (q, semkey, out_ap, in_ap, list(reads), list(writes)), kw))
            return None
        waits = self._tickets(q, reads, writes)
        self.cnt[semkey] = self.cnt.get(semkey, 0) + 16
        ticket = (semkey, self.cnt[semkey])
        self.ops[q].append((waits, lambda e: e.dma_start(out=out_ap, in_=in_ap, **kw), (semkey, 16)))
        for b in reads:
            b.r.append(ticket)
        for b in writes:
            b.w = ticket
            b.r = []
        return ticket

    COST = {"pe": 0.2, "act": 0.6, "dve": 0.55, "pool": 0.9, "sp": 0.1}

    def schedule(self, L):
        units = []
        curu = None
        for a in L:
            if a[0] == "dma":
                q, semkey, out_ap, in_ap, reads, writes = a[1]
                nb = 4
                for d in out_ap.shape:
                    nb *= int(d)
                units.append({"eng": q, "ops": [a], "reads": list(reads), "writes": list(writes), "cost": 0.1, "lat": 2.5 if nb < 600000 else 2.0 + nb / 180e3})
                continue
            eng, fn, reads, writes, inc = a[:5]
            cost = a[5] if len(a) > 5 and a[5] is not None else None
            if cost is None:
                cost = _est_cost(eng, fn)
            if cost is None:
                cost = self.COST[eng]
            if eng == "pe":
                if curu is None:
                    curu = {"eng": "pe", "ops": [], "reads": [], "writes": [], "cost": 0.0, "lat": 0.3}
                curu["ops"].append(a)
                curu["reads"] += list(reads)
                curu["writes"] += list(writes)
                curu["cost"] += cost
                if inc:
                    units.append(curu)
                    curu = None
            else:
                units.append({"eng": eng, "ops": [a], "reads": list(reads), "writes": list(writes), "cost": cost, "lat": 0.15})
        assert curu is None
        n = len(units)
        lastw, readers = {}, {}
        deps = [set() for _ in range(n)]
        for j, u in enumerate(units):
            rs = [b for b in u["reads"] if not b.excl]
            ws = list(u["writes"]) + [b for b in u["reads"] if b.excl]
            for b in rs:
                if id(b) in lastw:
                    deps[j].add(lastw[id(b)])
            for b in ws:
                if id(b) in lastw:
                    deps[j].add(lastw[id(b)])
                for r in readers.get(id(b), ()):
                    deps[j].add(r)
            for b in rs:
                readers.setdefault(id(b), []).append(j)
            for b in ws:
                lastw[id(b)] = j
                readers[id(b)] = []
            deps[j].discard(j)
        succ = [[] for _ in range(n)]
        ndep = [len(d) for d in deps]
        for j, d in enumerate(deps):
            for i in d:
                succ[i].append(j)
        ready = {e: [] for e in self.ENG}
        depready = [0.0] * n
        done = [0.0] * n
        efree = {e: 0.0 for e in self.ENG}
        for j in range(n):
            if ndep[j] == 0:
                ready[units[j]["eng"]].append(j)
        order = []
        WIN = 1000
        nsched = 0
        scheduled = [False] * n
        oldest = 0
        while nsched < n:
            while oldest < n and scheduled[oldest]:
                oldest += 1
            best = None
            for e in self.ENG:
                cand = None
                for j in ready[e]:
                    if j > oldest + WIN:
                        continue
                    st = max(depready[j], efree[e])
                    key = (st, j)
                    if cand is None or key < cand[0]:
                        cand = (key, j)
                if cand is not None and (best is None or cand[0] < best[0]):
                    best = cand
            if best is None:
                j = min(j for e in self.ENG for j in ready[e])
                best = ((max(depready[j], efree[units[j]["eng"]]), j), j)
            (st, _), j = best
            u = units[j]
            ready[u["eng"]].remove(j)
            scheduled[j] = True
            nsched += 1
            efree[u["eng"]] = st + u["cost"]
            done[j] = st + u["cost"] + u["lat"]
            order.append(j)
            for k in succ[j]:
                ndep[k] -= 1
                depready[k] = max(depready[k], done[j])
                if ndep[k] == 0:
                    ready[units[k]["eng"]].append(k)
        cap, self.capture = self.capture, None
        for j in order:
            for a in units[j]["ops"]:
                if a[0] == "dma":
                    self.dma(*a[1], **a[2])
                else:
                    self.op(*a[:5])
        self.capture = cap
        return max(done) if done else 0.0

    def final_wait(self, eng, keys):
        waits = [(k, self.cnt[k]) for k in keys if self.cnt.get(k, 0) > 0]
        self.ops[eng].append((waits, None, None))

    def sem_keys(self):
        keys = set(self.cnt.keys())
        for e in self.ENG:
            keys.add(e)
        return sorted(keys)


def build(NT=32, NP=32, self_sync=True, stop=99):
    nc = bass.Bass("TRN2", target_bir_lowering=False)
    S = Sched(self_sync=self_sync)

    def din(name, shape, dt=F32):
        return nc.dram_tensor(name, list(shape), dt, kind="ExternalInput").ap()

    xm = din("xm", [NT * 128, D])
    xp = din("xp", [max(NP, 1) * 128, D])
    posm = din("posm", [128, NT + 1], I32)
    c_col = din("c_col", [128, 8])
    w_ada = din("w_ada", [D, 3 * D])
    b_ada = din("b_ada", [1, 3 * D])
    gnorm_col = din("gnorm_col", [128, 8])
    w_in = din("w_in", [D, DIN])
    wdec = din("wdec", [17, 256])
    gmix_col = din("gmix_col", [128, 8])
    sinks = din("sinks", [1, 8])
    w_out = din("w_out", [D, D])
    g_final = din("g_final", [1, D])
    flag_col = din("flag_col", [128, 1])
    invf = din("invf", [128, 32])
    c_ident = din("c_ident", [128, 128])
    c_tricum = din("c_tricum", [128, 128])
    c_trisuf = din("c_trisuf", [128, 128])
    c_mgla = din("c_mgla", [128, 128])
    c_mbcur = din("c_mbcur", [128, 512])
    c_mbprev = din("c_mbprev", [128, 512])
    c_mbprev0 = din("c_mbprev0", [128, 512])
    out = nc.dram_tensor("out", [NT * 128, D], F32, kind="ExternalOutput").ap()

    with ExitStack() as es:
        def sb(name, shape, dt=F32):
            return Buf(name, es.enter_context(nc.sbuf_tensor(name, list(shape), dt)))

        def ps(name, shape, dt=F32):
            return Buf(name, es.enter_context(nc.psum_tensor(name, list(shape), dt)), excl=True)

        bk = [ps(f"bk{i}", [128, 512]) for i in range(8) if i != 2]
        b0, b1, b3, b4, b5, b6, b7 = bk
        ptb = ps("ptb", [128, 8, 128], BF16)

        NSTG = 3
        stage = [sb(f"stage{i}", [128, 2 * D]) for i in range(NSTG)]
        win = sb("win", [128, KC, DIN], BF16)
        wout = sb("wout", [128, KC, D], BF16)
        ones_f = sb("ones_f", [1, 128])
        negcol = sb("negcol", [128, 1], BF16)
        ident_f = sb("ident_f", [128, 128])
        ident_bf = sb("ident_bf", [128, 128], BF16)
        tri_bf = sb("tri_bf", [128, 2, 128], BF16)
        L_hi = sb("L_hi", [128, 256], BF16)
        L_lo = sb("L_lo", [128, 256], BF16)
        mgla = sb("mgla", [128, 128])
        mb = sb("mb", [128, 3, 512], BF16)
        posi = sb("posi", [128, NT + 1], I32)
        posf = sb("posf", [128, NT + 1])
        invf_sb = sb("invf_sb", [128, 32])
        ang = sb("ang", [128, NT + 1, 32])
        angm = sb("angm", [128, NT + 1, 32])
        cosT = sb("cosT", [128, NT + 1, 32])
        sinS = sb("sinS", [128, NT + 1, 2, 32])
        sink_bc = sb("sink_bc", [128, 8])
        esink = sb("esink", [128, 8])
        gfin_bc = sb("gfin_bc", [128, D])
        wdec_sb = sb("wdec_sb", [17, 256])
        flag_sb = sb("flag_sb", [128, 1])
        ccol = sb("ccol", [128, 8])
        ecol = sb("ecol", [128, 8])
        siluc = sb("siluc", [128, 8])
        gncol = sb("gncol", [128, 8])
        gmcol = sb("gmcol", [128, 8])
        gscol = sb("gscol", [128, 8])
        shcol = sb("shcol", [128, 8])
        modrow = sb("modrow", [1, 3 * D])
        St = sb("St", [128, 2, 128])
        St_bf = sb("St_bf", [128, 2, 128], BF16)
        decay = sb("decay", [128, 2])
        gaT = sb("gaT", [17, 128], BF16)
        wdec_bf = sb("wdec_bf", [17, 256], BF16)

        xin = [sb(f"xin{i}", [128, D]) for i in range(3)]
        ssq = sb("ssq", [128, 1])
        rstd = sb("rstd", [128, 1])
        hb = sb("hb", [128, D], BF16)
        hTk = [sb(f"hT{k}", [128, 128], BF16) for k in range(KC)]
        QK = [sb(f"qk_sb{i}", [128, 512], BF16) for i in range(2)]
        VV = [sb(f"v_sb{i}", [128, 512], BF16) for i in range(2)]
        GA = [sb(f"ga_sb{i}", [128, 16], BF16) for i in range(2)]
        GG = [sb(f"gate_g{i}", [128, 512], BF16) for i in range(2)]
        GS = [sb(f"gate_s{i}", [128, 512], BF16) for i in range(2)]
        e_t = sb("e_t", [128, 512])
        lnvA = sb("lnvA", [128, 1])
        lnvB = sb("lnvB", [128, 4])
        lnvC = sb("lnvC", [128, 1])
        tmpc = sb("tmpc", [128, 512])
        tmps = sb("tmps", [128, 512])
        QR = [sb(f"q_r{i}", [128, 512], BF16) for i in range(2)]
        tmpck = sb("tmpck", [128, 128])
        tmpsk = sb("tmpsk", [128, 128])
        KR = [sb(f"k_r{i}", [128, 128], BF16) for i in range(2)]
        kT = [sb(f"kT{i}", [128, 2, 128], BF16) for i in range(2)]
        vaug = [sb(f"vaug{i}", [128, 2, 65], BF16) for i in range(3)]
        qT = sb("qT", [128, 512], BF16)
        e_z = sb("e_z", [128, 256])
        Lz = sb("Lz", [128, 256])
        Eplus = sb("Eplus", [128, 2, 128])
        Eminus = sb("Eminus", [128, 2, 128])
        Esuf = sb("Esuf", [128, 256])
        qkT_sb = sb("qkT_sb", [128, 4, 128], BF16)
        nln8 = sb("nln8", [128, 1])
        q_dT = sb("q_dT", [128, 4, 128], BF16)
        k_dT = sb("k_dT", [128, 2, 128], BF16)
        k_tail = sb("k_tail", [128, 256], BF16)
        AT_sb = sb("AT_sb", [128, 512], BF16)
        junk2 = sb("junk2", [128, 128], BF16)
        ssq_g = sb("ssq_g", [128, 4])
        rstd_g = sb("rstd_g", [128, 4])
        tmp_g = sb("tmp_g", [128, 512])
        mix = sb("mix", [128, D], BF16)
        mixT = sb("mixT", [128, KC, 128], BF16)
        PT = [sb(f"PT{g}", [128, 2, 512], BF16) for g in range(2)]
        den = sb("den", [128, 8])
        rden = sb("rden", [128, 8])
        tmp_s = tmp_g
        xnew = sb("xnew", [128, D])
        gate_bc = xnew
        ssq2 = sb("ssq2", [128, 1])
        rstd2 = sb("rstd2", [128, 1])
        yout = [sb(f"yout{i}", [128, D]) for i in range(2)]

        def ld(dst, src, key="ld0", q="sp", **kw):
            S.dma(q, key, dst.t[:] if not isinstance(dst, tuple) else dst[1], src, writes=[dst if not isinstance(dst, tuple) else dst[0]], **kw)

        small = [
            (ccol, c_col), (gncol, gnorm_col), (gmcol, gmix_col), (flag_sb, flag_col), (invf_sb, invf),
            (ident_f, c_ident), (mgla, c_mgla), (posi, posm),
            (wdec_sb, wdec), (modrow, b_ada),
        ]
        for dst, src in small:
            S.dma("sp", "ld0", dst.t[:], src, writes=[dst])
        mbstage = stage[0]
        mbv = stage[0].t[:, 0:1536].rearrange("p (a n) -> p a n", a=3)
        S.dma("sp", "ld0", mbv[:, 0, :], c_mbcur, writes=[mbstage])
        S.dma("sp", "ld0", mbv[:, 1, :], c_mbprev, writes=[mbstage])
        S.dma("sp", "ld0", mbv[:, 2, :], c_mbprev0, writes=[mbstage])
        triv = stage[1].t[:, 0:256].rearrange("p (a n) -> p a n", a=2)
        S.dma("sp", "ld0", triv[:, 0, :], c_tricum, writes=[stage[1]])
        S.dma("sp", "ld0", triv[:, 1, :], c_trisuf, writes=[stage[1]])
        S.dma("sp", "ld0", sink_bc.t[:], sinks.partition_broadcast(128), writes=[sink_bc])
        S.dma("sp", "ld0", gfin_bc.t[:], g_final.partition_broadcast(128), writes=[gfin_bc])
        fin = ("ld0", S.cnt["ld0"])
        for b in [d for d, _ in small] + [mbstage, stage[1], sink_bc, gfin_bc]:
            b.w = fin

        eps_d = sb("eps_d", [128, 1])
        eps_h = sb("eps_h", [128, 1])
        S.op("pool", lambda e: e.memset(nln8.t[:], -math.log(8.0)), writes=[nln8])
        S.op("pool", lambda e: e.memset(eps_d.t[:], D * EPS), writes=[eps_d])
        S.op("pool", lambda e: e.memset(eps_h.t[:], 128.0 * EPS), writes=[eps_h])
        S.op("pool", lambda e: e.memset(ones_f.t[:], 1.0), writes=[ones_f])
        S.op("pool", lambda e: e.memset(negcol.t[:], -1.0 / 16.0), writes=[negcol])
        S.op("dve", lambda e: e.tensor_copy(out=tri_bf.t[:], in_=triv), reads=[stage[1]], writes=[tri_bf])
        S.op("pool", lambda e: e.memset(gaT.t[:], 1.0), writes=[gaT])
        S.op("pool", lambda e: e.memset(St.t[:], 0.0), writes=[St])
        S.op("pool", lambda e: e.memset(St_bf.t[:], 0.0), writes=[St_bf])
        S.op("pool", lambda e: e.memset(q_dT.t[:], 0.0), writes=[q_dT])
        for i in range(3):
            S.op("pool", lambda e, i=i: e.memset(vaug[i].t[:], 1.0), writes=[vaug[i]])
        for i in range(2):
            S.op("pool", lambda e, i=i: e.memset(kT[i].t[:], 0.0), writes=[kT[i]])
        S.op("dve", lambda e: e.tensor_copy(out=ident_bf.t[:], in_=ident_f.t[:]), reads=[ident_f], writes=[ident_bf])
        S.op("dve", lambda e: e.tensor_copy(out=wdec_bf.t[:], in_=wdec_sb.t[:]), reads=[wdec_sb], writes=[wdec_bf])
        S.op("dve", lambda e: e.tensor_copy(out=mb.t[:], in_=mbv), reads=[mbstage], writes=[mb])
        S.op("dve", lambda e: e.tensor_single_scalar(out=gfin_bc.t[:], in_=gfin_bc.t[:], scalar=32.0, op=ALU.mult),
             reads=[gfin_bc], writes=[gfin_bc])
        S.op("dve", lambda e: e.tensor_single_scalar(out=gmcol.t[:, 0:4], in_=gmcol.t[:, 0:4], scalar=math.sqrt(128.0), op=ALU.mult),
             reads=[gmcol], writes=[gmcol])

        S.off = stop < 1
        L_setup = []
        S.capture = L_setup
        S.op("dve", lambda e: e.tensor_copy(out=posf.t[:], in_=posi.t[:]), reads=[posi], writes=[posf])
        S.op("dve", lambda e: e.tensor_tensor(
            out=ang.t[:], in0=posf.t[:].unsqueeze(2).broadcast_to([128, NT + 1, 32]),
            in1=invf_sb.t[:].unsqueeze(1).broadcast_to([128, NT + 1, 32]), op=ALU.mult),
            reads=[posf, invf_sb], writes=[ang])
        C1 = 6.28125
        C2 = 2.0 * math.pi - 6.28125
        angi = sb("angi", [128, NT + 1, 32], I32)
        S.op("dve", lambda e: e.tensor_single_scalar(out=angm.t[:], in_=ang.t[:], scalar=1.0 / (2.0 * math.pi), op=ALU.mult),
             reads=[ang], writes=[angm])
        S.op("dve", lambda e: e.tensor_copy(out=angi.t[:], in_=angm.t[:]), reads=[angm], writes=[angi])
        S.op("dve", lambda e: e.tensor_copy(out=angm.t[:], in_=angi.t[:]), reads=[angi], writes=[angm])
        S.op("dve", lambda e: e.scalar_tensor_tensor(out=ang.t[:], in0=angm.t[:], scalar=-C1, in1=ang.t[:], op0=ALU.mult, op1=ALU.add),
             reads=[angm, ang], writes=[ang])
        S.op("dve", lambda e: e.scalar_tensor_tensor(out=ang.t[:], in0=angm.t[:], scalar=-C2, in1=ang.t[:], op0=ALU.mult, op1=ALU.add),
             reads=[angm, ang], writes=[ang])
        SC = [-1.0 / 6, 1.0 / 120, -1.0 / 5040, 1.0 / 362880, -1.0 / 39916800]
        CC = [-1.0 / 2, 1.0 / 24, -1.0 / 720, 1.0 / 40320, -1.0 / 3628800, 1.0 / 479001600]
        s_v = sinS.t[:, :, 1, :]
        t_v = sinS.t[:, :, 0, :]
        S.op("dve", lambda e: e.tensor_single_scalar(out=ang.t[:], in_=ang.t[:], scalar=0.5, op=ALU.mult), reads=[ang], writes=[ang])
        S.op("dve", lambda e: e.tensor_tensor(out=angm.t[:], in0=ang.t[:], in1=ang.t[:], op=ALU.mult), reads=[ang], writes=[angm])

        def horner(dst, coefs, dbuf):
            S.op("dve", lambda e: e.tensor_single_scalar(out=dst, in_=angm.t[:], scalar=coefs[-1], op=ALU.mult), reads=[angm], writes=[dbuf])
            for a in reversed(coefs[:-1]):
                S.op("dve", lambda e, a=a: e.scalar_tensor_tensor(out=dst, in0=dst, scalar=a, in1=angm.t[:], op0=ALU.add, op1=ALU.mult),
                     reads=[dbuf, angm], writes=[dbuf])

        horner(cosT.t[:], SC, cosT)
        S.op("dve", lambda e: e.scalar_tensor_tensor(out=s_v, in0=cosT.t[:], scalar=1.0, in1=ang.t[:], op0=ALU.add, op1=ALU.mult),
             reads=[cosT, ang], writes=[sinS])
        horner(cosT.t[:], CC, cosT)
        S.op("dve", lambda e: e.tensor_single_scalar(out=cosT.t[:], in_=cosT.t[:], scalar=1.0, op=ALU.add), reads=[cosT], writes=[cosT])
        S.op("dve", lambda e: e.scalar_tensor_tensor(out=ang.t[:], in0=s_v, scalar=2.0, in1=cosT.t[:], op0=ALU.mult, op1=ALU.mult),
             reads=[sinS, cosT], writes=[ang])
        S.op("dve", lambda e: e.tensor_tensor(out=angm.t[:], in0=s_v, in1=s_v, op=ALU.mult), reads=[sinS], writes=[angm])
        S.op("dve", lambda e: e.tensor_scalar(out=cosT.t[:], in0=angm.t[:], scalar1=-2.0, scalar2=1.0, op0=ALU.mult, op1=ALU.add),
             reads=[angm], writes=[cosT])
        S.op("dve", lambda e: e.tensor_copy(out=s_v, in_=ang.t[:]), reads=[ang], writes=[sinS])
        S.op("dve", lambda e: e.tensor_single_scalar(out=t_v, in_=ang.t[:], scalar=-1.0, op=ALU.mult), reads=[ang], writes=[sinS])
        S.op("act", lambda e: e.activation(out=esink.t[:], in_=sink_bc.t[:], func=AF.Exp), reads=[sink_bc], writes=[esink])
        S.op("act", lambda e: e.activation(out=ecol.t[:], in_=ccol.t[:], func=AF.Exp, scale=-1.0), reads=[ccol], writes=[ecol])
        S.op("dve", lambda e: e.tensor_single_scalar(out=ecol.t[:], in_=ecol.t[:], scalar=1.0, op=ALU.add), reads=[ecol], writes=[ecol])
        S.op("dve", lambda e: e.reciprocal(out=ecol.t[:], in_=ecol.t[:]), reads=[ecol], writes=[ecol])
        S.op("dve", lambda e: e.tensor_tensor(out=siluc.t[:], in0=ccol.t[:], in1=ecol.t[:], op=ALU.mult),
             reads=[ccol, ecol], writes=[siluc])

        S.off = stop < 2
        WQ = "sp"
        modbanks = [b0, b1, b3, b4, b5, b6]
        stg_i = [0]

        def mod_phase(c0, ngrp, gbase):
            for k in range(KC):
                st = stage[stg_i[0] % NSTG]
                S.dma(WQ, f"stg{stg_i[0] % NSTG}", st.t[:, 0:ngrp * 512], w_ada[k * 128:(k + 1) * 128, c0:c0 + ngrp * 512], writes=[st])
                stg_i[0] += 1
                for g in range(ngrp):
                    S.op("pe", lambda e, k=k, g=g, st=st: e.matmul(
                        modbanks[gbase + g].t[0:1, :], lhsT=siluc.t[:, k:k + 1], rhs=st.t[:, g * 512:(g + 1) * 512],
                        start=(k == 0), stop=(k == KC - 1)), reads=[siluc, st], writes=[modbanks[gbase + g]], inc=(g == ngrp - 1))
            for g in range(ngrp):
                gg = gbase + g
                S.op("dve", lambda e, g=g, gg=gg: e.tensor_tensor(out=modrow.t[0:1, gg * 512:(gg + 1) * 512], in0=modbanks[gg].t[0:1, :],
                                                                  in1=modrow.t[0:1, gg * 512:(gg + 1) * 512], op=ALU.add),
                     reads=[modbanks[gg], modrow], writes=[modrow])

        mod_phase(0, 4, 0)
        for j in range(16):
            src = (0 if j < 8 else D) + (j % 8) * 128
            S.op("pe", lambda e, j=j, src=src: e.matmul(b7.t[:, j:j + 1], lhsT=modrow.t[0:1, src:src + 128], rhs=ones_f.t[0:1, 0:1],
                                                        start=True, stop=True), reads=[modrow, ones_f], writes=[b7], inc=(j == 15))
        S.op("dve", lambda e: e.tensor_copy(out=shcol.t[:], in_=b7.t[:, 0:8]), reads=[b7], writes=[shcol])
        S.op("dve", lambda e: e.scalar_tensor_tensor(out=gscol.t[:], in0=b7.t[:, 8:16], scalar=1.0, in1=gncol.t[:],
                                                     op0=ALU.add, op1=ALU.mult), reads=[b7, gncol], writes=[gscol])
        S.op("dve", lambda e: e.tensor_single_scalar(out=gscol.t[:], in_=gscol.t[:], scalar=32.0, op=ALU.mult),
             reads=[gscol], writes=[gscol])

        S.off = stop < 3
        hlf = DIN // 2
        for k in range(KC):
            for hh in range(2):
                st = stage[stg_i[0] % NSTG]
                S.dma(WQ, f"stg{stg_i[0] % NSTG}", st.t[:, 0:hlf], w_in[k * 128:(k + 1) * 128, hh * hlf:(hh + 1) * hlf], writes=[st])
                stg_i[0] += 1
                if hh == 0:
                    S.op("act", lambda e, k=k, st=st: e.activation(out=win.t[:, k, 0:hlf], in_=st.t[:, 0:hlf], func=AF.Copy),
                         reads=[st], writes=[win])
                else:
                    S.op("dve", lambda e, k=k, st=st: e.tensor_copy(out=win.t[:, k, hlf:DIN], in_=st.t[:, 0:hlf]), reads=[st], writes=[win])
        S.off = stop < 4
        mod_phase(2 * D, 2, 4)
        for g in range(2):
            bb = (b3, b4)[g]
            S.op("pe", lambda e, g=g, bb=bb: e.matmul(bb.t[:, :], lhsT=ones_f.t[0:1, :], rhs=modrow.t[0:1, 2 * D + g * 512:2 * D + (g + 1) * 512],
                                                      start=True, stop=True), reads=[ones_f, modrow], writes=[bb])
            S.op("act", lambda e, g=g, bb=bb: e.activation(out=gate_bc.t[:, g * 512:(g + 1) * 512], in_=bb.t[:, :], func=AF.Copy),
                 reads=[bb], writes=[gate_bc])
        for k in range(KC):
            st = stage[stg_i[0] % NSTG]
            S.dma(WQ, f"stg{stg_i[0] % NSTG}", st.t[:, 0:D], w_out[k * 128:(k + 1) * 128, :], writes=[st])
            stg_i[0] += 1
            eng = "dve"
            S.op(eng, lambda e, k=k, st=st: e.scalar_tensor_tensor(out=wout.t[:, k, :], in0=st.t[:, 0:D], scalar=gmcol.t[:, k:k + 1],
                                                                   in1=gate_bc.t[:], op0=ALU.mult, op1=ALU.mult),
                 reads=[st, gmcol, gate_bc], writes=[wout])

        S.capture = None
        H = lambda i: i % 2
        HX = lambda i: i % 3
        ACT_EVAC = 0

        def front(x_ap, slot, xbuf):
            S.dma("sp", f"x{slot}", xbuf.t[:], x_ap, writes=[xbuf])
            S.op("act", lambda e: e.activation(out=hb.t[:], in_=xbuf.t[:], func=AF.Square, accum_out=ssq.t[:, 0:1]),
                 reads=[xbuf], writes=[hb, ssq])
            S.op("act", lambda e: e.activation(out=lnvA.t[:, 0:1], in_=ssq.t[:], func=AF.Ln, bias=eps_d.t[:, 0:1]), reads=[ssq, eps_d], writes=[lnvA])
            S.op("act", lambda e: e.activation(out=rstd.t[:], in_=lnvA.t[:, 0:1], func=AF.Exp, scale=-0.5), reads=[lnvA], writes=[rstd])
            S.op("dve", lambda e: e.tensor_scalar(out=hb.t[:], in0=xbuf.t[:], scalar1=rstd.t[:, 0:1], scalar2=None, op0=ALU.mult),
                 reads=[xbuf, rstd], writes=[hb])
            for k in range(KC):
                S.op("pe", lambda e, k=k: e.transpose(ptb.t[:, k, :], hb.t[:, k * 128:(k + 1) * 128], ident_bf.t[:]),
                     reads=[hb, ident_bf], writes=[ptb], inc=(k == KC - 1))
            for k in range(KC):
                if k < ACT_EVAC:
                    S.op("act", lambda e, k=k: e.activation(out=hTk[k].t[:], in_=ptb.t[:, k, :], func=AF.Identity,
                                                            scale=gscol.t[:, k:k + 1], bias=shcol.t[:, k:k + 1]),
                         reads=[ptb, gscol, shcol], writes=[hTk[k]])
                else:
                    S.op("dve", lambda e, k=k: e.tensor_scalar(out=hTk[k].t[:], in0=ptb.t[:, k, :], scalar1=gscol.t[:, k:k + 1],
                                                               scalar2=shcol.t[:, k:k + 1], op0=ALU.mult, op1=ALU.add),
                         reads=[ptb, gscol, shcol], writes=[hTk[k]])

        def proj(bank, o, w):
            for k in range(KC):
                S.op("pe", lambda e, k=k: e.matmul(bank.t[:, 0:w], lhsT=hTk[k].t[:], rhs=win.t[:, k, o:o + w], start=(k == 0), stop=(k == KC - 1)),
                     reads=[hTk[k], win], writes=[bank], inc=(k % 3 == 2 or k == KC - 1))

        def rope_k(src_bank, col0, tcol, krb, vab):
            skv = src_bank.t[:, col0:col0 + 128].rearrange("p (g a f) -> p g a f", g=2, a=2)
            cosb = cosT.t[:, tcol, :].unsqueeze(1).unsqueeze(1).broadcast_to([128, 2, 2, 32])
            S.op("dve", lambda e: e.tensor_tensor(out=tmpck.t[:].rearrange("p (g a f) -> p g a f", g=2, a=2), in0=skv, in1=cosb, op=ALU.mult),
                 reads=[src_bank, cosT], writes=[tmpck])
            for a in range(2):
                S.op("dve", lambda e, a=a: e.tensor_tensor(
                    out=tmpsk.t[:].rearrange("p (g a f) -> p g a f", g=2, a=2)[:, :, a, :], in0=skv[:, :, 1 - a, :],
                    in1=sinS.t[:, tcol, a, :].unsqueeze(1).broadcast_to([128, 2, 32]), op=ALU.mult),
                    reads=[src_bank, sinS], writes=[tmpsk])
            S.op("pool", lambda e: e.tensor_tensor(out=krb.t[:], in0=tmpck.t[:], in1=tmpsk.t[:], op=ALU.add),
                 reads=[tmpck, tmpsk], writes=[krb])
            S.op("dve", lambda e: e.tensor_copy(out=vab.t[:, :, 0:64],
                                                in_=src_bank.t[:, col0 + 128:col0 + 256].rearrange("p (g f) -> p g f", g=2)),
                 reads=[src_bank], writes=[vab])

        def gate(bank, gbuf):
            S.op("act", lambda e: e.activation(out=e_t.t[:], in_=bank.t[:, :], func=AF.Exp, scale=-1.0), reads=[bank], writes=[e_t])
            S.op("act", lambda e: e.activation(out=e_t.t[:], in_=e_t.t[:], func=AF.Ln, bias=1.0), reads=[e_t], writes=[e_t])
            S.op("act", lambda e: e.activation(out=e_t.t[:], in_=e_t.t[:], func=AF.Exp, scale=-1.0), reads=[e_t], writes=[e_t])
            S.op("dve", lambda e: e.tensor_tensor(out=gbuf.t[:], in0=bank.t[:, :], in1=e_t.t[:], op=ALU.mult),
                 reads=[bank, e_t], writes=[gbuf])

        def gla_decay_common(gab):
            gT_ps = b4.t[:, 256:384].bitcast(BF16)[0:16, 0:128]
            S.op("pe", lambda e: e.transpose(gT_ps, gab.t[:, 0:16], ident_bf.t[:]), reads=[gab, ident_bf], writes=[b4])
            S.op("dve", lambda e: e.tensor_copy(out=gaT.t[0:16, :], in_=gT_ps), reads=[b4], writes=[gaT])
            S.op("pe", lambda e: e.matmul(b3.t[:, 0:256], lhsT=gaT.t[0:17, :], rhs=wdec_bf.t[0:17, :], start=True, stop=True),
                 reads=[gaT, wdec_bf], writes=[b3])
            S.op("act", lambda e: e.activation(out=e_z.t[:], in_=b3.t[:, 0:256], func=AF.Exp, scale=-1.0), reads=[b3], writes=[e_z])
            S.op("act", lambda e: e.activation(out=Lz.t[:], in_=e_z.t[:], func=AF.Ln, bias=1.0), reads=[e_z], writes=[Lz])
            S.op("dve", lambda e: e.tensor_copy(out=L_hi.t[:], in_=Lz.t[:]), reads=[Lz], writes=[L_hi])
            S.op("dve", lambda e: e.tensor_tensor(out=L_lo.t[:], in0=Lz.t[:], in1=L_hi.t[:], op=ALU.subtract), reads=[Lz, L_hi], writes=[L_lo])

        def state_update(ub):
            for p in range(2):
                for hp in range(2):
                    r0 = 64 * hp
                    h = 2 * p + hp
                    S.op("dve", lambda e, p=p, r0=r0, h=h: e.scalar_tensor_tensor(
                        out=St.t[r0:r0 + 64, p, :], in0=St.t[r0:r0 + 64, p, :], scalar=decay.t[r0:r0 + 64, p:p + 1],
                        in1=ub.t[r0:r0 + 64, h * 128:(h + 1) * 128], op0=ALU.mult, op1=ALU.add),
                        reads=[St, decay, ub], writes=[St])

        def u_matmuls(ub, vb):
            for h in range(4):
                p = h // 2
                S.op("pe", lambda e, h=h, p=p: e.matmul(ub.t[:, h * 128:(h + 1) * 128], lhsT=k_tail.t[:, p * 128:(p + 1) * 128],
                                                        rhs=vb.t[:, h * 128:(h + 1) * 128], start=True, stop=True),
                     reads=[k_tail, vb], writes=[ub], inc=(h == 3))

        def sections(names):
            return {k: [] for k in names}

        def A_pre(t, i):
            sec = sections(("a0", "a1", "a2", "a3"))
            xbuf = xin[HX(i)]
            qk, vb, gab = QK[H(i)], VV[H(i)], GA[H(i)]
            S.capture = sec["a0"]
            front(xp[t * 128:(t + 1) * 128, :], HX(i), xbuf)
            S.capture = sec["a1"]
            proj(b0, O_GK, 512)
            S.op("act", lambda e: e.activation(out=qk.t[:, 256:512], in_=b0.t[:, 0:256], func=AF.Copy), reads=[b0], writes=[qk])
            S.op("act", lambda e: e.activation(out=vb.t[:, 0:256], in_=b0.t[:, 256:512], func=AF.Copy), reads=[b0], writes=[vb])
            S.capture = sec["a2"]
            proj(b1, O_GV + 256, 272)
            S.op("act", lambda e: e.activation(out=vb.t[:, 256:512], in_=b1.t[:, 0:256], func=AF.Copy), reads=[b1], writes=[vb])
            S.op("act", lambda e: e.activation(out=gab.t[:], in_=b1.t[:, 256:272], func=AF.Copy), reads=[b1], writes=[gab])
            if t == NP - 1:
                proj(b0, O_SK, 256)
                rope_k(b0, 0, NT, KR[H(i)], vaug[2])
            S.capture = None
            return sec

        def B_pre(t, i):
            sec = sections(("b0", "b1"))
            qk, vb, gab = QK[H(i)], VV[H(i)], GA[H(i)]
            S.capture = sec["b0"]
            if t == NP - 1:
                krb = KR[H(i)]
                S.op("pe", lambda e: e.transpose(ptb.t[:, 0, :], krb.t[:], ident_bf.t[:]), reads=[krb, ident_bf], writes=[ptb])
                for g in range(2):
                    S.op("act", lambda e, g=g: e.activation(out=kT[1].t[64 * g:64 * g + 64, g, :], in_=ptb.t[64 * g:64 * g + 64, 0, :], func=AF.Copy),
                         reads=[ptb], writes=[kT[1]])
            gla_decay_common(gab)
            for hl, Lb in enumerate((L_hi, L_lo)):
                S.op("pe", lambda e, hl=hl, Lb=Lb: e.matmul(b3.t[:, 256:512], lhsT=tri_bf.t[:, 1, :], rhs=Lb.t[:], start=(hl == 0), stop=(hl == 1)),
                     reads=[tri_bf, Lb], writes=[b3], inc=(hl == 1))
            for p in range(2):
                for hl, Lb in enumerate((L_hi, L_lo)):
                    S.op("pe", lambda e, p=p, hl=hl, Lb=Lb: e.matmul(b4.t[:, p:p + 1], lhsT=Lb.t[:, p * 128:(p + 1) * 128], rhs=negcol.t[:, 0:1],
                                                                    start=(hl == 0), stop=(hl == 1)),
                         reads=[Lb, negcol], writes=[b4], inc=(p == 1 and hl == 1))
            S.op("act", lambda e: e.activation(out=Esuf.t[:], in_=b3.t[:, 256:512], func=AF.Exp), reads=[b3], writes=[Esuf])
            S.op("act", lambda e: e.activation(out=decay.t[:], in_=b4.t[:, 0:2], func=AF.Exp), reads=[b4], writes=[decay])
            S.op("pool", lambda e: e.tensor_tensor(out=k_tail.t[:], in0=qk.t[:, 256:512], in1=Esuf.t[:], op=ALU.mult),
                 reads=[qk, Esuf], writes=[k_tail])
            S.capture = sec["b1"]
            u_matmuls(b5, vb)
            state_update(b5)
            if t == NP - 1:
                S.op("dve", lambda e: e.tensor_scalar(out=St.t[:], in0=St.t[:], scalar1=flag_sb.t[:, 0:1], scalar2=None, op0=ALU.mult),
                     reads=[St, flag_sb], writes=[St])
                S.op("act", lambda e: e.activation(out=St_bf.t[:], in_=St.t[:], func=AF.Copy), reads=[St], writes=[St_bf])
            S.capture = None
            return sec

        def A_main(t, i):
            sec = sections(("a0", "a1", "a2", "a3"))
            xbuf = xin[HX(i)]
            qk, vb, gab, krb, qrb, ggb, gsb = QK[H(i)], VV[H(i)], GA[H(i)], KR[H(i)], QR[H(i)], GG[H(i)], GS[H(i)]
            S.capture = sec["a0"]
            front(xm[t * 128:(t + 1) * 128, :], HX(i), xbuf)
            S.capture = sec["a1"]
            proj(b0, O_GQ, 512)
            S.op("act", lambda e: e.activation(out=qk.t[:], in_=b0.t[:, :], func=AF.Copy), reads=[b0], writes=[qk])
            proj(b1, O_GV, 512)
            S.op("dve", lambda e: e.tensor_copy(out=vb.t[:], in_=b1.t[:, :]), reads=[b1], writes=[vb])
            S.capture = sec["a2"]
            proj(b0, O_GA, 272)
            S.op("act", lambda e: e.activation(out=gab.t[:], in_=b0.t[:, 0:16], func=AF.Copy), reads=[b0], writes=[gab])
            rope_k(b0, 16, t, krb, vaug[t % 3])
            proj(b1, O_GZ, 512)
            gate(b1, ggb)
            S.capture = sec["a3"]
            proj(b0, O_SQ, 512)
            sqv = b0.t[:, :].rearrange("p (h a f) -> p h a f", h=8, a=2)
            S.op("dve", lambda e: e.tensor_tensor(
                out=tmpc.t[:].rearrange("p (h a f) -> p h a f", h=8, a=2), in0=sqv,
                in1=cosT.t[:, t, :].unsqueeze(1).unsqueeze(1).broadcast_to([128, 8, 2, 32]), op=ALU.mult),
                reads=[b0, cosT], writes=[tmpc])
            for a in range(2):
                S.op("dve", lambda e, a=a: e.tensor_tensor(
                    out=tmps.t[:].rearrange("p (h a f) -> p h a f", h=8, a=2)[:, :, a, :], in0=sqv[:, :, 1 - a, :],
                    in1=sinS.t[:, t, a, :].unsqueeze(1).broadcast_to([128, 8, 32]), op=ALU.mult),
                    reads=[b0, sinS], writes=[tmps])
            S.op("pool", lambda e: e.tensor_tensor(out=qrb.t[:].rearrange("p (j g f) -> p g j f", j=4, g=2),
                                                   in0=tmpc.t[:].rearrange("p (g j f) -> p g j f", g=2, j=4),
                                                   in1=tmps.t[:].rearrange("p (g j f) -> p g j f", g=2, j=4), op=ALU.add),
                 reads=[tmpc, tmps], writes=[qrb])
            proj(b1, O_SZ, 512)
            gate(b1, gsb)
            S.capture = None
            return sec

        def B_main(t, i):
            sec = sections(("gla1", "swa1", "gla2", "swa2", "out"))
            xbuf = xin[HX(i)]
            qk, vb, gab, krb, qrb, ggb, gsb = QK[H(i)], VV[H(i)], GA[H(i)], KR[H(i)], QR[H(i)], GG[H(i)], GS[H(i)]
            cur, prev = t % 2, (t + 1) % 2
            vcur, vprev = vaug[t % 3], vaug[(t - 1) % 3]
            S.capture = sec["gla1"]
            gla_decay_common(gab)
            for p in range(2):
                for hl, Lb in enumerate((L_hi, L_lo)):
                    S.op("pe", lambda e, p=p, hl=hl, Lb=Lb: e.matmul(b4.t[:, p * 128:(p + 1) * 128], lhsT=Lb.t[:, p * 128:(p + 1) * 128],
                                                                    rhs=tri_bf.t[:, 0, :], start=(hl == 0), stop=(hl == 1)),
                         reads=[Lb, tri_bf], writes=[b4], inc=(p == 1 and hl == 1))
            for hl, Lb in enumerate((L_hi, L_lo)):
                S.op("pe", lambda e, hl=hl, Lb=Lb: e.matmul(b3.t[:, 256:512], lhsT=tri_bf.t[:, 1, :], rhs=Lb.t[:], start=(hl == 0), stop=(hl == 1)),
                     reads=[tri_bf, Lb], writes=[b3], inc=(hl == 1))
            b4v = b4.t[:, 0:256].rearrange("p (a t) -> p a t", a=2)
            S.op("act", lambda e: e.activation(out=Eplus.t[:], in_=b4v, func=AF.Exp, bias=nln8.t[:, 0:1]), reads=[b4, nln8], writes=[Eplus])
            S.op("act", lambda e: e.activation(out=Eminus.t[:], in_=b4v, func=AF.Exp, scale=-1.0), reads=[b4], writes=[Eminus])
            S.op("act", lambda e: e.activation(out=Esuf.t[:], in_=b3.t[:, 256:512], func=AF.Exp), reads=[b3], writes=[Esuf])
            S.op("dve", lambda e: e.reciprocal(out=decay.t[:], in_=Eminus.t[:, :, 127]), reads=[Eminus], writes=[decay])
            S.capture = sec["gla2"]
            for j in range(4):
                S.op("pe", lambda e, j=j: e.transpose(ptb.t[:, j, :], qk.t[:, j * 128:(j + 1) * 128], ident_bf.t[:]),
                     reads=[qk, ident_bf], writes=[ptb], inc=(j == 3))
            S.op("act", lambda e: e.activation(out=qkT_sb.t[:], in_=ptb.t[:, 0:4, :], func=AF.Copy), reads=[ptb], writes=[qkT_sb])
            for hp in range(2):
                r0 = 64 * hp
                S.op("dve", lambda e, hp=hp, r0=r0: e.tensor_tensor(
                    out=q_dT.t[r0:r0 + 64, :, :].rearrange("r (p two) t -> r p two t", two=2)[:, :, hp, :],
                    in0=qkT_sb.t[r0:r0 + 64, 0:2, :], in1=Eplus.t[r0:r0 + 64, :, :], op=ALU.mult),
                    reads=[qkT_sb, Eplus], writes=[q_dT])
            S.op("pool", lambda e: e.tensor_tensor(out=k_dT.t[:], in0=qkT_sb.t[:, 2:4, :], in1=Eminus.t[:], op=ALU.mult),
                 reads=[qkT_sb, Eminus], writes=[k_dT])
            S.op("pool", lambda e: e.tensor_tensor(out=k_tail.t[:], in0=qk.t[:, 256:512], in1=Esuf.t[:], op=ALU.mult),
                 reads=[qk, Esuf], writes=[k_tail])
            for h in range(4):
                p = h // 2
                S.op("pe", lambda e, h=h, p=p: e.matmul(b5.t[:, h * 128:(h + 1) * 128], lhsT=k_dT.t[:, p, :],
                                                        rhs=q_dT.t[:, h, :], start=True, stop=True),
                     reads=[k_dT, q_dT], writes=[b5], inc=(h == 3))
            S.op("dve", lambda e: e.tensor_tensor(out=AT_sb.t[:].rearrange("p (h i) -> p h i", h=4),
                                                  in0=b5.t[:, :].rearrange("p (h i) -> p h i", h=4),
                                                  in1=mgla.t[:].unsqueeze(1).broadcast_to([128, 4, 128]), op=ALU.mult),
                 reads=[b5, mgla], writes=[AT_sb])
            for h in range(4):
                p = h // 2
                S.op("pe", lambda e, h=h: e.matmul(b5.t[:, h * 128:(h + 1) * 128], lhsT=AT_sb.t[:, h * 128:(h + 1) * 128],
                                                   rhs=vb.t[:, h * 128:(h + 1) * 128], start=True, stop=False),
                     reads=[AT_sb, vb], writes=[b5], inc=False)
                S.op("pe", lambda e, h=h, p=p: e.matmul(b5.t[:, h * 128:(h + 1) * 128], lhsT=q_dT.t[:, h, :],
                                                        rhs=St_bf.t[:, p, :], start=False, stop=True),
                     reads=[q_dT, St_bf], writes=[b5], inc=(h == 3))
            for h in range(4):
                S.op("act", lambda e, h=h: e.activation(out=junk2.t[:], in_=b5.t[:, h * 128:(h + 1) * 128], func=AF.Square,
                                                        accum_out=ssq_g.t[:, h:h + 1]), reads=[b5], writes=[junk2, ssq_g])
            S.op("act", lambda e: e.activation(out=lnvB.t[:, 0:4], in_=ssq_g.t[:], func=AF.Ln, bias=eps_h.t[:, 0:1]), reads=[ssq_g, eps_h], writes=[lnvB])
            S.op("act", lambda e: e.activation(out=rstd_g.t[:], in_=lnvB.t[:, 0:4], func=AF.Exp, scale=-0.5), reads=[lnvB], writes=[rstd_g])
            S.op("dve", lambda e: e.tensor_tensor(out=tmp_g.t[:].rearrange("p (h v) -> p h v", h=4),
                                                  in0=b5.t[:, :].rearrange("p (h v) -> p h v", h=4),
                                                  in1=rstd_g.t[:].unsqueeze(2).broadcast_to([128, 4, 128]), op=ALU.mult),
                 reads=[b5, rstd_g], writes=[tmp_g])
            S.op("pool", lambda e: e.tensor_tensor(out=mix.t[:, 0:512], in0=tmp_g.t[:], in1=ggb.t[:], op=ALU.mult),
                 reads=[tmp_g, ggb], writes=[mix])
            u_matmuls(b5, vb)
            state_update(b5)
            S.op("act", lambda e: e.activation(out=St_bf.t[:], in_=St.t[:], func=AF.Copy), reads=[St], writes=[St_bf])
            S.capture = sec["swa1"]
            for j in range(4):
                S.op("pe", lambda e, j=j: e.transpose(ptb.t[:, j, :], qrb.t[:, j * 128:(j + 1) * 128], ident_bf.t[:]),
                     reads=[qrb, ident_bf], writes=[ptb], inc=False)
            S.op("pe", lambda e: e.transpose(ptb.t[:, 4, :], krb.t[:], ident_bf.t[:]), reads=[krb, ident_bf], writes=[ptb])
            S.op("act", lambda e: e.activation(out=qT.t[:].rearrange("p (j q) -> p j q", j=4), in_=ptb.t[:, 0:4, :], func=AF.Copy), reads=[ptb], writes=[qT])
            for g in range(2):
                S.op("dve", lambda e, g=g: e.tensor_copy(out=kT[cur].t[64 * g:64 * g + 64, g, :], in_=ptb.t[64 * g:64 * g + 64, 4, :]),
                     reads=[ptb], writes=[kT[cur]])
            mbp = 2 if t == 0 else 1
            for g in range(2):
                for which, bank, kbuf, mi in ((0, b6, kT[prev], mbp), (1, b7, kT[cur], 0)):
                    S.op("pe", lambda e, bank=bank, mi=mi: e.matmul(bank.t[:, :], lhsT=ident_bf.t[:], rhs=mb.t[:, mi, :], start=True, stop=False),
                         reads=[ident_bf, mb], writes=[bank], inc=False)
                    S.op("pe", lambda e, bank=bank, kbuf=kbuf, g=g: e.matmul(
                        bank.t[:, :], lhsT=kbuf.t[:, g, :], rhs=qT.t[:, :],
                        start=False, stop=True), reads=[kbuf, qT], writes=[bank])
                    S.op("act", lambda e, bank=bank, g=g, which=which: e.activation(out=PT[g].t[:, which, :], in_=bank.t[:, :], func=AF.Exp, scale=0.125),
                         reads=[bank], writes=[PT[g]])
            S.capture = sec["swa2"]
            for g in range(2):
                ob = (b3, b4)[g]
                for j in range(4):
                    S.op("pe", lambda e, g=g, j=j, ob=ob: e.matmul(ob.t[:, j * 65:(j + 1) * 65], lhsT=PT[g].t[:, 0, j * 128:(j + 1) * 128],
                                                                   rhs=vprev.t[:, g, :], start=True, stop=False),
                         reads=[PT[g], vprev], writes=[ob], inc=False)
                    S.op("pe", lambda e, g=g, j=j, ob=ob: e.matmul(ob.t[:, j * 65:(j + 1) * 65], lhsT=PT[g].t[:, 1, j * 128:(j + 1) * 128],
                                                                   rhs=vcur.t[:, g, :], start=False, stop=True),
                         reads=[PT[g], vcur], writes=[ob], inc=(j == 3))
            for g in range(2):
                ob = (b3, b4)[g]
                obv = ob.t[:, 0:260].rearrange("p (j f) -> p j f", j=4)
                S.op("dve", lambda e, g=g, obv=obv: e.tensor_tensor(out=den.t[:, 4 * g:4 * g + 4], in0=obv[:, :, 64], in1=esink.t[:, 4 * g:4 * g + 4], op=ALU.add),
                     reads=[ob, esink], writes=[den])
            S.op("dve", lambda e: e.reciprocal(out=rden.t[:], in_=den.t[:]), reads=[den], writes=[rden])
            for g in range(2):
                ob = (b3, b4)[g]
                obv = ob.t[:, 0:260].rearrange("p (j f) -> p j f", j=4)
                S.op("dve", lambda e, g=g, obv=obv: e.tensor_tensor(
                    out=tmp_s.t[:, 256 * g:256 * (g + 1)].rearrange("p (j f) -> p j f", j=4), in0=obv[:, :, 0:64],
                    in1=rden.t[:, 4 * g:4 * g + 4].unsqueeze(2).broadcast_to([128, 4, 64]), op=ALU.mult),
                    reads=[ob, rden], writes=[tmp_s])
            S.op("pool", lambda e: e.tensor_tensor(out=mix.t[:, 512:1024], in0=tmp_s.t[:], in1=gsb.t[:], op=ALU.mult),
                 reads=[tmp_s, gsb], writes=[mix])
            S.capture = sec["out"]
            for k in range(KC):
                S.op("pe", lambda e, k=k: e.transpose(ptb.t[:, k, :], mix.t[:, k * 128:(k + 1) * 128], ident_bf.t[:]),
                     reads=[mix, ident_bf], writes=[ptb], inc=(k == KC - 1))
            S.op("act", lambda e: e.activation(out=mixT.t[:], in_=ptb.t[:], func=AF.Copy), reads=[ptb], writes=[mixT])
            for n in range(2):
                bank = (b0, b1)[n]
                for k in range(KC):
                    S.op("pe", lambda e, k=k, n=n, bank=bank: e.matmul(bank.t[:, :], lhsT=mixT.t[:, k, :], rhs=wout.t[:, k, n * 512:(n + 1) * 512],
                                                                       start=(k == 0), stop=(k == KC - 1)),
                         reads=[mixT, wout], writes=[bank], inc=(k % 4 == 3))
                S.op("dve", lambda e, n=n, bank=bank: e.tensor_tensor(out=xnew.t[:, n * 512:(n + 1) * 512], in0=bank.t[:, :],
                                                                      in1=xbuf.t[:, n * 512:(n + 1) * 512], op=ALU.add),
                     reads=[bank, xbuf], writes=[xnew])
            yo = yout[t % 2]
            S.op("act", lambda e: e.activation(out=yo.t[:], in_=xnew.t[:], func=AF.Square, accum_out=ssq2.t[:, 0:1]),
                 reads=[xnew], writes=[yo, ssq2])
            S.op("act", lambda e: e.activation(out=lnvC.t[:, 0:1], in_=ssq2.t[:], func=AF.Ln, bias=eps_d.t[:, 0:1]), reads=[ssq2, eps_d], writes=[lnvC])
            S.op("act", lambda e: e.activation(out=rstd2.t[:], in_=lnvC.t[:, 0:1], func=AF.Exp, scale=-0.5), reads=[lnvC], writes=[rstd2])
            S.op("dve", lambda e: e.scalar_tensor_tensor(out=yo.t[:], in0=xnew.t[:], scalar=rstd2.t[:, 0:1], in1=gfin_bc.t[:],
                                                          op0=ALU.mult, op1=ALU.mult), reads=[xnew, rstd2, gfin_bc], writes=[yo])
            S.dma("sp", f"o{t % 2}", out[t * 128:(t + 1) * 128, :], yo.t[:], reads=[yo])
            S.capture = None
            return sec

        S.off = stop < 5
        tiles = [("p", t) for t in range(NP)] + [("m", t) for t in range(NT)]
        L = list(L_setup)
        mkA = lambda i: (A_pre(tiles[i][1], i) if tiles[i][0] == "p" else A_main(tiles[i][1], i))
        secA = mkA(0) if tiles else None
        if tiles:
            for nm in ("a0", "a1", "a2", "a3"):
                L += secA[nm]
        for i, (kind, t) in enumerate(tiles):
            secB = B_pre(t, i) if kind == "p" else B_main(t, i)
            for nm in (("b0", "b1") if kind == "p" else ("gla1", "gla2", "swa1", "swa2")):
                L += secB[nm]
            if i + 1 < len(tiles):
                secA = mkA(i + 1)
                for nm in ("a0", "a1", "a2", "a3"):
                    L += secA[nm]
            if kind == "m":
                L += secB["out"]
        if not S.off:
            S.schedule(L)
        S.off = False
        S.final_wait("sp", ["o0", "o1", "ld0", "stg0", "stg1", "stg2", "stg3"])

        S.ops = {e: [o for o in S.ops[e] if o is not None] for e in S.ENG}
        keys = S.sem_keys()
        sems = {k: es.enter_context(nc.semaphore(f"s_{k}")) for k in keys}
        block = es.enter_context(nc.Block())

        def emit(eng_name):
            def body(e):
                for (waits, fn, inc) in S.ops[eng_name]:
                    for (s, v) in waits:
                        e.wait_ge(sems[s], v)
                    if fn is not None:
                        ins = fn(e)
                        if inc is not None:
                            ins.then_inc(sems[inc[0]], inc[1])
            return body

        block.tensor(emit("pe"))
        block.scalar(emit("act"))
        block.vector(emit("dve"))
        block.gpsimd(emit("pool"))
        block.sync(emit("sp"))
    return nc


def _consts():
    j = np.arange(128)[:, None]
    i = np.arange(128)[None, :]
    c = {}
    c["c_ident"] = np.eye(128, dtype=np.float32)
    c["c_tricum"] = np.where(j <= i, -1.0 / 16.0, 0.0).astype(np.float32)
    c["c_trisuf"] = np.where(j > i, -1.0 / 16.0, 0.0).astype(np.float32)
    c["c_mgla"] = np.where(j <= i, 1.0, 0.0).astype(np.float32)
    cur = np.where(j <= i, 0.0, NEG).astype(np.float32)
    prev = np.where(j > i, 0.0, NEG).astype(np.float32)
    c["c_mbcur"] = np.tile(cur, (1, 4))
    c["c_mbprev"] = np.tile(prev, (1, 4))
    inv_freq = (1.0 / (10000.0 ** (np.arange(0, 64, 2, dtype=np.float64) / 64.0))).astype(np.float32)
    c["invf"] = np.tile(inv_freq[None, :], (128, 1)).astype(np.float32)
    return c


_NC_CACHE = {}


def _col(v):
    return np.ascontiguousarray(v.reshape(-1, 128).T).astype(np.float32)


def make_in_maps(x, c, positions, w_ada, b_ada, g_norm, w_in, w_decay, b_decay, g_gla_head, sinks, w_out, g_final,
                 cfgs, NT, NP):
    cs = _consts()
    w_in_p = np.ascontiguousarray(w_in[0][:, PERM])
    wdec = np.concatenate([w_decay[0], b_decay[0][None, :]], axis=0).astype(np.float32)
    gmix = np.concatenate([_col(g_gla_head[0]), np.ones((128, 4), np.float32)], axis=1)
    maps = []
    for (b, s0, hasp) in cfgs:
        m = dict(cs)
        m["xm"] = np.ascontiguousarray(x[b, s0:s0 + NT * 128])
        npre = max(NP, 1) * 128
        if hasp:
            m["xp"] = np.ascontiguousarray(x[b, s0 - NP * 128:s0]) if NP > 0 else np.zeros((128, D), np.float32)
        else:
            m["xp"] = np.ascontiguousarray(x[b, 0:npre])
        pm = np.zeros((128, NT + 1), np.int32)
        pm[:, :NT] = positions[b, s0:s0 + NT * 128].reshape(NT, 128).T
        if hasp and NP > 0:
            pm[:, NT] = positions[b, s0 - 128:s0]
        m["posm"] = pm
        m["c_col"] = _col(c[b])
        m["w_ada"] = w_ada[0]
        m["b_ada"] = b_ada[0][None, :]
        m["gnorm_col"] = _col(g_norm[0])
        m["w_in"] = w_in_p
        m["wdec"] = wdec
        m["gmix_col"] = gmix
        m["sinks"] = sinks[0][None, :]
        m["w_out"] = w_out[0]
        m["g_final"] = g_final[None, :]
        m["flag_col"] = np.full((128, 1), 1.0 if hasp else 0.0, np.float32)
        m["c_mbprev0"] = cs["c_mbprev"] if hasp else np.full((128, 512), NEG, np.float32)
        maps.append({k: np.ascontiguousarray(v) for k, v in m.items()})
    return maps


def kernel(x, c, positions, w_ada, b_ada, g_norm, w_in, w_decay, b_decay, g_gla_head, sinks, w_out, g_final):
    args = [np.asarray(a) for a in (x, c, positions, w_ada, b_ada, g_norm, w_in, w_decay, b_decay, g_gla_head, sinks, w_out, g_final)]
    x = args[0]
    B, SEQ, _ = x.shape
    NT = NP = SEQ // 2 // 128
    cfgs = [(b, h * (SEQ // 2), h == 1) for b in range(B) for h in range(2)]
    key = (NT, NP)
    if key not in _NC_CACHE:
        _NC_CACHE[key] = build(NT, NP)
    nc = _NC_CACHE[key]
    maps = make_in_maps(*args, cfgs=cfgs, NT=NT, NP=NP)
    res = run_bass_kernel_spmd(nc, maps, core_ids=list(range(len(cfgs))))
    outp = np.empty((B, SEQ, D), np.float32)
    for i, (b, s0, _) in enumerate(cfgs):
        outp[b, s0:s0 + NT * 128] = res.results[i]["out"]
    return outp
```

```python
import math
import os as _os
from contextlib import ExitStack

import numpy as np
import concourse.bass as bass
import concourse.mybir as mybir
from concourse.bass_utils import run_bass_kernel_spmd

F32 = mybir.dt.float32
BF16 = mybir.dt.bfloat16
I32 = mybir.dt.int32
AF = mybir.ActivationFunctionType
ALU = mybir.AluOpType

D = 1024
KC = 8
DIN = 2832
EPS = 1e-6
NEG = -30000.0
SIN_S = 0.999999

O_GQ, O_GK, O_GV, O_GA, O_SK, O_SV, O_GZ, O_SQ, O_SZ = 0, 256, 512, 1024, 1040, 1168, 1296, 1808, 2320
PERM = np.concatenate([
    np.arange(0, 256), np.arange(256, 512), np.arange(512, 1024), np.arange(1024, 1040),
    np.arange(2064, 2192), np.arange(2192, 2320), np.arange(1040, 1552), np.arange(1552, 2064),
    np.arange(2320, 2832)])


class Buf:
    __slots__ = ("name", "t", "w", "r", "excl", "wread")

    def __init__(self, name, t, excl=False):
        self.name = name
        self.t = t
        self.w = None
        self.r = []
        self.excl = excl
        self.wread = False

    def __getitem__(self, k):
        return self.t[k]


class _Probe:
    def __init__(self):
        self.name = None
        self.args = ()
        self.kw = {}

    def __getattr__(self, name):
        def f(*args, **kw):
            self.name, self.args, self.kw = name, args, kw
            return self
        return f


def _est_cost(eng, fn):
    try:
        p = _Probe()
        fn(p)
        out = p.kw.get("out", p.args[0] if p.args else None)
        shp = list(out.shape)
        n = 1
        for d in shp[1:]:
            n *= int(d)
        if eng == "pe":
            if p.name == "transpose":
                return 0.08
            lhsT = p.kw.get("lhsT", p.args[1] if len(p.args) > 1 else None)
            f32 = lhsT is not None and lhsT.dtype == F32
            c = 0.012 + n / 1950.0
            return c * (4.5 if f32 else 1.0)
        if eng == "act":
            return 0.22 + n / 1150.0
        if eng == "dve":
            c = 0.12 + n / 950.0
            if p.name == "reciprocal":
                c = 0.12 + n / 160.0
            if p.name == "scalar_tensor_tensor":
                c = 0.15 + n / 850.0
            return c
        if eng == "pool":
            return 0.2 + n / 550.0
    except Exception:
        pass
    return None


class Sched:
    ENG = ("pe", "act", "dve", "pool", "sp")

    def __init__(self, self_sync=True):
        self.ops = {e: [] for e in self.ENG}
        self.cnt = {}
        self.seen = {e: {} for e in self.ENG}
        self.self_sync = self_sync
        self.off = False
        self.capture = None

    def _tickets(self, eng, reads, writes, xreads=()):
        tk = []
        for b in xreads:
            if b.w is not None and not (b.wread and b.w[0] == eng):
                tk.append(b.w)
            tk.extend(b.r)
        for b in reads:
            if b.excl:
                continue
            if b.w is not None:
                tk.append(b.w)
        for b in writes:
            if b.w is not None:
                tk.append(b.w)
            tk.extend(b.r)
        need = {}
        for (s, v) in tk:
            if s == eng and (eng == "pe" or not self.self_sync):
                continue
            if self.seen[eng].get(s, 0) < v:
                need[s] = max(need.get(s, 0), v)
        for s, v in need.items():
            self.seen[eng][s] = v
        return list(need.items())

    def replay(self, lst):
        cap, self.capture = self.capture, None
        for a in lst:
            if a[0] == "dma":
                self.dma(*a[1], **a[2])
            else:
                self.op(*a[:5])
        self.capture = cap

    def op(self, eng, fn, reads=(), writes=(), inc=True, cost=None):
        if self.off:
            return None
        if self.capture is not None:
            self.capture.append((eng, fn, list(reads), list(writes), inc, cost))
            return None
        xr = [b for b in reads if b.excl]
        waits = self._tickets(eng, reads, list(writes), xr)
        ticket = (eng, self.cnt.get(eng, 0) + 1)
        if inc:
            self.cnt[eng] = ticket[1]
        self.ops[eng].append((waits, fn, (eng, 1) if inc else None))
        for b in reads:
            b.r.append(ticket)
        for b in xr:
            b.w = ticket
            b.r = []
            b.wread = True
        for b in writes:
            b.w = ticket
            b.r = []
            b.wread = False
        return ticket

    def dma(self, q, semkey, out_ap, in_ap, reads=(), writes=(), **kw):
        if self.off:
            return None
        if self.capture is not None:
            self.capture.append(("dma", (q, semkey, out_ap, in_ap, list(reads), list(writes)), kw))
            return None
        waits = self._tickets(q, reads, writes)
        self.cnt[semkey] = self.cnt.get(semkey, 0) + 16
        ticket = (semkey, self.cnt[semkey])
        self.ops[q].append((waits, lambda e: e.dma_start(out=out_ap, in_=in_ap, **kw), (semkey, 16)))
        for b in reads:
            b.r.append(ticket)
        for b in writes:
            b.w = ticket
            b.r = []
        return ticket

    COST = {"pe": 0.2, "act": 0.6, "dve": 0.55, "pool": 0.9, "sp": 0.1}

    def schedule(self, L):
        units = []
        curu = None
        for a in L:
            if a[0] == "dma":
                q, semkey, out_ap, in_ap, reads, writes = a[1]
                nb = 4
                for d in out_ap.shape:
                    nb *= int(d)
                units.append({"eng": q, "ops": [a], "reads": list(reads), "writes": list(writes), "cost": 0.1, "lat": 2.5 if nb < 600000 else 2.0 + nb / 180e3})
                continue
            eng, fn, reads, writes, inc = a[:5]
            cost = a[5] if len(a) > 5 and a[5] is not None else None
            if cost is None:
                cost = _est_cost(eng, fn)
            if cost is None:
                cost = self.COST[eng]
            if eng == "pe":
                if curu is None:
                    curu = {"eng": "pe", "ops": [], "reads": [], "writes": [], "cost": 0.0, "lat": 0.3}
                curu["ops"].append(a)
                curu["reads"] += list(reads)
                curu["writes"] += list(writes)
                curu["cost"] += cost
                if inc:
                    units.append(curu)
                    curu = None
            else:
                units.append({"eng": eng, "ops": [a], "reads": list(reads), "writes": list(writes), "cost": cost, "lat": 0.15})
        assert curu is None
        n = len(units)
        lastw, readers = {}, {}
        deps = [set() for _ in range(n)]
        for j, u in enumerate(units):
            rs = [b for b in u["reads"] if not b.excl]
            ws = list(u["writes"]) + [b for b in u["reads"] if b.excl]
            for b in rs:
                if id(b) in lastw:
                    deps[j].add(lastw[id(b)])
            for b in ws:
                if id(b) in lastw:
                    deps[j].add(lastw[id(b)])
                for r in readers.get(id(b), ()):
                    deps[j].add(r)
            for b in rs:
                readers.setdefault(id(b), []).append(j)
            for b in ws:
                lastw[id(b)] = j
                readers[id(b)] = []
            deps[j].discard(j)
        succ = [[] for _ in range(n)]
        ndep = [len(d) for d in deps]
        for j, d in enumerate(deps):
            for i in d:
                succ[i].append(j)
        ready = {e: [] for e in self.ENG}
        depready = [0.0] * n
        done = [0.0] * n
        efree = {e: 0.0 for e in self.ENG}
        for j in range(n):
            if ndep[j] == 0:
                ready[units[j]["eng"]].append(j)
        order = []
        WIN = 1000
        nsched = 0
        scheduled = [False] * n
        oldest = 0
        while nsched < n:
            while oldest < n and scheduled[oldest]:
                oldest += 1
            best = None
            for e in self.ENG:
                cand = None
                for j in ready[e]:
                    if j > oldest + WIN:
                        continue
                    st = max(depready[j], efree[e])
                    key = (st, j)
                    if cand is None or key < cand[0]:
                        cand = (key, j)
                if cand is not None and (best is None or cand[0] < best[0]):
                    best = cand
            if best is None:
                j = min(j for e in self.ENG for j in ready[e])
                best = ((max(depready[j], efree[units[j]["eng"]]), j), j)
            (st, _), j = best
            u = units[j]
            ready[u["eng"]].remove(j)
            scheduled[j] = True
            nsched += 1
            efree[u["eng"]] = st + u["cost"]
            done[j] = st + u["cost"] + u["lat"]
            order.append(j)
            for k in succ[j]:
                ndep[k] -= 1
                depready[k] = max(depready[k], done[j])
                if ndep[k] == 0:
                    ready[units[k]["eng"]].append(k)
        cap, self.capture = self.capture, None
        for j in order:
            for a in units[j]["ops"]:
                if a[0] == "dma":
                    self.dma(*a[1], **a[2])
                else:
                    self.op(*a[:5])
        self.capture = cap
        return max(done) if done else 0.0

    def final_wait(self, eng, keys):
        waits = [(k, self.cnt[k]) for k in keys if self.cnt.get(k, 0) > 0]
        self.ops[eng].append((waits, None, None))

    def sem_keys(self):
        keys = set(self.cnt.keys())
        for e in self.ENG:
            keys.add(e)
        return sorted(keys)


def build(NT=32, NP=32, self_sync=True, stop=99):
    nc = bass.Bass("TRN2", target_bir_lowering=False)
    S = Sched(self_sync=self_sync)

    def din(name, shape, dt=F32):
        return nc.dram_tensor(name, list(shape), dt, kind="ExternalInput").ap()

    xm = din("xm", [NT * 128, D])
    xp = din("xp", [max(NP, 1) * 128, D])
    posm = din("posm", [128, NT + 1], I32)
    c_col = din("c_col", [128, 8])
    w_ada = din("w_ada", [D, 3 * D])
    b_ada = din("b_ada", [1, 3 * D])
    gnorm_col = din("gnorm_col", [128, 8])
    w_in = din("w_in", [D, DIN])
    wdec = din("wdec", [17, 256])
    gmix_col = din("gmix_col", [128, 8])
    sinks = din("sinks", [1, 8])
    w_out = din("w_out", [D, D])
    g_final = din("g_final", [1, D])
    flag_col = din("flag_col", [128, 1])
    invf = din("invf", [128, 32])
    c_ident = din("c_ident", [128, 128])
    c_tricum = din("c_tricum", [128, 128])
    c_trisuf = din("c_trisuf", [128, 128])
    c_mgla = din("c_mgla", [128, 128])
    c_mbcur = din("c_mbcur", [128, 512])
    c_mbprev = din("c_mbprev", [128, 512])
    c_mbprev0 = din("c_mbprev0", [128, 512])
    out = nc.dram_tensor("out", [NT * 128, D], F32, kind="ExternalOutput").ap()

    with ExitStack() as es:
        def sb(name, shape, dt=F32):
            return Buf(name, es.enter_context(nc.sbuf_tensor(name, list(shape), dt)))

        def ps(name, shape, dt=F32):
            return Buf(name, es.enter_context(nc.psum_tensor(name, list(shape), dt)), excl=True)

        bk = [ps(f"bk{i}", [128, 512]) for i in range(8) if i != 2]
        b0, b1, b3, b4, b5, b6, b7 = bk
        ptb = ps("ptb", [128, 8, 128], BF16)

        NSTG = 3
        stage = [sb(f"stage{i}", [128, 2 * D]) for i in range(NSTG)]
        win = sb("win", [128, KC, DIN], BF16)
        wout = sb("wout", [128, KC, D], BF16)
        ones_f = sb("ones_f", [1, 128])
        negcol = sb("negcol", [128, 1])
        ident_f = sb("ident_f", [128, 128])
        ident_bf = sb("ident_bf", [128, 128], BF16)
        tricum = sb("tricum", [128, 128])
        trisuf = sb("trisuf", [128, 128])
        mgla = sb("mgla", [128, 128])
        mb = sb("mb", [128, 3, 512], BF16)
        posi = sb("posi", [128, NT + 1], I32)
        posf = sb("posf", [128, NT + 1])
        invf_sb = sb("invf_sb", [128, 32])
        ang = sb("ang", [128, NT + 1, 32])
        angm = sb("angm", [128, NT + 1, 32])
        cosT = sb("cosT", [128, NT + 1, 32])
        sinS = sb("sinS", [128, NT + 1, 2, 32])
        sink_bc = sb("sink_bc", [128, 8])
        esink = sb("esink", [128, 8])
        gfin_bc = sb("gfin_bc", [128, D])
        wdec_sb = sb("wdec_sb", [17, 256])
        flag_sb = sb("flag_sb", [128, 1])
        ccol = sb("ccol", [128, 8])
        ecol = sb("ecol", [128, 8])
        siluc = sb("siluc", [128, 8])
        gncol = sb("gncol", [128, 8])
        gmcol = sb("gmcol", [128, 8])
        gscol = sb("gscol", [128, 8])
        shcol = sb("shcol", [128, 8])
        modrow = sb("modrow", [1, 3 * D])
        St = sb("St", [128, 2, 128])
        St_bf = sb("St_bf", [128, 2, 128], BF16)
        decay = sb("decay", [128, 2])
        gaT = sb("gaT", [17, 128], BF16)
        wdec_bf = sb("wdec_bf", [17, 256], BF16)

        xin = [sb(f"xin{i}", [128, D]) for i in range(3)]
        ssq = sb("ssq", [128, 1])
        rstd = sb("rstd", [128, 1])
        hb = sb("hb", [128, D], BF16)
        hTk = [sb(f"hT{k}", [128, 128], BF16) for k in range(KC)]
        QK = [sb(f"qk_sb{i}", [128, 512], BF16) for i in range(2)]
        VV = [sb(f"v_sb{i}", [128, 512], BF16) for i in range(2)]
        GA = [sb(f"ga_sb{i}", [128, 16], BF16) for i in range(2)]
        GG = [sb(f"gate_g{i}", [128, 512], BF16) for i in range(2)]
        GS = [sb(f"gate_s{i}", [128, 512], BF16) for i in range(2)]
        e_t = sb("e_t", [128, 512])
        lnvA = sb("lnvA", [128, 1])
        lnvB = sb("lnvB", [128, 4])
        lnvC = sb("lnvC", [128, 1])
        tmpc = sb("tmpc", [128, 512])
        tmps = sb("tmps", [128, 512])
        QR = [sb(f"q_r{i}", [128, 512], BF16) for i in range(2)]
        tmpck = sb("tmpck", [128, 128])
        tmpsk = sb("tmpsk", [128, 128])
        KR = [sb(f"k_r{i}", [128, 128], BF16) for i in range(2)]
        kT = [sb(f"kT{i}", [128, 2, 128], BF16) for i in range(2)]
        vaug = [sb(f"vaug{i}", [128, 2, 65], BF16) for i in range(3)]
        qT = sb("qT", [128, 512], BF16)
        e_z = sb("e_z", [128, 256])
        Lz = sb("Lz", [128, 256])
        Eplus = sb("Eplus", [128, 2, 128])
        Eminus = sb("Eminus", [128, 2, 128])
        Esuf = sb("Esuf", [128, 256])
        qkT_sb = sb("qkT_sb", [128, 4, 128], BF16)
        nln8 = sb("nln8", [128, 1])
        q_dT = sb("q_dT", [128, 4, 128], BF16)
        k_dT = sb("k_dT", [128, 2, 128], BF16)
        k_tail = sb("k_tail", [128, 256], BF16)
        AT_sb = sb("AT_sb", [128, 512], BF16)
        junk2 = sb("junk2", [128, 128], BF16)
        ssq_g = sb("ssq_g", [128, 4])
        rstd_g = sb("rstd_g", [128, 4])
        tmp_g = sb("tmp_g", [128, 512])
        mix = sb("mix", [128, D], BF16)
        mixT = sb("mixT", [128, KC, 128], BF16)
        PT = [sb(f"PT{g}", [128, 2, 512], BF16) for g in range(2)]
        den = sb("den", [128, 8])
        rden = sb("rden", [128, 8])
        tmp_s = tmp_g
        xnew = sb("xnew", [128, D])
        gate_bc = xnew
        ssq2 = sb("ssq2", [128, 1])
        rstd2 = sb("rstd2", [128, 1])
        yout = [sb(f"yout{i}", [128, D]) for i in range(2)]

        def ld(dst, src, key="ld0", q="sp", **kw):
            S.dma(q, key, dst.t[:] if not isinstance(dst, tuple) else dst[1], src, writes=[dst if not isinstance(dst, tuple) else dst[0]], **kw)

        small = [
            (ccol, c_col), (gncol, gnorm_col), (gmcol, gmix_col), (flag_sb, flag_col), (invf_sb, invf),
            (ident_f, c_ident), (tricum, c_tricum), (trisuf, c_trisuf), (mgla, c_mgla), (posi, posm),
            (wdec_sb, wdec), (modrow, b_ada),
        ]
        for dst, src in small:
            S.dma("sp", "ld0", dst.t[:], src, writes=[dst])
        mbstage = stage[0]
        mbv = stage[0].t[:, 0:1536].rearrange("p (a n) -> p a n", a=3)
        S.dma("sp", "ld0", mbv[:, 0, :], c_mbcur, writes=[mbstage])
        S.dma("sp", "ld0", mbv[:, 1, :], c_mbprev, writes=[mbstage])
        S.dma("sp", "ld0", mbv[:, 2, :], c_mbprev0, writes=[mbstage])
        S.dma("sp", "ld0", sink_bc.t[:], sinks.partition_broadcast(128), writes=[sink_bc])
        S.dma("sp", "ld0", gfin_bc.t[:], g_final.partition_broadcast(128), writes=[gfin_bc])
        fin = ("ld0", S.cnt["ld0"])
        for b in [d for d, _ in small] + [mbstage, sink_bc, gfin_bc]:
            b.w = fin

        eps_d = sb("eps_d", [128, 1])
        eps_h = sb("eps_h", [128, 1])
        S.op("pool", lambda e: e.memset(nln8.t[:], -math.log(8.0)), writes=[nln8])
        S.op("pool", lambda e: e.memset(eps_d.t[:], D * EPS), writes=[eps_d])
        S.op("pool", lambda e: e.memset(eps_h.t[:], 128.0 * EPS), writes=[eps_h])
        S.op("pool", lambda e: e.memset(ones_f.t[:], 1.0), writes=[ones_f])
        S.op("pool", lambda e: e.memset(negcol.t[:], -1.0 / 16.0), writes=[negcol])
        S.op("pool", lambda e: e.memset(gaT.t[:], 1.0), writes=[gaT])
        S.op("pool", lambda e: e.memset(St.t[:], 0.0), writes=[St])
        S.op("pool", lambda e: e.memset(St_bf.t[:], 0.0), writes=[St_bf])
        S.op("pool", lambda e: e.memset(q_dT.t[:], 0.0), writes=[q_dT])
        for i in range(3):
            S.op("pool", lambda e, i=i: e.memset(vaug[i].t[:], 1.0), writes=[vaug[i]])
        for i in range(2):
            S.op("pool", lambda e, i=i: e.memset(kT[i].t[:], 0.0), writes=[kT[i]])
        S.op("dve", lambda e: e.tensor_copy(out=ident_bf.t[:], in_=ident_f.t[:]), reads=[ident_f], writes=[ident_bf])
        S.op("dve", lambda e: e.tensor_copy(out=wdec_bf.t[:], in_=wdec_sb.t[:]), reads=[wdec_sb], writes=[wdec_bf])
        S.op("dve", lambda e: e.tensor_copy(out=mb.t[:], in_=mbv), reads=[mbstage], writes=[mb])
        S.op("dve", lambda e: e.tensor_single_scalar(out=gfin_bc.t[:], in_=gfin_bc.t[:], scalar=32.0, op=ALU.mult),
             reads=[gfin_bc], writes=[gfin_bc])
        S.op("dve", lambda e: e.tensor_single_scalar(out=gmcol.t[:, 0:4], in_=gmcol.t[:, 0:4], scalar=math.sqrt(128.0), op=ALU.mult),
             reads=[gmcol], writes=[gmcol])

        S.off = stop < 1
        L_setup = []
        S.capture = L_setup
        S.op("dve", lambda e: e.tensor_copy(out=posf.t[:], in_=posi.t[:]), reads=[posi], writes=[posf])
        S.op("dve", lambda e: e.tensor_tensor(
            out=ang.t[:], in0=posf.t[:].unsqueeze(2).broadcast_to([128, NT + 1, 32]),
            in1=invf_sb.t[:].unsqueeze(1).broadcast_to([128, NT + 1, 32]), op=ALU.mult),
            reads=[posf, invf_sb], writes=[ang])
        C1 = 6.28125
        C2 = 2.0 * math.pi - 6.28125
        angi = sb("angi", [128, NT + 1, 32], I32)
        S.op("dve", lambda e: e.tensor_single_scalar(out=angm.t[:], in_=ang.t[:], scalar=1.0 / (2.0 * math.pi), op=ALU.mult),
             reads=[ang], writes=[angm])
        S.op("dve", lambda e: e.tensor_copy(out=angi.t[:], in_=angm.t[:]), reads=[angm], writes=[angi])
        S.op("dve", lambda e: e.tensor_copy(out=angm.t[:], in_=angi.t[:]), reads=[angi], writes=[angm])
        S.op("dve", lambda e: e.scalar_tensor_tensor(out=ang.t[:], in0=angm.t[:], scalar=-C1, in1=ang.t[:], op0=ALU.mult, op1=ALU.add),
             reads=[angm, ang], writes=[ang])
        S.op("dve", lambda e: e.scalar_tensor_tensor(out=ang.t[:], in0=angm.t[:], scalar=-C2, in1=ang.t[:], op0=ALU.mult, op1=ALU.add),
             reads=[angm, ang], writes=[ang])
        SC = [-1.0 / 6, 1.0 / 120, -1.0 / 5040, 1.0 / 362880, -1.0 / 39916800]
        CC = [-1.0 / 2, 1.0 / 24, -1.0 / 720, 1.0 / 40320, -1.0 / 3628800, 1.0 / 479001600]
        s_v = sinS.t[:, :, 1, :]
        t_v = sinS.t[:, :, 0, :]
        S.op("dve", lambda e: e.tensor_single_scalar(out=ang.t[:], in_=ang.t[:], scalar=0.5, op=ALU.mult), reads=[ang], writes=[ang])
        S.op("dve", lambda e: e.tensor_tensor(out=angm.t[:], in0=ang.t[:], in1=ang.t[:], op=ALU.mult), reads=[ang], writes=[angm])

        def horner(dst, coefs, dbuf):
            S.op("dve", lambda e: e.tensor_single_scalar(out=dst, in_=angm.t[:], scalar=coefs[-1], op=ALU.mult), reads=[angm], writes=[dbuf])
            for a in reversed(coefs[:-1]):
                S.op("dve", lambda e, a=a: e.scalar_tensor_tensor(out=dst, in0=dst, scalar=a, in1=angm.t[:], op0=ALU.add, op1=ALU.mult),
                     reads=[dbuf, angm], writes=[dbuf])

        horner(cosT.t[:], SC, cosT)
        S.op("dve", lambda e: e.scalar_tensor_tensor(out=s_v, in0=cosT.t[:], scalar=1.0, in1=ang.t[:], op0=ALU.add, op1=ALU.mult),
             reads=[cosT, ang], writes=[sinS])
        horner(cosT.t[:], CC, cosT)
        S.op("dve", lambda e: e.tensor_single_scalar(out=cosT.t[:], in_=cosT.t[:], scalar=1.0, op=ALU.add), reads=[cosT], writes=[cosT])
        S.op("dve", lambda e: e.scalar_tensor_tensor(out=ang.t[:], in0=s_v, scalar=2.0, in1=cosT.t[:], op0=ALU.mult, op1=ALU.mult),
             reads=[sinS, cosT], writes=[ang])
        S.op("dve", lambda e: e.tensor_tensor(out=angm.t[:], in0=s_v, in1=s_v, op=ALU.mult), reads=[sinS], writes=[angm])
        S.op("dve", lambda e: e.tensor_scalar(out=cosT.t[:], in0=angm.t[:], scalar1=-2.0, scalar2=1.0, op0=ALU.mult, op1=ALU.add),
             reads=[angm], writes=[cosT])
        S.op("dve", lambda e: e.tensor_copy(out=s_v, in_=ang.t[:]), reads=[ang], writes=[sinS])
        S.op("dve", lambda e: e.tensor_single_scalar(out=t_v, in_=ang.t[:], scalar=-1.0, op=ALU.mult), reads=[ang], writes=[sinS])
        S.op("act", lambda e: e.activation(out=esink.t[:], in_=sink_bc.t[:], func=AF.Exp), reads=[sink_bc], writes=[esink])
        S.op("act", lambda e: e.activation(out=ecol.t[:], in_=ccol.t[:], func=AF.Exp, scale=-1.0), reads=[ccol], writes=[ecol])
        S.op("dve", lambda e: e.tensor_single_scalar(out=ecol.t[:], in_=ecol.t[:], scalar=1.0, op=ALU.add), reads=[ecol], writes=[ecol])
        S.op("dve", lambda e: e.reciprocal(out=ecol.t[:], in_=ecol.t[:]), reads=[ecol], writes=[ecol])
        S.op("dve", lambda e: e.tensor_tensor(out=siluc.t[:], in0=ccol.t[:], in1=ecol.t[:], op=ALU.mult),
             reads=[ccol, ecol], writes=[siluc])

        S.off = stop < 2
        WQ = "sp"
        modbanks = [b0, b1, b3, b4, b5, b6]
        stg_i = [0]

        def mod_phase(c0, ngrp, gbase):
            for k in range(KC):
                st = stage[stg_i[0] % NSTG]
                S.dma(WQ, f"stg{stg_i[0] % NSTG}", st.t[:, 0:ngrp * 512], w_ada[k * 128:(k + 1) * 128, c0:c0 + ngrp * 512], writes=[st])
                stg_i[0] += 1
                for g in range(ngrp):
                    S.op("pe", lambda e, k=k, g=g, st=st: e.matmul(
                        modbanks[gbase + g].t[0:1, :], lhsT=siluc.t[:, k:k + 1], rhs=st.t[:, g * 512:(g + 1) * 512],
                        start=(k == 0), stop=(k == KC - 1)), reads=[siluc, st], writes=[modbanks[gbase + g]], inc=(g == ngrp - 1))
            for g in range(ngrp):
                gg = gbase + g
                S.op("dve", lambda e, g=g, gg=gg: e.tensor_tensor(out=modrow.t[0:1, gg * 512:(gg + 1) * 512], in0=modbanks[gg].t[0:1, :],
                                                                  in1=modrow.t[0:1, gg * 512:(gg + 1) * 512], op=ALU.add),
                     reads=[modbanks[gg], modrow], writes=[modrow])

        mod_phase(0, 4, 0)
        for j in range(16):
            src = (0 if j < 8 else D) + (j % 8) * 128
            S.op("pe", lambda e, j=j, src=src: e.matmul(b7.t[:, j:j + 1], lhsT=modrow.t[0:1, src:src + 128], rhs=ones_f.t[0:1, 0:1],
                                                        start=True, stop=True), reads=[modrow, ones_f], writes=[b7], inc=(j == 15))
        S.op("dve", lambda e: e.tensor_copy(out=shcol.t[:], in_=b7.t[:, 0:8]), reads=[b7], writes=[shcol])
        S.op("dve", lambda e: e.scalar_tensor_tensor(out=gscol.t[:], in0=b7.t[:, 8:16], scalar=1.0, in1=gncol.t[:],
                                                     op0=ALU.add, op1=ALU.mult), reads=[b7, gncol], writes=[gscol])
        S.op("dve", lambda e: e.tensor_single_scalar(out=gscol.t[:], in_=gscol.t[:], scalar=32.0, op=ALU.mult),
             reads=[gscol], writes=[gscol])

        S.off = stop < 3
        hlf = DIN // 2
        for k in range(KC):
            for hh in range(2):
                st = stage[stg_i[0] % NSTG]
                S.dma(WQ, f"stg{stg_i[0] % NSTG}", st.t[:, 0:hlf], w_in[k * 128:(k + 1) * 128, hh * hlf:(hh + 1) * hlf], writes=[st])
                stg_i[0] += 1
                if hh == 0:
                    S.op("act", lambda e, k=k, st=st: e.activation(out=win.t[:, k, 0:hlf], in_=st.t[:, 0:hlf], func=AF.Copy),
                         reads=[st], writes=[win])
                else:
                    S.op("dve", lambda e, k=k, st=st: e.tensor_copy(out=win.t[:, k, hlf:DIN], in_=st.t[:, 0:hlf]), reads=[st], writes=[win])
        S.off = stop < 4
        mod_phase(2 * D, 2, 4)
        for g in range(2):
            bb = (b3, b4)[g]
            S.op("pe", lambda e, g=g, bb=bb: e.matmul(bb.t[:, :], lhsT=ones_f.t[0:1, :], rhs=modrow.t[0:1, 2 * D + g * 512:2 * D + (g + 1) * 512],
                                                      start=True, stop=True), reads=[ones_f, modrow], writes=[bb])
            S.op("act", lambda e, g=g, bb=bb: e.activation(out=gate_bc.t[:, g * 512:(g + 1) * 512], in_=bb.t[:, :], func=AF.Copy),
                 reads=[bb], writes=[gate_bc])
        for k in range(KC):
            st = stage[stg_i[0] % NSTG]
            S.dma(WQ, f"stg{stg_i[0] % NSTG}", st.t[:, 0:D], w_out[k * 128:(k + 1) * 128, :], writes=[st])
            stg_i[0] += 1
            eng = "dve"
            S.op(eng, lambda e, k=k, st=st: e.scalar_tensor_tensor(out=wout.t[:, k, :], in0=st.t[:, 0:D], scalar=gmcol.t[:, k:k + 1],
                                                                   in1=gate_bc.t[:], op0=ALU.mult, op1=ALU.mult),
                 reads=[st, gmcol, gate_bc], writes=[wout])

        S.capture = None
        H = lambda i: i % 2
        HX = lambda i: i % 3
        ACT_EVAC = 0

        def front(x_ap, slot, xbuf):
            S.dma("sp", f"x{slot}", xbuf.t[:], x_ap, writes=[xbuf])
            S.op("act", lambda e: e.activation(out=hb.t[:], in_=xbuf.t[:], func=AF.Square, accum_out=ssq.t[:, 0:1]),
                 reads=[xbuf], writes=[hb, ssq])
            S.op("act", lambda e: e.activation(out=lnvA.t[:, 0:1], in_=ssq.t[:], func=AF.Ln, bias=eps_d.t[:, 0:1]), reads=[ssq, eps_d], writes=[lnvA])
            S.op("act", lambda e: e.activation(out=rstd.t[:], in_=lnvA.t[:, 0:1], func=AF.Exp, scale=-0.5), reads=[lnvA], writes=[rstd])
            S.op("dve", lambda e: e.tensor_scalar(out=hb.t[:], in0=xbuf.t[:], scalar1=rstd.t[:, 0:1], scalar2=None, op0=ALU.mult),
                 reads=[xbuf, rstd], writes=[hb])
            for k in range(KC):
                S.op("pe", lambda e, k=k: e.transpose(ptb.t[:, k, :], hb.t[:, k * 128:(k + 1) * 128], ident_bf.t[:]),
                     reads=[hb, ident_bf], writes=[ptb], inc=(k == KC - 1))
            for k in range(KC):
                if k < ACT_EVAC:
                    S.op("act", lambda e, k=k: e.activation(out=hTk[k].t[:], in_=ptb.t[:, k, :], func=AF.Identity,
                                                            scale=gscol.t[:, k:k + 1], bias=shcol.t[:, k:k + 1]),
                         reads=[ptb, gscol, shcol], writes=[hTk[k]])
                else:
                    S.op("dve", lambda e, k=k: e.tensor_scalar(out=hTk[k].t[:], in0=ptb.t[:, k, :], scalar1=gscol.t[:, k:k + 1],
                                                               scalar2=shcol.t[:, k:k + 1], op0=ALU.mult, op1=ALU.add),
                         reads=[ptb, gscol, shcol], writes=[hTk[k]])

        def proj(bank, o, w):
            for k in range(KC):
                S.op("pe", lambda e, k=k: e.matmul(bank.t[:, 0:w], lhsT=hTk[k].t[:], rhs=win.t[:, k, o:o + w], start=(k == 0), stop=(k == KC - 1)),
                     reads=[hTk[k], win], writes=[bank], inc=(k % 3 == 2 or k == KC - 1))

        def rope_k(src_bank, col0, tcol, krb, vab):
            skv = src_bank.t[:, col0:col0 + 128].rearrange("p (g a f) -> p g a f", g=2, a=2)
            cosb = cosT.t[:, tcol, :].unsqueeze(1).unsqueeze(1).broadcast_to([128, 2, 2, 32])
            S.op("dve", lambda e: e.tensor_tensor(out=tmpck.t[:].rearrange("p (g a f) -> p g a f", g=2, a=2), in0=skv, in1=cosb, op=ALU.mult),
                 reads=[src_bank, cosT], writes=[tmpck])
            for a in range(2):
                S.op("dve", lambda e, a=a: e.tensor_tensor(
                    out=tmpsk.t[:].rearrange("p (g a f) -> p g a f", g=2, a=2)[:, :, a, :], in0=skv[:, :, 1 - a, :],
                    in1=sinS.t[:, tcol, a, :].unsqueeze(1).broadcast_to([128, 2, 32]), op=ALU.mult),
                    reads=[src_bank, sinS], writes=[tmpsk])
            S.op("pool", lambda e: e.tensor_tensor(out=krb.t[:], in0=tmpck.t[:], in1=tmpsk.t[:], op=ALU.add),
                 reads=[tmpck, tmpsk], writes=[krb])
            S.op("dve", lambda e: e.tensor_copy(out=vab.t[:, :, 0:64],
                                                in_=src_bank.t[:, col0 + 128:col0 + 256].rearrange("p (g f) -> p g f", g=2)),
                 reads=[src_bank], writes=[vab])

        def gate(bank, gbuf):
            S.op("act", lambda e: e.activation(out=e_t.t[:], in_=bank.t[:, :], func=AF.Exp, scale=-1.0), reads=[bank], writes=[e_t])
            S.op("act", lambda e: e.activation(out=e_t.t[:], in_=e_t.t[:], func=AF.Ln, bias=1.0), reads=[e_t], writes=[e_t])
            S.op("act", lambda e: e.activation(out=e_t.t[:], in_=e_t.t[:], func=AF.Exp, scale=-1.0), reads=[e_t], writes=[e_t])
            S.op("dve", lambda e: e.tensor_tensor(out=gbuf.t[:], in0=bank.t[:, :], in1=e_t.t[:], op=ALU.mult),
                 reads=[bank, e_t], writes=[gbuf])

        def gla_decay_common(gab):
            gT_ps = b4.t[:, 256:384].bitcast(BF16)[0:16, 0:128]
            S.op("pe", lambda e: e.transpose(gT_ps, gab.t[:, 0:16], ident_bf.t[:]), reads=[gab, ident_bf], writes=[b4])
            S.op("dve", lambda e: e.tensor_copy(out=gaT.t[0:16, :], in_=gT_ps), reads=[b4], writes=[gaT])
            S.op("pe", lambda e: e.matmul(b3.t[:, 0:256], lhsT=gaT.t[0:17, :], rhs=wdec_bf.t[0:17, :], start=True, stop=True),
                 reads=[gaT, wdec_bf], writes=[b3])
            S.op("act", lambda e: e.activation(out=e_z.t[:], in_=b3.t[:, 0:256], func=AF.Exp, scale=-1.0), reads=[b3], writes=[e_z])
            S.op("act", lambda e: e.activation(out=Lz.t[:], in_=e_z.t[:], func=AF.Ln, bias=1.0), reads=[e_z], writes=[Lz])

        def state_update(ub):
            for p in range(2):
                for hp in range(2):
                    r0 = 64 * hp
                    h = 2 * p + hp
                    S.op("dve", lambda e, p=p, r0=r0, h=h: e.scalar_tensor_tensor(
                        out=St.t[r0:r0 + 64, p, :], in0=St.t[r0:r0 + 64, p, :], scalar=decay.t[r0:r0 + 64, p:p + 1],
                        in1=ub.t[r0:r0 + 64, h * 128:(h + 1) * 128], op0=ALU.mult, op1=ALU.add),
                        reads=[St, decay, ub], writes=[St])

        def u_matmuls(ub, vb):
            for h in range(4):
                p = h // 2
                S.op("pe", lambda e, h=h, p=p: e.matmul(ub.t[:, h * 128:(h + 1) * 128], lhsT=k_tail.t[:, p * 128:(p + 1) * 128],
                                                        rhs=vb.t[:, h * 128:(h + 1) * 128], start=True, stop=True),
                     reads=[k_tail, vb], writes=[ub], inc=(h == 3))

        def sections(names):
            return {k: [] for k in names}

        def A_pre(t, i):
            sec = sections(("a0", "a1", "a2", "a3"))
            xbuf = xin[HX(i)]
            qk, vb, gab = QK[H(i)], VV[H(i)], GA[H(i)]
            S.capture = sec["a0"]
            front(xp[t * 128:(t + 1) * 128, :], HX(i), xbuf)
            S.capture = sec["a1"]
            proj(b0, O_GK, 512)
            S.op("act", lambda e: e.activation(out=qk.t[:, 256:512], in_=b0.t[:, 0:256], func=AF.Copy), reads=[b0], writes=[qk])
            S.op("act", lambda e: e.activation(out=vb.t[:, 0:256], in_=b0.t[:, 256:512], func=AF.Copy), reads=[b0], writes=[vb])
            S.capture = sec["a2"]
            proj(b1, O_GV + 256, 272)
            S.op("act", lambda e: e.activation(out=vb.t[:, 256:512], in_=b1.t[:, 0:256], func=AF.Copy), reads=[b1], writes=[vb])
            S.op("act", lambda e: e.activation(out=gab.t[:], in_=b1.t[:, 256:272], func=AF.Copy), reads=[b1], writes=[gab])
            if t == NP - 1:
                proj(b0, O_SK, 256)
                rope_k(b0, 0, NT, KR[H(i)], vaug[2])
            S.capture = None
            return sec

        def B_pre(t, i):
            sec = sections(("b0", "b1"))
            qk, vb, gab = QK[H(i)], VV[H(i)], GA[H(i)]
            S.capture = sec["b0"]
            if t == NP - 1:
                krb = KR[H(i)]
                S.op("pe", lambda e: e.transpose(ptb.t[:, 0, :], krb.t[:], ident_bf.t[:]), reads=[krb, ident_bf], writes=[ptb])
                for g in range(2):
                    S.op("act", lambda e, g=g: e.activation(out=kT[1].t[64 * g:64 * g + 64, g, :], in_=ptb.t[64 * g:64 * g + 64, 0, :], func=AF.Copy),
                         reads=[ptb], writes=[kT[1]])
            gla_decay_common(gab)
            S.op("pe", lambda e: e.matmul(b3.t[:, 256:512], lhsT=trisuf.t[:], rhs=Lz.t[:], start=True, stop=True),
                 reads=[trisuf, Lz], writes=[b3])
            for p in range(2):
                S.op("pe", lambda e, p=p: e.matmul(b4.t[:, p:p + 1], lhsT=Lz.t[:, p * 128:(p + 1) * 128], rhs=negcol.t[:, 0:1], start=True, stop=True),
                     reads=[Lz, negcol], writes=[b4], inc=(p == 1))
            S.op("act", lambda e: e.activation(out=Esuf.t[:], in_=b3.t[:, 256:512], func=AF.Exp), reads=[b3], writes=[Esuf])
            S.op("act", lambda e: e.activation(out=decay.t[:], in_=b4.t[:, 0:2], func=AF.Exp), reads=[b4], writes=[decay])
            S.op("pool", lambda e: e.tensor_tensor(out=k_tail.t[:], in0=qk.t[:, 256:512], in1=Esuf.t[:], op=ALU.mult),
                 reads=[qk, Esuf], writes=[k_tail])
            S.capture = sec["b1"]
            u_matmuls(b5, vb)
            state_update(b5)
            if t == NP - 1:
                S.op("dve", lambda e: e.tensor_scalar(out=St.t[:], in0=St.t[:], scalar1=flag_sb.t[:, 0:1], scalar2=None, op0=ALU.mult),
                     reads=[St, flag_sb], writes=[St])
                S.op("act", lambda e: e.activation(out=St_bf.t[:], in_=St.t[:], func=AF.Copy), reads=[St], writes=[St_bf])
            S.capture = None
            return sec

        def A_main(t, i):
            sec = sections(("a0", "a1", "a2", "a3"))
            xbuf = xin[HX(i)]
            qk, vb, gab, krb, qrb, ggb, gsb = QK[H(i)], VV[H(i)], GA[H(i)], KR[H(i)], QR[H(i)], GG[H(i)], GS[H(i)]
            S.capture = sec["a0"]
            front(xm[t * 128:(t + 1) * 128, :], HX(i), xbuf)
            S.capture = sec["a1"]
            proj(b0, O_GQ, 512)
            S.op("act", lambda e: e.activation(out=qk.t[:], in_=b0.t[:, :], func=AF.Copy), reads=[b0], writes=[qk])
            proj(b1, O_GV, 512)
            S.op("dve", lambda e: e.tensor_copy(out=vb.t[:], in_=b1.t[:, :]), reads=[b1], writes=[vb])
            S.capture = sec["a2"]
            proj(b0, O_GA, 272)
            S.op("act", lambda e: e.activation(out=gab.t[:], in_=b0.t[:, 0:16], func=AF.Copy), reads=[b0], writes=[gab])
            rope_k(b0, 16, t, krb, vaug[t % 3])
            proj(b1, O_GZ, 512)
            gate(b1, ggb)
            S.capture = sec["a3"]
            proj(b0, O_SQ, 512)
            sqv = b0.t[:, :].rearrange("p (h a f) -> p h a f", h=8, a=2)
            S.op("dve", lambda e: e.tensor_tensor(
                out=tmpc.t[:].rearrange("p (h a f) -> p h a f", h=8, a=2), in0=sqv,
                in1=cosT.t[:, t, :].unsqueeze(1).unsqueeze(1).broadcast_to([128, 8, 2, 32]), op=ALU.mult),
                reads=[b0, cosT], writes=[tmpc])
            for a in range(2):
                S.op("dve", lambda e, a=a: e.tensor_tensor(
                    out=tmps.t[:].rearrange("p (h a f) -> p h a f", h=8, a=2)[:, :, a, :], in0=sqv[:, :, 1 - a, :],
                    in1=sinS.t[:, t, a, :].unsqueeze(1).broadcast_to([128, 8, 32]), op=ALU.mult),
                    reads=[b0, sinS], writes=[tmps])
            S.op("pool", lambda e: e.tensor_tensor(out=qrb.t[:].rearrange("p (j g f) -> p g j f", j=4, g=2),
                                                   in0=tmpc.t[:].rearrange("p (g j f) -> p g j f", g=2, j=4),
                                                   in1=tmps.t[:].rearrange("p (g j f) -> p g j f", g=2, j=4), op=ALU.add),
                 reads=[tmpc, tmps], writes=[qrb])
            proj(b1, O_SZ, 512)
            gate(b1, gsb)
            S.capture = None
            return sec

        def B_main(t, i):
            sec = sections(("gla1", "swa1", "gla2", "swa2", "out"))
            xbuf = xin[HX(i)]
            qk, vb, gab, krb, qrb, ggb, gsb = QK[H(i)], VV[H(i)], GA[H(i)], KR[H(i)], QR[H(i)], GG[H(i)], GS[H(i)]
            cur, prev = t % 2, (t + 1) % 2
            vcur, vprev = vaug[t % 3], vaug[(t - 1) % 3]
            S.capture = sec["gla1"]
            gla_decay_common(gab)
            for p in range(2):
                S.op("pe", lambda e, p=p: e.matmul(b4.t[:, p * 128:(p + 1) * 128], lhsT=Lz.t[:, p * 128:(p + 1) * 128], rhs=tricum.t[:],
                                                   start=True, stop=True), reads=[Lz, tricum], writes=[b4], inc=(p == 1))
            S.op("pe", lambda e: e.matmul(b3.t[:, 256:512], lhsT=trisuf.t[:], rhs=Lz.t[:], start=True, stop=True),
                 reads=[trisuf, Lz], writes=[b3])
            b4v = b4.t[:, 0:256].rearrange("p (a t) -> p a t", a=2)
            S.op("act", lambda e: e.activation(out=Eplus.t[:], in_=b4v, func=AF.Exp, bias=nln8.t[:, 0:1]), reads=[b4, nln8], writes=[Eplus])
            S.op("act", lambda e: e.activation(out=Eminus.t[:], in_=b4v, func=AF.Exp, scale=-1.0), reads=[b4], writes=[Eminus])
            S.op("act", lambda e: e.activation(out=Esuf.t[:], in_=b3.t[:, 256:512], func=AF.Exp), reads=[b3], writes=[Esuf])
            S.op("dve", lambda e: e.reciprocal(out=decay.t[:], in_=Eminus.t[:, :, 127]), reads=[Eminus], writes=[decay])
            S.capture = sec["gla2"]
            t5 = b5.t[:, 0:256].bitcast(BF16).rearrange("p (j t) -> p j t", j=4)
            for j in range(4):
                S.op("pe", lambda e, j=j: e.transpose(t5[:, j, :], qk.t[:, j * 128:(j + 1) * 128], ident_bf.t[:]),
                     reads=[qk, ident_bf], writes=[b5], inc=(j == 3))
            S.op("act", lambda e: e.activation(out=qkT_sb.t[:], in_=t5, func=AF.Copy), reads=[b5], writes=[qkT_sb])
            for hp in range(2):
                r0 = 64 * hp
                S.op("dve", lambda e, hp=hp, r0=r0: e.tensor_tensor(
                    out=q_dT.t[r0:r0 + 64, :, :].rearrange("r (p two) t -> r p two t", two=2)[:, :, hp, :],
                    in0=qkT_sb.t[r0:r0 + 64, 0:2, :], in1=Eplus.t[r0:r0 + 64, :, :], op=ALU.mult),
                    reads=[qkT_sb, Eplus], writes=[q_dT])
            S.op("pool", lambda e: e.tensor_tensor(out=k_dT.t[:], in0=qkT_sb.t[:, 2:4, :], in1=Eminus.t[:], op=ALU.mult),
                 reads=[qkT_sb, Eminus], writes=[k_dT])
            S.op("pool", lambda e: e.tensor_tensor(out=k_tail.t[:], in0=qk.t[:, 256:512], in1=Esuf.t[:], op=ALU.mult),
                 reads=[qk, Esuf], writes=[k_tail])
            for h in range(4):
                p = h // 2
                S.op("pe", lambda e, h=h, p=p: e.matmul(b5.t[:, h * 128:(h + 1) * 128], lhsT=k_dT.t[:, p, :],
                                                        rhs=q_dT.t[:, h, :], start=True, stop=True),
                     reads=[k_dT, q_dT], writes=[b5], inc=(h == 3))
            S.op("dve", lambda e: e.tensor_tensor(out=AT_sb.t[:].rearrange("p (h i) -> p h i", h=4),
                                                  in0=b5.t[:, :].rearrange("p (h i) -> p h i", h=4),
                                                  in1=mgla.t[:].unsqueeze(1).broadcast_to([128, 4, 128]), op=ALU.mult),
                 reads=[b5, mgla], writes=[AT_sb])
            for h in range(4):
                p = h // 2
                S.op("pe", lambda e, h=h: e.matmul(b5.t[:, h * 128:(h + 1) * 128], lhsT=AT_sb.t[:, h * 128:(h + 1) * 128],
                                                   rhs=vb.t[:, h * 128:(h + 1) * 128], start=True, stop=False),
                     reads=[AT_sb, vb], writes=[b5], inc=False)
                S.op("pe", lambda e, h=h, p=p: e.matmul(b5.t[:, h * 128:(h + 1) * 128], lhsT=q_dT.t[:, h, :],
                                                        rhs=St_bf.t[:, p, :], start=False, stop=True),
                     reads=[q_dT, St_bf], writes=[b5], inc=(h == 3))
            for h in range(4):
                S.op("act", lambda e, h=h: e.activation(out=junk2.t[:], in_=b5.t[:, h * 128:(h + 1) * 128], func=AF.Square,
                                                        accum_out=ssq_g.t[:, h:h + 1]), reads=[b5], writes=[junk2, ssq_g])
            S.op("act", lambda e: e.activation(out=lnvB.t[:, 0:4], in_=ssq_g.t[:], func=AF.Ln, bias=eps_h.t[:, 0:1]), reads=[ssq_g, eps_h], writes=[lnvB])
            S.op("act", lambda e: e.activation(out=rstd_g.t[:], in_=lnvB.t[:, 0:4], func=AF.Exp, scale=-0.5), reads=[lnvB], writes=[rstd_g])
            S.op("dve", lambda e: e.tensor_tensor(out=tmp_g.t[:].rearrange("p (h v) -> p h v", h=4),
                                                  in0=b5.t[:, :].rearrange("p (h v) -> p h v", h=4),
                                                  in1=rstd_g.t[:].unsqueeze(2).broadcast_to([128, 4, 128]), op=ALU.mult),
                 reads=[b5, rstd_g], writes=[tmp_g])
            S.op("pool", lambda e: e.tensor_tensor(out=mix.t[:, 0:512], in0=tmp_g.t[:], in1=ggb.t[:], op=ALU.mult),
                 reads=[tmp_g, ggb], writes=[mix])
            u_matmuls(b5, vb)
            state_update(b5)
            S.op("act", lambda e: e.activation(out=St_bf.t[:], in_=St.t[:], func=AF.Copy), reads=[St], writes=[St_bf])
            S.capture = sec["swa1"]
            t7 = b7.t[:, :].bitcast(BF16).rearrange("p (j t) -> p j t", j=8)
            for j in range(4):
                S.op("pe", lambda e, j=j: e.transpose(t7[:, j, :], qrb.t[:, j * 128:(j + 1) * 128], ident_bf.t[:]),
                     reads=[qrb, ident_bf], writes=[b7], inc=False)
            S.op("pe", lambda e: e.transpose(t7[:, 4, :], krb.t[:], ident_bf.t[:]), reads=[krb, ident_bf], writes=[b7])
            S.op("act", lambda e: e.activation(out=qT.t[:].rearrange("p (j q) -> p j q", j=4), in_=t7[:, 0:4, :], func=AF.Copy), reads=[b7], writes=[qT])
            for g in range(2):
                S.op("dve", lambda e, g=g: e.tensor_copy(out=kT[cur].t[64 * g:64 * g + 64, g, :], in_=t7[64 * g:64 * g + 64, 4, :]),
                     reads=[b7], writes=[kT[cur]])
            mbp = 2 if t == 0 else 1
            for g in range(2):
                for which, bank, kbuf, mi in ((0, b6, kT[prev], mbp), (1, b7, kT[cur], 0)):
                    S.op("pe", lambda e, bank=bank, mi=mi: e.matmul(bank.t[:, :], lhsT=ident_bf.t[:], rhs=mb.t[:, mi, :], start=True, stop=False),
                         reads=[ident_bf, mb], writes=[bank], inc=False)
                    S.op("pe", lambda e, bank=bank, kbuf=kbuf, g=g: e.matmul(
                        bank.t[:, :], lhsT=kbuf.t[:, g, :], rhs=qT.t[:, :],
                        start=False, stop=True), reads=[kbuf, qT], writes=[bank])
                    S.op("act", lambda e, bank=bank, g=g, which=which: e.activation(out=PT[g].t[:, which, :], in_=bank.t[:, :], func=AF.Exp, scale=0.125),
                         reads=[bank], writes=[PT[g]])
            S.capture = sec["swa2"]
            for g in range(2):
                ob = (b3, b4)[g]
                for j in range(4):
                    S.op("pe", lambda e, g=g, j=j, ob=ob: e.matmul(ob.t[:, j * 65:(j + 1) * 65], lhsT=PT[g].t[:, 0, j * 128:(j + 1) * 128],
                                                                   rhs=vprev.t[:, g, :], start=True, stop=False),
                         reads=[PT[g], vprev], writes=[ob], inc=False)
                    S.op("pe", lambda e, g=g, j=j, ob=ob: e.matmul(ob.t[:, j * 65:(j + 1) * 65], lhsT=PT[g].t[:, 1, j * 128:(j + 1) * 128],
                                                                   rhs=vcur.t[:, g, :], start=False, stop=True),
                         reads=[PT[g], vcur], writes=[ob], inc=(j == 3))
            for g in range(2):
                ob = (b3, b4)[g]
                obv = ob.t[:, 0:260].rearrange("p (j f) -> p j f", j=4)
                S.op("dve", lambda e, g=g, obv=obv: e.tensor_tensor(out=den.t[:, 4 * g:4 * g + 4], in0=obv[:, :, 64], in1=esink.t[:, 4 * g:4 * g + 4], op=ALU.add),
                     reads=[ob, esink], writes=[den])
            S.op("dve", lambda e: e.reciprocal(out=rden.t[:], in_=den.t[:]), reads=[den], writes=[rden])
            for g in range(2):
                ob = (b3, b4)[g]
                obv = ob.t[:, 0:260].rearrange("p (j f) -> p j f", j=4)
                S.op("dve", lambda e, g=g, obv=obv: e.tensor_tensor(
                    out=tmp_s.t[:, 256 * g:256 * (g + 1)].rearrange("p (j f) -> p j f", j=4), in0=obv[:, :, 0:64],
                    in1=rden.t[:, 4 * g:4 * g + 4].unsqueeze(2).broadcast_to([128, 4, 64]), op=ALU.mult),
                    reads=[ob, rden], writes=[tmp_s])
            S.op("pool", lambda e: e.tensor_tensor(out=mix.t[:, 512:1024], in0=tmp_s.t[:], in1=gsb.t[:], op=ALU.mult),
                 reads=[tmp_s, gsb], writes=[mix])
            S.capture = sec["out"]
            for k in range(KC):
                S.op("pe", lambda e, k=k: e.transpose(ptb.t[:, k, :], mix.t[:, k * 128:(k + 1) * 128], ident_bf.t[:]),
                     reads=[mix, ident_bf], writes=[ptb], inc=(k == KC - 1))
            S.op("act", lambda e: e.activation(out=mixT.t[:], in_=ptb.t[:], func=AF.Copy), reads=[ptb], writes=[mixT])
            for n in range(2):
                bank = (b0, b1)[n]
                for k in range(KC):
                    S.op("pe", lambda e, k=k, n=n, bank=bank: e.matmul(bank.t[:, :], lhsT=mixT.t[:, k, :], rhs=wout.t[:, k, n * 512:(n + 1) * 512],
                                                                       start=(k == 0), stop=(k == KC - 1)),
                         reads=[mixT, wout], writes=[bank], inc=(k % 4 == 3))
                S.op("dve", lambda e, n=n, bank=bank: e.tensor_tensor(out=xnew.t[:, n * 512:(n + 1) * 512], in0=bank.t[:, :],
                                                                      in1=xbuf.t[:, n * 512:(n + 1) * 512], op=ALU.add),
                     reads=[bank, xbuf], writes=[xnew])
            yo = yout[t % 2]
            S.op("act", lambda e: e.activation(out=yo.t[:], in_=xnew.t[:], func=AF.Square, accum_out=ssq2.t[:, 0:1]),
                 reads=[xnew], writes=[yo, ssq2])
            S.op("act", lambda e: e.activation(out=lnvC.t[:, 0:1], in_=ssq2.t[:], func=AF.Ln, bias=eps_d.t[:, 0:1]), reads=[ssq2, eps_d], writes=[lnvC])
            S.op("act", lambda e: e.activation(out=rstd2.t[:], in_=lnvC.t[:, 0:1], func=AF.Exp, scale=-0.5), reads=[lnvC], writes=[rstd2])
            S.op("dve", lambda e: e.scalar_tensor_tensor(out=yo.t[:], in0=xnew.t[:], scalar=rstd2.t[:, 0:1], in1=gfin_bc.t[:],
                                                          op0=ALU.mult, op1=ALU.mult), reads=[xnew, rstd2, gfin_bc], writes=[yo])
            S.dma("sp", f"o{t % 2}", out[t * 128:(t + 1) * 128, :], yo.t[:], reads=[yo])
            S.capture = None
            return sec

        S.off = stop < 5
        tiles = [("p", t) for t in range(NP)] + [("m", t) for t in range(NT)]
        L = list(L_setup)
        mkA = lambda i: (A_pre(tiles[i][1], i) if tiles[i][0] == "p" else A_main(tiles[i][1], i))
        secA = mkA(0) if tiles else None
        if tiles:
            for nm in ("a0", "a1", "a2", "a3"):
                L += secA[nm]
        for i, (kind, t) in enumerate(tiles):
            secB = B_pre(t, i) if kind == "p" else B_main(t, i)
            for nm in (("b0", "b1") if kind == "p" else ("gla1", "gla2", "swa1", "swa2")):
                L += secB[nm]
            if i + 1 < len(tiles):
                secA = mkA(i + 1)
                for nm in ("a0", "a1", "a2", "a3"):
                    L += secA[nm]
            if kind == "m":
                L += secB["out"]
        if not S.off:
            S.schedule(L)
        S.off = False
        S.final_wait("sp", ["o0", "o1", "ld0", "stg0", "stg1", "stg2", "stg3"])

        S.ops = {e: [o for o in S.ops[e] if o is not None] for e in S.ENG}
        keys = S.sem_keys()
        sems = {k: es.enter_context(nc.semaphore(f"s_{k}")) for k in keys}
        block = es.enter_context(nc.Block())

        def emit(eng_name):
            def body(e):
                for (waits, fn, inc) in S.ops[eng_name]:
                    for (s, v) in waits:
                        e.wait_ge(sems[s], v)
                    if fn is not None:
                        ins = fn(e)
                        if inc is not None:
                            ins.then_inc(sems[inc[0]], inc[1])
            return body

        block.tensor(emit("pe"))
        block.scalar(emit("act"))
        block.vector(emit("dve"))
        block.gpsimd(emit("pool"))
        block.sync(emit("sp"))
    return nc


def _consts():
    j = np.arange(128)[:, None]
    i = np.arange(128)[None, :]
    c = {}
    c["c_ident"] = np.eye(128, dtype=np.float32)
    c["c_tricum"] = np.where(j <= i, -1.0 / 16.0, 0.0).astype(np.float32)
    c["c_trisuf"] = np.where(j > i, -1.0 / 16.0, 0.0).astype(np.float32)
    c["c_mgla"] = np.where(j <= i, 1.0, 0.0).astype(np.float32)
    cur = np.where(j <= i, 0.0, NEG).astype(np.float32)
    prev = np.where(j > i, 0.0, NEG).astype(np.float32)
    c["c_mbcur"] = np.tile(cur, (1, 4))
    c["c_mbprev"] = np.tile(prev, (1, 4))
    inv_freq = (1.0 / (10000.0 ** (np.arange(0, 64, 2, dtype=np.float64) / 64.0))).astype(np.float32)
    c["invf"] = np.tile(inv_freq[None, :], (128, 1)).astype(np.float32)
    return c


_NC_CACHE = {}


def _col(v):
    return np.ascontiguousarray(v.reshape(-1, 128).T).astype(np.float32)


def make_in_maps(x, c, positions, w_ada, b_ada, g_norm, w_in, w_decay, b_decay, g_gla_head, sinks, w_out, g_final,
                 cfgs, NT, NP):
    cs = _consts()
    w_in_p = np.ascontiguousarray(w_in[0][:, PERM])
    wdec = np.concatenate([w_decay[0], b_decay[0][None, :]], axis=0).astype(np.float32)
    gmix = np.concatenate([_col(g_gla_head[0]), np.ones((128, 4), np.float32)], axis=1)
    maps = []
    for (b, s0, hasp) in cfgs:
        m = dict(cs)
        m["xm"] = np.ascontiguousarray(x[b, s0:s0 + NT * 128])
        npre = max(NP, 1) * 128
        if hasp:
            m["xp"] = np.ascontiguousarray(x[b, s0 - NP * 128:s0]) if NP > 0 else np.zeros((128, D), np.float32)
        else:
            m["xp"] = np.ascontiguousarray(x[b, 0:npre])
        pm = np.zeros((128, NT + 1), np.int32)
        pm[:, :NT] = positions[b, s0:s0 + NT * 128].reshape(NT, 128).T
        if hasp and NP > 0:
            pm[:, NT] = positions[b, s0 - 128:s0]
        m["posm"] = pm
        m["c_col"] = _col(c[b])
        m["w_ada"] = w_ada[0]
        m["b_ada"] = b_ada[0][None, :]
        m["gnorm_col"] = _col(g_norm[0])
        m["w_in"] = w_in_p
        m["wdec"] = wdec
        m["gmix_col"] = gmix
        m["sinks"] = sinks[0][None, :]
        m["w_out"] = w_out[0]
        m["g_final"] = g_final[None, :]
        m["flag_col"] = np.full((128, 1), 1.0 if hasp else 0.0, np.float32)
        m["c_mbprev0"] = cs["c_mbprev"] if hasp else np.full((128, 512), NEG, np.float32)
        maps.append({k: np.ascontiguousarray(v) for k, v in m.items()})
    return maps


def kernel(x, c, positions, w_ada, b_ada, g_norm, w_in, w_decay, b_decay, g_gla_head, sinks, w_out, g_final):
    args = [np.asarray(a) for a in (x, c, positions, w_ada, b_ada, g_norm, w_in, w_decay, b_decay, g_gla_head, sinks, w_out, g_final)]
    x = args[0]
    B, SEQ, _ = x.shape
    NT = NP = SEQ // 2 // 128
    cfgs = [(b, h * (SEQ // 2), h == 1) for b in range(B) for h in range(2)]
    key = (NT, NP)
    if key not in _NC_CACHE:
        _NC_CACHE[key] = build(NT, NP)
    nc = _NC_CACHE[key]
    maps = make_in_maps(*args, cfgs=cfgs, NT=NT, NP=NP)
    res = run_bass_kernel_spmd(nc, maps, core_ids=list(range(len(cfgs))))
    outp = np.empty((B, SEQ, D), np.float32)
    for i, (b, s0, _) in enumerate(cfgs):
        outp[b, s0:s0 + NT * 128] = res.results[i]["out"]
    return outp
```

```python
import math
import os as _os
from contextlib import ExitStack

import numpy as np
import concourse.bass as bass
import concourse.mybir as mybir
from concourse.bass_utils import run_bass_kernel_spmd

F32 = mybir.dt.float32
BF16 = mybir.dt.bfloat16
I32 = mybir.dt.int32
AF = mybir.ActivationFunctionType
ALU = mybir.AluOpType

D = 1024
KC = 8
DIN = 2832
EPS = 1e-6
NEG = -30000.0
SIN_S = 0.999999

O_GQ, O_GK, O_GV, O_GA, O_SK, O_SV, O_GZ, O_SQ, O_SZ = 0, 256, 512, 1024, 1040, 1168, 1296, 1808, 2320
PERM = np.concatenate([
    np.arange(0, 256), np.arange(256, 512), np.arange(512, 1024), np.arange(1024, 1040),
    np.arange(2064, 2192), np.arange(2192, 2320), np.arange(1040, 1552), np.arange(1552, 2064),
    np.arange(2320, 2832)])


class Buf:
    __slots__ = ("name", "t", "w", "r", "excl", "wread")

    def __init__(self, name, t, excl=False):
        self.name = name
        self.t = t
        self.w = None
        self.r = []
        self.excl = excl
        self.wread = False

    def __getitem__(self, k):
        return self.t[k]


class _Probe:
    def __init__(self):
        self.name = None
        self.args = ()
        self.kw = {}

    def __getattr__(self, name):
        def f(*args, **kw):
            self.name, self.args, self.kw = name, args, kw
            return self
        return f


def _est_cost(eng, fn):
    try:
        p = _Probe()
        fn(p)
        out = p.kw.get("out", p.args[0] if p.args else None)
        shp = list(out.shape)
        n = 1
        for d in shp[1:]:
            n *= int(d)
        if eng == "pe":
            if p.name == "transpose":
                return 0.08
            lhsT = p.kw.get("lhsT", p.args[1] if len(p.args) > 1 else None)
            f32 = lhsT is not None and lhsT.dtype == F32
            c = 0.012 + n / 1950.0
            return c * (4.5 if f32 else 1.0)
        if eng == "act":
            return 0.22 + n / 1150.0
        if eng == "dve":
            c = 0.12 + n / 950.0
            if p.name == "reciprocal":
                c = 0.12 + n / 160.0
            if p.name == "scalar_tensor_tensor":
                c = 0.15 + n / 850.0
            return c
        if eng == "pool":
            return 0.2 + n / 550.0
    except Exception:
        pass
    return None


class Sched:
    ENG = ("pe", "act", "dve", "pool", "sp")

    def __init__(self, self_sync=True):
        self.ops = {e: [] for e in self.ENG}
        self.cnt = {}
        self.seen = {e: {} for e in self.ENG}
        self.self_sync = self_sync
        self.off = False
        self.capture = None

    def _tickets(self, eng, reads, writes, xreads=()):
        tk = []
        for b in xreads:
            if b.w is not None and not (b.wread and b.w[0] == eng):
                tk.append(b.w)
            tk.extend(b.r)
        for b in reads:
            if b.excl:
                continue
            if b.w is not None:
                tk.append(b.w)
        for b in writes:
            if b.w is not None:
                tk.append(b.w)
            tk.extend(b.r)
        need = {}
        for (s, v) in tk:
            if s == eng and (eng == "pe" or not self.self_sync):
                continue
            if self.seen[eng].get(s, 0) < v:
                need[s] = max(need.get(s, 0), v)
        for s, v in need.items():
            self.seen[eng][s] = v
        return list(need.items())

    def replay(self, lst):
        cap, self.capture = self.capture, None
        for a in lst:
            if a[0] == "dma":
                self.dma(*a[1], **a[2])
            else:
                self.op(*a[:5])
        self.capture = cap

    def op(self, eng, fn, reads=(), writes=(), inc=True, cost=None):
        if self.off:
            return None
        if self.capture is not None:
            self.capture.append((eng, fn, list(reads), list(writes), inc, cost))
            return None
        xr = [b for b in reads if b.excl]
        waits = self._tickets(eng, reads, list(writes), xr)
        ticket = (eng, self.cnt.get(eng, 0) + 1)
        if inc:
            self.cnt[eng] = ticket[1]
        self.ops[eng].append((waits, fn, (eng, 1) if inc else None))
        for b in reads:
            b.r.append(ticket)
        for b in xr:
            b.w = ticket
            b.r = []
            b.wread = True
        for b in writes:
            b.w = ticket
            b.r = []
            b.wread = False
        return ticket

    def dma(self, q, semkey, out_ap, in_ap, reads=(), writes=(), **kw):
        if self.off:
            return None
        if self.capture is not None:
            self.capture.append(("dma", (q, semkey, out_ap, in_ap, list(reads), list(writes)), kw))
            return None
        waits = self._tickets(q, reads, writes)
        self.cnt[semkey] = self.cnt.get(semkey, 0) + 16
        ticket = (semkey, self.cnt[semkey])
        self.ops[q].append((waits, lambda e: e.dma_start(out=out_ap, in_=in_ap, **kw), (semkey, 16)))
        for b in reads:
            b.r.append(ticket)
        for b in writes:
            b.w = ticket
            b.r = []
        return ticket

    COST = {"pe": 0.2, "act": 0.6, "dve": 0.55, "pool": 0.9, "sp": 0.1}

    def schedule(self, L):
        units = []
        curu = None
        for a in L:
            if a[0] == "dma":
                q, semkey, out_ap, in_ap, reads, writes = a[1]
                nb = 4
                for d in out_ap.shape:
                    nb *= int(d)
                units.append({"eng": q, "ops": [a], "reads": list(reads), "writes": list(writes), "cost": 0.1, "lat": 2.5 if nb < 600000 else 2.0 + nb / 180e3})
                continue
            eng, fn, reads, writes, inc = a[:5]
            cost = a[5] if len(a) > 5 and a[5] is not None else None
            if cost is None:
                cost = _est_cost(eng, fn)
            if cost is None:
                cost = self.COST[eng]
            if eng == "pe":
                if curu is None:
                    curu = {"eng": "pe", "ops": [], "reads": [], "writes": [], "cost": 0.0, "lat": 0.3}
                curu["ops"].append(a)
                curu["reads"] += list(reads)
                curu["writes"] += list(writes)
                curu["cost"] += cost
                if inc:
                    units.append(curu)
                    curu = None
            else:
                units.append({"eng": eng, "ops": [a], "reads": list(reads), "writes": list(writes), "cost": cost, "lat": 0.15})
        assert curu is None
        n = len(units)
        lastw, readers = {}, {}
        deps = [set() for _ in range(n)]
        for j, u in enumerate(units):
            rs = [b for b in u["reads"] if not b.excl]
            ws = list(u["writes"]) + [b for b in u["reads"] if b.excl]
            for b in rs:
                if id(b) in lastw:
                    deps[j].add(lastw[id(b)])
            for b in ws:
                if id(b) in lastw:
                    deps[j].add(lastw[id(b)])
                for r in readers.get(id(b), ()):
                    deps[j].add(r)
            for b in rs:
                readers.setdefault(id(b), []).append(j)
            for b in ws:
                lastw[id(b)] = j
                readers[id(b)] = []
            deps[j].discard(j)
        succ = [[] for _ in range(n)]
        ndep = [len(d) for d in deps]
        for j, d in enumerate(deps):
            for i in d:
                succ[i].append(j)
        ready = {e: [] for e in self.ENG}
        depready = [0.0] * n
        done = [0.0] * n
        efree = {e: 0.0 for e in self.ENG}
        for j in range(n):
            if ndep[j] == 0:
                ready[units[j]["eng"]].append(j)
        order = []
        WIN = 1000
        nsched = 0
        scheduled = [False] * n
        oldest = 0
        while nsched < n:
            while oldest < n and scheduled[oldest]:
                oldest += 1
            best = None
            for e in self.ENG:
                cand = None
                for j in ready[e]:
                    if j > oldest + WIN:
                        continue
                    st = max(depready[j], efree[e])
                    key = (st, j)
                    if cand is None or key < cand[0]:
                        cand = (key, j)
                if cand is not None and (best is None or cand[0] < best[0]):
                    best = cand
            if best is None:
                j = min(j for e in self.ENG for j in ready[e])
                best = ((max(depready[j], efree[units[j]["eng"]]), j), j)
            (st, _), j = best
            u = units[j]
            ready[u["eng"]].remove(j)
            scheduled[j] = True
            nsched += 1
            efree[u["eng"]] = st + u["cost"]
            done[j] = st + u["cost"] + u["lat"]
            order.append(j)
            for k in succ[j]:
                ndep[k] -= 1
                depready[k] = max(depready[k], done[j])
                if ndep[k] == 0:
                    ready[units[k]["eng"]].append(k)
        cap, self.capture = self.capture, None
        for j in order:
            for a in units[j]["ops"]:
                if a[0] == "dma":
                    self.dma(*a[1], **a[2])
                else:
                    self.op(*a[:5])
        self.capture = cap
        return max(done) if done else 0.0

    def final_wait(self, eng, keys):
        waits = [(k, self.cnt[k]) for k in keys if self.cnt.get(k, 0) > 0]
        self.ops[eng].append((waits, None, None))

    def sem_keys(self):
        keys = set(self.cnt.keys())
        for e in self.ENG:
            keys.add(e)
        return sorted(keys)


def build(NT=32, NP=32, self_sync=True, stop=99):
    nc = bass.Bass("TRN2", target_bir_lowering=False)
    S = Sched(self_sync=self_sync)

    def din(name, shape, dt=F32):
        return nc.dram_tensor(name, list(shape), dt, kind="ExternalInput").ap()

    xm = din("xm", [NT * 128, D])
    xp = din("xp", [max(NP, 1) * 128, D])
    posm = din("posm", [128, NT + 1], I32)
    c_col = din("c_col", [128, 8])
    w_ada = din("w_ada", [D, 3 * D])
    b_ada = din("b_ada", [1, 3 * D])
    gnorm_col = din("gnorm_col", [128, 8])
    w_in = din("w_in", [D, DIN])
    wdec = din("wdec", [17, 256])
    gmix_col = din("gmix_col", [128, 8])
    sinks = din("sinks", [1, 8])
    w_out = din("w_out", [D, D])
    g_final = din("g_final", [1, D])
    flag_col = din("flag_col", [128, 1])
    invf = din("invf", [128, 32])
    c_ident = din("c_ident", [128, 128])
    c_tricum = din("c_tricum", [128, 128])
    c_trisuf = din("c_trisuf", [128, 128])
    c_mgla = din("c_mgla", [128, 128])
    c_mbcur = din("c_mbcur", [128, 512])
    c_mbprev = din("c_mbprev", [128, 512])
    c_mbprev0 = din("c_mbprev0", [128, 512])
    out = nc.dram_tensor("out", [NT * 128, D], F32, kind="ExternalOutput").ap()

    with ExitStack() as es:
        def sb(name, shape, dt=F32):
            return Buf(name, es.enter_context(nc.sbuf_tensor(name, list(shape), dt)))

        def ps(name, shape, dt=F32):
            return Buf(name, es.enter_context(nc.psum_tensor(name, list(shape), dt)), excl=True)

        bk = [ps(f"bk{i}", [128, 512]) for i in range(8) if i != 2]
        b0, b1, b3, b4, b5, b6, b7 = bk
        ptb = ps("ptb", [128, 8, 128], BF16)

        NSTG = 3
        stage = [sb(f"stage{i}", [128, 2 * D]) for i in range(NSTG)]
        win = sb("win", [128, KC, DIN], BF16)
        wout = sb("wout", [128, KC, D], BF16)
        ones_f = sb("ones_f", [1, 128])
        negcol = sb("negcol", [128, 1])
        ident_f = sb("ident_f", [128, 128])
        ident_bf = sb("ident_bf", [128, 128], BF16)
        tricum = sb("tricum", [128, 128])
        trisuf = sb("trisuf", [128, 128])
        mgla = sb("mgla", [128, 128])
        mb = sb("mb", [128, 3, 512], BF16)
        posi = sb("posi", [128, NT + 1], I32)
        posf = sb("posf", [128, NT + 1])
        invf_sb = sb("invf_sb", [128, 32])
        ang = sb("ang", [128, NT + 1, 32])
        angm = sb("angm", [128, NT + 1, 32])
        cosT = sb("cosT", [128, NT + 1, 32])
        sinS = sb("sinS", [128, NT + 1, 2, 32])
        sink_bc = sb("sink_bc", [128, 8])
        esink = sb("esink", [128, 8])
        gfin_bc = sb("gfin_bc", [128, D])
        wdec_sb = sb("wdec_sb", [17, 256])
        flag_sb = sb("flag_sb", [128, 1])
        ccol = sb("ccol", [128, 8])
        ecol = sb("ecol", [128, 8])
        siluc = sb("siluc", [128, 8])
        gncol = sb("gncol", [128, 8])
        gmcol = sb("gmcol", [128, 8])
        gscol = sb("gscol", [128, 8])
        shcol = sb("shcol", [128, 8])
        modrow = sb("modrow", [1, 3 * D])
        St = sb("St", [128, 2, 128])
        St_bf = sb("St_bf", [128, 2, 128], BF16)
        decay = sb("decay", [128, 2])
        gaT = sb("gaT", [17, 128], BF16)
        wdec_bf = sb("wdec_bf", [17, 256], BF16)

        xin = [sb(f"xin{i}", [128, D]) for i in range(3)]
        ssq = sb("ssq", [128, 1])
        rstd = sb("rstd", [128, 1])
        hb = sb("hb", [128, D], BF16)
        hTk = [sb(f"hT{k}", [128, 128], BF16) for k in range(KC)]
        QK = [sb(f"qk_sb{i}", [128, 512], BF16) for i in range(2)]
        VV = [sb(f"v_sb{i}", [128, 512], BF16) for i in range(2)]
        GA = [sb(f"ga_sb{i}", [128, 16], BF16) for i in range(2)]
        GG = [sb(f"gate_g{i}", [128, 512], BF16) for i in range(2)]
        GS = [sb(f"gate_s{i}", [128, 512], BF16) for i in range(2)]
        e_t = sb("e_t", [128, 512])
        lnvA = sb("lnvA", [128, 1])
        lnvB = sb("lnvB", [128, 4])
        lnvC = sb("lnvC", [128, 1])
        tmpc = sb("tmpc", [128, 512])
        tmps = sb("tmps", [128, 512])
        QR = [sb(f"q_r{i}", [128, 512], BF16) for i in range(2)]
        tmpck = sb("tmpck", [128, 128])
        tmpsk = sb("tmpsk", [128, 128])
        KR = [sb(f"k_r{i}", [128, 128], BF16) for i in range(2)]
        kT = [sb(f"kT{i}", [128, 2, 128], BF16) for i in range(2)]
        vaug = [sb(f"vaug{i}", [128, 2, 65], BF16) for i in range(3)]
        qT = sb("qT", [128, 512], BF16)
        e_z = sb("e_z", [128, 256])
        Lz = sb("Lz", [128, 256])
        Eplus = sb("Eplus", [128, 2, 128])
        Eminus = sb("Eminus", [128, 2, 128])
        Esuf = sb("Esuf", [128, 256])
        qkT_sb = sb("qkT_sb", [128, 4, 128], BF16)
        nln8 = sb("nln8", [128, 1])
        q_dT = sb("q_dT", [128, 4, 128], BF16)
        k_dT = sb("k_dT", [128, 2, 128], BF16)
        k_tail = sb("k_tail", [128, 256], BF16)
        AT_sb = sb("AT_sb", [128, 512], BF16)
        junk2 = sb("junk2", [128, 128], BF16)
        ssq_g = sb("ssq_g", [128, 4])
        rstd_g = sb("rstd_g", [128, 4])
        tmp_g = sb("tmp_g", [128, 512])
        mix = sb("mix", [128, D], BF16)
        mixT = sb("mixT", [128, KC, 128], BF16)
        PT = [sb(f"PT{g}", [128, 2, 512], BF16) for g in range(2)]
        den = sb("den", [128, 8])
        rden = sb("rden", [128, 8])
        tmp_s = tmp_g
        xnew = sb("xnew", [128, D])
        gate_bc = xnew
        ssq2 = sb("ssq2", [128, 1])
        rstd2 = sb("rstd2", [128, 1])
        yout = [sb(f"yout{i}", [128, D]) for i in range(2)]

        def ld(dst, src, key="ld0", q="sp", **kw):
            S.dma(q, key, dst.t[:] if not isinstance(dst, tuple) else dst[1], src, writes=[dst if not isinstance(dst, tuple) else dst[0]], **kw)

        small = [
            (ccol, c_col), (gncol, gnorm_col), (gmcol, gmix_col), (flag_sb, flag_col), (invf_sb, invf),
            (ident_f, c_ident), (tricum, c_tricum), (trisuf, c_trisuf), (mgla, c_mgla), (posi, posm),
            (wdec_sb, wdec), (modrow, b_ada),
        ]
        for dst, src in small:
            S.dma("sp", "ld0", dst.t[:], src, writes=[dst])
        mbstage = stage[0]
        mbv = stage[0].t[:, 0:1536].rearrange("p (a n) -> p a n", a=3)
        S.dma("sp", "ld0", mbv[:, 0, :], c_mbcur, writes=[mbstage])
        S.dma("sp", "ld0", mbv[:, 1, :], c_mbprev, writes=[mbstage])
        S.dma("sp", "ld0", mbv[:, 2, :], c_mbprev0, writes=[mbstage])
        S.dma("sp", "ld0", sink_bc.t[:], sinks.partition_broadcast(128), writes=[sink_bc])
        S.dma("sp", "ld0", gfin_bc.t[:], g_final.partition_broadcast(128), writes=[gfin_bc])
        fin = ("ld0", S.cnt["ld0"])
        for b in [d for d, _ in small] + [mbstage, sink_bc, gfin_bc]:
            b.w = fin

        eps_d = sb("eps_d", [128, 1])
        eps_h = sb("eps_h", [128, 1])
        S.op("pool", lambda e: e.memset(nln8.t[:], -math.log(8.0)), writes=[nln8])
        S.op("pool", lambda e: e.memset(eps_d.t[:], D * EPS), writes=[eps_d])
        S.op("pool", lambda e: e.memset(eps_h.t[:], 128.0 * EPS), writes=[eps_h])
        S.op("pool", lambda e: e.memset(ones_f.t[:], 1.0), writes=[ones_f])
        S.op("pool", lambda e: e.memset(negcol.t[:], -1.0 / 16.0), writes=[negcol])
        S.op("pool", lambda e: e.memset(gaT.t[:], 1.0), writes=[gaT])
        S.op("pool", lambda e: e.memset(St.t[:], 0.0), writes=[St])
        S.op("pool", lambda e: e.memset(St_bf.t[:], 0.0), writes=[St_bf])
        S.op("pool", lambda e: e.memset(q_dT.t[:], 0.0), writes=[q_dT])
        for i in range(3):
            S.op("pool", lambda e, i=i: e.memset(vaug[i].t[:], 1.0), writes=[vaug[i]])
        for i in range(2):
            S.op("pool", lambda e, i=i: e.memset(kT[i].t[:], 0.0), writes=[kT[i]])
        S.op("dve", lambda e: e.tensor_copy(out=ident_bf.t[:], in_=ident_f.t[:]), reads=[ident_f], writes=[ident_bf])
        S.op("dve", lambda e: e.tensor_copy(out=wdec_bf.t[:], in_=wdec_sb.t[:]), reads=[wdec_sb], writes=[wdec_bf])
        S.op("dve", lambda e: e.tensor_copy(out=mb.t[:], in_=mbv), reads=[mbstage], writes=[mb])
        S.op("dve", lambda e: e.tensor_single_scalar(out=gfin_bc.t[:], in_=gfin_bc.t[:], scalar=32.0, op=ALU.mult),
             reads=[gfin_bc], writes=[gfin_bc])
        S.op("dve", lambda e: e.tensor_single_scalar(out=gmcol.t[:, 0:4], in_=gmcol.t[:, 0:4], scalar=math.sqrt(128.0), op=ALU.mult),
             reads=[gmcol], writes=[gmcol])

        S.off = stop < 1
        L_setup = []
        S.capture = L_setup
        S.op("dve", lambda e: e.tensor_copy(out=posf.t[:], in_=posi.t[:]), reads=[posi], writes=[posf])
        S.op("dve", lambda e: e.tensor_tensor(
            out=ang.t[:], in0=posf.t[:].unsqueeze(2).broadcast_to([128, NT + 1, 32]),
            in1=invf_sb.t[:].unsqueeze(1).broadcast_to([128, NT + 1, 32]), op=ALU.mult),
            reads=[posf, invf_sb], writes=[ang])
        C1 = 6.28125
        C2 = 2.0 * math.pi - 6.28125
        angi = sb("angi", [128, NT + 1, 32], I32)
        S.op("dve", lambda e: e.tensor_single_scalar(out=angm.t[:], in_=ang.t[:], scalar=1.0 / (2.0 * math.pi), op=ALU.mult),
             reads=[ang], writes=[angm])
        S.op("dve", lambda e: e.tensor_copy(out=angi.t[:], in_=angm.t[:]), reads=[angm], writes=[angi])
        S.op("dve", lambda e: e.tensor_copy(out=angm.t[:], in_=angi.t[:]), reads=[angi], writes=[angm])
        S.op("dve", lambda e: e.scalar_tensor_tensor(out=ang.t[:], in0=angm.t[:], scalar=-C1, in1=ang.t[:], op0=ALU.mult, op1=ALU.add),
             reads=[angm, ang], writes=[ang])
        S.op("dve", lambda e: e.scalar_tensor_tensor(out=ang.t[:], in0=angm.t[:], scalar=-C2, in1=ang.t[:], op0=ALU.mult, op1=ALU.add),
             reads=[angm, ang], writes=[ang])
        SC = [-1.0 / 6, 1.0 / 120, -1.0 / 5040, 1.0 / 362880, -1.0 / 39916800]
        CC = [-1.0 / 2, 1.0 / 24, -1.0 / 720, 1.0 / 40320, -1.0 / 3628800, 1.0 / 479001600]
        s_v = sinS.t[:, :, 1, :]
        t_v = sinS.t[:, :, 0, :]
        S.op("dve", lambda e: e.tensor_single_scalar(out=ang.t[:], in_=ang.t[:], scalar=0.5, op=ALU.mult), reads=[ang], writes=[ang])
        S.op("dve", lambda e: e.tensor_tensor(out=angm.t[:], in0=ang.t[:], in1=ang.t[:], op=ALU.mult), reads=[ang], writes=[angm])

        def horner(dst, coefs, dbuf):
            S.op("dve", lambda e: e.tensor_single_scalar(out=dst, in_=angm.t[:], scalar=coefs[-1], op=ALU.mult), reads=[angm], writes=[dbuf])
            for a in reversed(coefs[:-1]):
                S.op("dve", lambda e, a=a: e.scalar_tensor_tensor(out=dst, in0=dst, scalar=a, in1=angm.t[:], op0=ALU.add, op1=ALU.mult),
                     reads=[dbuf, angm], writes=[dbuf])

        horner(cosT.t[:], SC, cosT)
        S.op("dve", lambda e: e.scalar_tensor_tensor(out=s_v, in0=cosT.t[:], scalar=1.0, in1=ang.t[:], op0=ALU.add, op1=ALU.mult),
             reads=[cosT, ang], writes=[sinS])
        horner(cosT.t[:], CC, cosT)
        S.op("dve", lambda e: e.tensor_single_scalar(out=cosT.t[:], in_=cosT.t[:], scalar=1.0, op=ALU.add), reads=[cosT], writes=[cosT])
        S.op("dve", lambda e: e.scalar_tensor_tensor(out=ang.t[:], in0=s_v, scalar=2.0, in1=cosT.t[:], op0=ALU.mult, op1=ALU.mult),
             reads=[sinS, cosT], writes=[ang])
        S.op("dve", lambda e: e.tensor_tensor(out=angm.t[:], in0=s_v, in1=s_v, op=ALU.mult), reads=[sinS], writes=[angm])
        S.op("dve", lambda e: e.tensor_scalar(out=cosT.t[:], in0=angm.t[:], scalar1=-2.0, scalar2=1.0, op0=ALU.mult, op1=ALU.add),
             reads=[angm], writes=[cosT])
        S.op("dve", lambda e: e.tensor_copy(out=s_v, in_=ang.t[:]), reads=[ang], writes=[sinS])
        S.op("dve", lambda e: e.tensor_single_scalar(out=t_v, in_=ang.t[:], scalar=-1.0, op=ALU.mult), reads=[ang], writes=[sinS])
        S.op("act", lambda e: e.activation(out=esink.t[:], in_=sink_bc.t[:], func=AF.Exp), reads=[sink_bc], writes=[esink])
        S.op("act", lambda e: e.activation(out=ecol.t[:], in_=ccol.t[:], func=AF.Exp, scale=-1.0), reads=[ccol], writes=[ecol])
        S.op("dve", lambda e: e.tensor_single_scalar(out=ecol.t[:], in_=ecol.t[:], scalar=1.0, op=ALU.add), reads=[ecol], writes=[ecol])
        S.op("dve", lambda e: e.reciprocal(out=ecol.t[:], in_=ecol.t[:]), reads=[ecol], writes=[ecol])
        S.op("dve", lambda e: e.tensor_tensor(out=siluc.t[:], in0=ccol.t[:], in1=ecol.t[:], op=ALU.mult),
             reads=[ccol, ecol], writes=[siluc])

        S.off = stop < 2
        WQ = "sp"
        modbanks = [b0, b1, b3, b4, b5, b6]
        stg_i = [0]

        def mod_phase(c0, ngrp, gbase):
            for k in range(KC):
                st = stage[stg_i[0] % NSTG]
                S.dma(WQ, f"stg{stg_i[0] % NSTG}", st.t[:, 0:ngrp * 512], w_ada[k * 128:(k + 1) * 128, c0:c0 + ngrp * 512], writes=[st])
                stg_i[0] += 1
                for g in range(ngrp):
                    S.op("pe", lambda e, k=k, g=g, st=st: e.matmul(
                        modbanks[gbase + g].t[0:1, :], lhsT=siluc.t[:, k:k + 1], rhs=st.t[:, g * 512:(g + 1) * 512],
                        start=(k == 0), stop=(k == KC - 1)), reads=[siluc, st], writes=[modbanks[gbase + g]], inc=(g == ngrp - 1))
            for g in range(ngrp):
                gg = gbase + g
                S.op("dve", lambda e, g=g, gg=gg: e.tensor_tensor(out=modrow.t[0:1, gg * 512:(gg + 1) * 512], in0=modbanks[gg].t[0:1, :],
                                                                  in1=modrow.t[0:1, gg * 512:(gg + 1) * 512], op=ALU.add),
                     reads=[modbanks[gg], modrow], writes=[modrow])

        mod_phase(0, 4, 0)
        for j in range(16):
            src = (0 if j < 8 else D) + (j % 8) * 128
            S.op("pe", lambda e, j=j, src=src: e.matmul(b7.t[:, j:j + 1], lhsT=modrow.t[0:1, src:src + 128], rhs=ones_f.t[0:1, 0:1],
                                                        start=True, stop=True), reads=[modrow, ones_f], writes=[b7], inc=(j == 15))
        S.op("dve", lambda e: e.tensor_copy(out=shcol.t[:], in_=b7.t[:, 0:8]), reads=[b7], writes=[shcol])
        S.op("dve", lambda e: e.scalar_tensor_tensor(out=gscol.t[:], in0=b7.t[:, 8:16], scalar=1.0, in1=gncol.t[:],
                                                     op0=ALU.add, op1=ALU.mult), reads=[b7, gncol], writes=[gscol])
        S.op("dve", lambda e: e.tensor_single_scalar(out=gscol.t[:], in_=gscol.t[:], scalar=32.0, op=ALU.mult),
             reads=[gscol], writes=[gscol])

        S.off = stop < 3
        hlf = DIN // 2
        for k in range(KC):
            for hh in range(2):
                st = stage[stg_i[0] % NSTG]
                S.dma(WQ, f"stg{stg_i[0] % NSTG}", st.t[:, 0:hlf], w_in[k * 128:(k + 1) * 128, hh * hlf:(hh + 1) * hlf], writes=[st])
                stg_i[0] += 1
                if hh == 0:
                    S.op("act", lambda e, k=k, st=st: e.activation(out=win.t[:, k, 0:hlf], in_=st.t[:, 0:hlf], func=AF.Copy),
                         reads=[st], writes=[win])
                else:
                    S.op("dve", lambda e, k=k, st=st: e.tensor_copy(out=win.t[:, k, hlf:DIN], in_=st.t[:, 0:hlf]), reads=[st], writes=[win])
        S.off = stop < 4
        mod_phase(2 * D, 2, 4)
        for g in range(2):
            bb = (b3, b4)[g]
            S.op("pe", lambda e, g=g, bb=bb: e.matmul(bb.t[:, :], lhsT=ones_f.t[0:1, :], rhs=modrow.t[0:1, 2 * D + g * 512:2 * D + (g + 1) * 512],
                                                      start=True, stop=True), reads=[ones_f, modrow], writes=[bb])
            S.op("act", lambda e, g=g, bb=bb: e.activation(out=gate_bc.t[:, g * 512:(g + 1) * 512], in_=bb.t[:, :], func=AF.Copy),
                 reads=[bb], writes=[gate_bc])
        for k in range(KC):
            st = stage[stg_i[0] % NSTG]
            S.dma(WQ, f"stg{stg_i[0] % NSTG}", st.t[:, 0:D], w_out[k * 128:(k + 1) * 128, :], writes=[st])
            stg_i[0] += 1
            eng = "dve"
            S.op(eng, lambda e, k=k, st=st: e.scalar_tensor_tensor(out=wout.t[:, k, :], in0=st.t[:, 0:D], scalar=gmcol.t[:, k:k + 1],
                                                                   in1=gate_bc.t[:], op0=ALU.mult, op1=ALU.mult),
                 reads=[st, gmcol, gate_bc], writes=[wout])

        S.capture = None
        H = lambda i: i % 2
        HX = lambda i: i % 3
        ACT_EVAC = 0

        def front(x_ap, slot, xbuf):
            S.dma("sp", f"x{slot}", xbuf.t[:], x_ap, writes=[xbuf])
            S.op("act", lambda e: e.activation(out=hb.t[:], in_=xbuf.t[:], func=AF.Square, accum_out=ssq.t[:, 0:1]),
                 reads=[xbuf], writes=[hb, ssq])
            S.op("act", lambda e: e.activation(out=lnvA.t[:, 0:1], in_=ssq.t[:], func=AF.Ln, bias=eps_d.t[:, 0:1]), reads=[ssq, eps_d], writes=[lnvA])
            S.op("act", lambda e: e.activation(out=rstd.t[:], in_=lnvA.t[:, 0:1], func=AF.Exp, scale=-0.5), reads=[lnvA], writes=[rstd])
            S.op("dve", lambda e: e.tensor_scalar(out=hb.t[:], in0=xbuf.t[:], scalar1=rstd.t[:, 0:1], scalar2=None, op0=ALU.mult),
                 reads=[xbuf, rstd], writes=[hb])
            for k in range(KC):
                S.op("pe", lambda e, k=k: e.transpose(ptb.t[:, k, :], hb.t[:, k * 128:(k + 1) * 128], ident_bf.t[:]),
                     reads=[hb, ident_bf], writes=[ptb], inc=(k == KC - 1))
            for k in range(KC):
                if k < ACT_EVAC:
                    S.op("act", lambda e, k=k: e.activation(out=hTk[k].t[:], in_=ptb.t[:, k, :], func=AF.Identity,
                                                            scale=gscol.t[:, k:k + 1], bias=shcol.t[:, k:k + 1]),
                         reads=[ptb, gscol, shcol], writes=[hTk[k]])
                else:
                    S.op("dve", lambda e, k=k: e.tensor_scalar(out=hTk[k].t[:], in0=ptb.t[:, k, :], scalar1=gscol.t[:, k:k + 1],
                                                               scalar2=shcol.t[:, k:k + 1], op0=ALU.mult, op1=ALU.add),
                         reads=[ptb, gscol, shcol], writes=[hTk[k]])

        def proj(bank, o, w):
            for k in range(KC):
                S.op("pe", lambda e, k=k: e.matmul(bank.t[:, 0:w], lhsT=hTk[k].t[:], rhs=win.t[:, k, o:o + w], start=(k == 0), stop=(k == KC - 1)),
                     reads=[hTk[k], win], writes=[bank], inc=(k % 3 == 2 or k == KC - 1))

        def rope_k(src_bank, col0, tcol, krb, vab):
            skv = src_bank.t[:, col0:col0 + 128].rearrange("p (g a f) -> p g a f", g=2, a=2)
            cosb = cosT.t[:, tcol, :].unsqueeze(1).unsqueeze(1).broadcast_to([128, 2, 2, 32])
            S.op("dve", lambda e: e.tensor_tensor(out=tmpck.t[:].rearrange("p (g a f) -> p g a f", g=2, a=2), in0=skv, in1=cosb, op=ALU.mult),
                 reads=[src_bank, cosT], writes=[tmpck])
            for a in range(2):
                S.op("dve", lambda e, a=a: e.tensor_tensor(
                    out=tmpsk.t[:].rearrange("p (g a f) -> p g a f", g=2, a=2)[:, :, a, :], in0=skv[:, :, 1 - a, :],
                    in1=sinS.t[:, tcol, a, :].unsqueeze(1).broadcast_to([128, 2, 32]), op=ALU.mult),
                    reads=[src_bank, sinS], writes=[tmpsk])
            S.op("pool", lambda e: e.tensor_tensor(out=krb.t[:], in0=tmpck.t[:], in1=tmpsk.t[:], op=ALU.add),
                 reads=[tmpck, tmpsk], writes=[krb])
            S.op("dve", lambda e: e.tensor_copy(out=vab.t[:, :, 0:64],
                                                in_=src_bank.t[:, col0 + 128:col0 + 256].rearrange("p (g f) -> p g f", g=2)),
                 reads=[src_bank], writes=[vab])

        def gate(bank, gbuf):
            S.op("act", lambda e: e.activation(out=e_t.t[:], in_=bank.t[:, :], func=AF.Exp, scale=-1.0), reads=[bank], writes=[e_t])
            S.op("act", lambda e: e.activation(out=e_t.t[:], in_=e_t.t[:], func=AF.Ln, bias=1.0), reads=[e_t], writes=[e_t])
            S.op("act", lambda e: e.activation(out=e_t.t[:], in_=e_t.t[:], func=AF.Exp, scale=-1.0), reads=[e_t], writes=[e_t])
            S.op("dve", lambda e: e.tensor_tensor(out=gbuf.t[:], in0=bank.t[:, :], in1=e_t.t[:], op=ALU.mult),
                 reads=[bank, e_t], writes=[gbuf])

        def gla_decay_common(gab):
            gT_ps = b4.t[:, 256:384].bitcast(BF16)[0:16, 0:128]
            S.op("pe", lambda e: e.transpose(gT_ps, gab.t[:, 0:16], ident_bf.t[:]), reads=[gab, ident_bf], writes=[b4])
            S.op("dve", lambda e: e.tensor_copy(out=gaT.t[0:16, :], in_=gT_ps), reads=[b4], writes=[gaT])
            S.op("pe", lambda e: e.matmul(b3.t[:, 0:256], lhsT=gaT.t[0:17, :], rhs=wdec_bf.t[0:17, :], start=True, stop=True),
                 reads=[gaT, wdec_bf], writes=[b3])
            S.op("act", lambda e: e.activation(out=e_z.t[:], in_=b3.t[:, 0:256], func=AF.Exp, scale=-1.0), reads=[b3], writes=[e_z])
            S.op("act", lambda e: e.activation(out=Lz.t[:], in_=e_z.t[:], func=AF.Ln, bias=1.0), reads=[e_z], writes=[Lz])

        def state_update(ub):
            for p in range(2):
                for hp in range(2):
                    r0 = 64 * hp
                    h = 2 * p + hp
                    S.op("dve", lambda e, p=p, r0=r0, h=h: e.scalar_tensor_tensor(
                        out=St.t[r0:r0 + 64, p, :], in0=St.t[r0:r0 + 64, p, :], scalar=decay.t[r0:r0 + 64, p:p + 1],
                        in1=ub.t[r0:r0 + 64, h * 128:(h + 1) * 128], op0=ALU.mult, op1=ALU.add),
                        reads=[St, decay, ub], writes=[St])

        def u_matmuls(ub, vb):
            for h in range(4):
                p = h // 2
                S.op("pe", lambda e, h=h, p=p: e.matmul(ub.t[:, h * 128:(h + 1) * 128], lhsT=k_tail.t[:, p * 128:(p + 1) * 128],
                                                        rhs=vb.t[:, h * 128:(h + 1) * 128], start=True, stop=True),
                     reads=[k_tail, vb], writes=[ub], inc=(h == 3))

        def sections(names):
            return {k: [] for k in names}

        def A_pre(t, i):
            sec = sections(("a0", "a1", "a2", "a3"))
            xbuf = xin[HX(i)]
            qk, vb, gab = QK[H(i)], VV[H(i)], GA[H(i)]
            S.capture = sec["a0"]
            front(xp[t * 128:(t + 1) * 128, :], HX(i), xbuf)
            S.capture = sec["a1"]
            proj(b0, O_GK, 512)
            S.op("act", lambda e: e.activation(out=qk.t[:, 256:512], in_=b0.t[:, 0:256], func=AF.Copy), reads=[b0], writes=[qk])
            S.op("act", lambda e: e.activation(out=vb.t[:, 0:256], in_=b0.t[:, 256:512], func=AF.Copy), reads=[b0], writes=[vb])
            S.capture = sec["a2"]
            proj(b1, O_GV + 256, 272)
            S.op("act", lambda e: e.activation(out=vb.t[:, 256:512], in_=b1.t[:, 0:256], func=AF.Copy), reads=[b1], writes=[vb])
            S.op("act", lambda e: e.activation(out=gab.t[:], in_=b1.t[:, 256:272], func=AF.Copy), reads=[b1], writes=[gab])
            if t == NP - 1:
                proj(b0, O_SK, 256)
                rope_k(b0, 0, NT, KR[H(i)], vaug[2])
            S.capture = None
            return sec

        def B_pre(t, i):
            sec = sections(("b0", "b1"))
            qk, vb, gab = QK[H(i)], VV[H(i)], GA[H(i)]
            S.capture = sec["b0"]
            if t == NP - 1:
                krb = KR[H(i)]
                S.op("pe", lambda e: e.transpose(ptb.t[:, 0, :], krb.t[:], ident_bf.t[:]), reads=[krb, ident_bf], writes=[ptb])
                for g in range(2):
                    S.op("act", lambda e, g=g: e.activation(out=kT[1].t[64 * g:64 * g + 64, g, :], in_=ptb.t[64 * g:64 * g + 64, 0, :], func=AF.Copy),
                         reads=[ptb], writes=[kT[1]])
            gla_decay_common(gab)
            S.op("pe", lambda e: e.matmul(b3.t[:, 256:512], lhsT=trisuf.t[:], rhs=Lz.t[:], start=True, stop=True),
                 reads=[trisuf, Lz], writes=[b3])
            for p in range(2):
                S.op("pe", lambda e, p=p: e.matmul(b4.t[:, p:p + 1], lhsT=Lz.t[:, p * 128:(p + 1) * 128], rhs=negcol.t[:, 0:1], start=True, stop=True),
                     reads=[Lz, negcol], writes=[b4], inc=(p == 1))
            S.op("act", lambda e: e.activation(out=Esuf.t[:], in_=b3.t[:, 256:512], func=AF.Exp), reads=[b3], writes=[Esuf])
            S.op("act", lambda e: e.activation(out=decay.t[:], in_=b4.t[:, 0:2], func=AF.Exp), reads=[b4], writes=[decay])
            S.op("pool", lambda e: e.tensor_tensor(out=k_tail.t[:], in0=qk.t[:, 256:512], in1=Esuf.t[:], op=ALU.mult),
                 reads=[qk, Esuf], writes=[k_tail])
            S.capture = sec["b1"]
            u_matmuls(b5, vb)
            state_update(b5)
            if t == NP - 1:
                S.op("dve", lambda e: e.tensor_scalar(out=St.t[:], in0=St.t[:], scalar1=flag_sb.t[:, 0:1], scalar2=None, op0=ALU.mult),
                     reads=[St, flag_sb], writes=[St])
                S.op("act", lambda e: e.activation(out=St_bf.t[:], in_=St.t[:], func=AF.Copy), reads=[St], writes=[St_bf])
            S.capture = None
            return sec

        def A_main(t, i):
            sec = sections(("a0", "a1", "a2", "a3"))
            xbuf = xin[HX(i)]
            qk, vb, gab, krb, qrb, ggb, gsb = QK[H(i)], VV[H(i)], GA[H(i)], KR[H(i)], QR[H(i)], GG[H(i)], GS[H(i)]
            S.capture = sec["a0"]
            front(xm[t * 128:(t + 1) * 128, :], HX(i), xbuf)
            S.capture = sec["a1"]
            proj(b0, O_GQ, 512)
            S.op("act", lambda e: e.activation(out=qk.t[:], in_=b0.t[:, :], func=AF.Copy), reads=[b0], writes=[qk])
            proj(b1, O_GV, 512)
            S.op("dve", lambda e: e.tensor_copy(out=vb.t[:], in_=b1.t[:, :]), reads=[b1], writes=[vb])
            S.capture = sec["a2"]
            proj(b0, O_GA, 272)
            S.op("act", lambda e: e.activation(out=gab.t[:], in_=b0.t[:, 0:16], func=AF.Copy), reads=[b0], writes=[gab])
            rope_k(b0, 16, t, krb, vaug[t % 3])
            proj(b1, O_GZ, 512)
            gate(b1, ggb)
            S.capture = sec["a3"]
            proj(b0, O_SQ, 512)
            sqv = b0.t[:, :].rearrange("p (h a f) -> p h a f", h=8, a=2)
            S.op("dve", lambda e: e.tensor_tensor(
                out=tmpc.t[:].rearrange("p (h a f) -> p h a f", h=8, a=2), in0=sqv,
                in1=cosT.t[:, t, :].unsqueeze(1).unsqueeze(1).broadcast_to([128, 8, 2, 32]), op=ALU.mult),
                reads=[b0, cosT], writes=[tmpc])
            for a in range(2):
                S.op("dve", lambda e, a=a: e.tensor_tensor(
                    out=tmps.t[:].rearrange("p (h a f) -> p h a f", h=8, a=2)[:, :, a, :], in0=sqv[:, :, 1 - a, :],
                    in1=sinS.t[:, t, a, :].unsqueeze(1).broadcast_to([128, 8, 32]), op=ALU.mult),
                    reads=[b0, sinS], writes=[tmps])
            S.op("pool", lambda e: e.tensor_tensor(out=qrb.t[:].rearrange("p (j g f) -> p g j f", j=4, g=2),
                                                   in0=tmpc.t[:].rearrange("p (g j f) -> p g j f", g=2, j=4),
                                                   in1=tmps.t[:].rearrange("p (g j f) -> p g j f", g=2, j=4), op=ALU.add),
                 reads=[tmpc, tmps], writes=[qrb])
            proj(b1, O_SZ, 512)
            gate(b1, gsb)
            S.capture = None
            return sec

        def B_main(t, i):
            sec = sections(("gla1", "swa1", "gla2", "swa2", "out"))
            xbuf = xin[HX(i)]
            qk, vb, gab, krb, qrb, ggb, gsb = QK[H(i)], VV[H(i)], GA[H(i)], KR[H(i)], QR[H(i)], GG[H(i)], GS[H(i)]
            cur, prev = t % 2, (t + 1) % 2
            vcur, vprev = vaug[t % 3], vaug[(t - 1) % 3]
            S.capture = sec["gla1"]
            gla_decay_common(gab)
            for p in range(2):
                S.op("pe", lambda e, p=p: e.matmul(b4.t[:, p * 128:(p + 1) * 128], lhsT=Lz.t[:, p * 128:(p + 1) * 128], rhs=tricum.t[:],
                                                   start=True, stop=True), reads=[Lz, tricum], writes=[b4], inc=(p == 1))
            S.op("pe", lambda e: e.matmul(b3.t[:, 256:512], lhsT=trisuf.t[:], rhs=Lz.t[:], start=True, stop=True),
                 reads=[trisuf, Lz], writes=[b3])
            b4v = b4.t[:, 0:256].rearrange("p (a t) -> p a t", a=2)
            S.op("act", lambda e: e.activation(out=Eplus.t[:], in_=b4v, func=AF.Exp, bias=nln8.t[:, 0:1]), reads=[b4, nln8], writes=[Eplus])
            S.op("act", lambda e: e.activation(out=Eminus.t[:], in_=b4v, func=AF.Exp, scale=-1.0), reads=[b4], writes=[Eminus])
            S.op("act", lambda e: e.activation(out=Esuf.t[:], in_=b3.t[:, 256:512], func=AF.Exp), reads=[b3], writes=[Esuf])
            S.op("dve", lambda e: e.reciprocal(out=decay.t[:], in_=Eminus.t[:, :, 127]), reads=[Eminus], writes=[decay])
            S.capture = sec["gla2"]
            t5 = b5.t[:, 0:256].bitcast(BF16).rearrange("p (j t) -> p j t", j=4)
            for j in range(4):
                S.op("pe", lambda e, j=j: e.transpose(t5[:, j, :], qk.t[:, j * 128:(j + 1) * 128], ident_bf.t[:]),
                     reads=[qk, ident_bf], writes=[b5], inc=(j == 3))
            S.op("act", lambda e: e.activation(out=qkT_sb.t[:], in_=t5, func=AF.Copy), reads=[b5], writes=[qkT_sb])
            for hp in range(2):
                r0 = 64 * hp
                S.op("dve", lambda e, hp=hp, r0=r0: e.tensor_tensor(
                    out=q_dT.t[r0:r0 + 64, :, :].rearrange("r (p two) t -> r p two t", two=2)[:, :, hp, :],
                    in0=qkT_sb.t[r0:r0 + 64, 0:2, :], in1=Eplus.t[r0:r0 + 64, :, :], op=ALU.mult),
                    reads=[qkT_sb, Eplus], writes=[q_dT])
            S.op("pool", lambda e: e.tensor_tensor(out=k_dT.t[:], in0=qkT_sb.t[:, 2:4, :], in1=Eminus.t[:], op=ALU.mult),
                 reads=[qkT_sb, Eminus], writes=[k_dT])
            S.op("pool", lambda e: e.tensor_tensor(out=k_tail.t[:], in0=qk.t[:, 256:512], in1=Esuf.t[:], op=ALU.mult),
                 reads=[qk, Esuf], writes=[k_tail])
            for h in range(4):
                p = h // 2
                S.op("pe", lambda e, h=h, p=p: e.matmul(b5.t[:, h * 128:(h + 1) * 128], lhsT=k_dT.t[:, p, :],
                                                        rhs=q_dT.t[:, h, :], start=True, stop=True),
                     reads=[k_dT, q_dT], writes=[b5], inc=(h == 3))
            S.op("dve", lambda e: e.tensor_tensor(out=AT_sb.t[:].rearrange("p (h i) -> p h i", h=4),
                                                  in0=b5.t[:, :].rearrange("p (h i) -> p h i", h=4),
                                                  in1=mgla.t[:].unsqueeze(1).broadcast_to([128, 4, 128]), op=ALU.mult),
                 reads=[b5, mgla], writes=[AT_sb])
            for h in range(4):
                p = h // 2
                S.op("pe", lambda e, h=h: e.matmul(b5.t[:, h * 128:(h + 1) * 128], lhsT=AT_sb.t[:, h * 128:(h + 1) * 128],
                                                   rhs=vb.t[:, h * 128:(h + 1) * 128], start=True, stop=False),
                     reads=[AT_sb, vb], writes=[b5], inc=False)
                S.op("pe", lambda e, h=h, p=p: e.matmul(b5.t[:, h * 128:(h + 1) * 128], lhsT=q_dT.t[:, h, :],
                                                        rhs=St_bf.t[:, p, :], start=False, stop=True),
                     reads=[q_dT, St_bf], writes=[b5], inc=(h == 3))
            for h in range(4):
                S.op("act", lambda e, h=h: e.activation(out=junk2.t[:], in_=b5.t[:, h * 128:(h + 1) * 128], func=AF.Square,
                                                        accum_out=ssq_g.t[:, h:h + 1]), reads=[b5], writes=[junk2, ssq_g])
            S.op("act", lambda e: e.activation(out=lnvB.t[:, 0:4], in_=ssq_g.t[:], func=AF.Ln, bias=eps_h.t[:, 0:1]), reads=[ssq_g, eps_h], writes=[lnvB])
            S.op("act", lambda e: e.activation(out=rstd_g.t[:], in_=lnvB.t[:, 0:4], func=AF.Exp, scale=-0.5), reads=[lnvB], writes=[rstd_g])
            S.op("dve", lambda e: e.tensor_tensor(out=tmp_g.t[:].rearrange("p (h v) -> p h v", h=4),
                                                  in0=b5.t[:, :].rearrange("p (h v) -> p h v", h=4),
                                                  in1=rstd_g.t[:].unsqueeze(2).broadcast_to([128, 4, 128]), op=ALU.mult),
                 reads=[b5, rstd_g], writes=[tmp_g])
            S.op("pool", lambda e: e.tensor_tensor(out=mix.t[:, 0:512], in0=tmp_g.t[:], in1=ggb.t[:], op=ALU.mult),
                 reads=[tmp_g, ggb], writes=[mix])
            u_matmuls(b5, vb)
            state_update(b5)
            S.op("act", lambda e: e.activation(out=St_bf.t[:], in_=St.t[:], func=AF.Copy), reads=[St], writes=[St_bf])
            S.capture = sec["swa1"]
            t7 = b7.t[:, :].bitcast(BF16).rearrange("p (j t) -> p j t", j=8)
            for j in range(4):
                S.op("pe", lambda e, j=j: e.transpose(t7[:, j, :], qrb.t[:, j * 128:(j + 1) * 128], ident_bf.t[:]),
                     reads=[qrb, ident_bf], writes=[b7], inc=False)
            S.op("pe", lambda e: e.transpose(t7[:, 4, :], krb.t[:], ident_bf.t[:]), reads=[krb, ident_bf], writes=[b7])
            S.op("act", lambda e: e.activation(out=qT.t[:].rearrange("p (j q) -> p j q", j=4), in_=t7[:, 0:4, :], func=AF.Copy), reads=[b7], writes=[qT])
            for g in range(2):
                S.op("dve", lambda e, g=g: e.tensor_copy(out=kT[cur].t[64 * g:64 * g + 64, g, :], in_=t7[64 * g:64 * g + 64, 4, :]),
                     reads=[b7], writes=[kT[cur]])
            mbp = 2 if t == 0 else 1
            for g in range(2):
                for which, bank, kbuf, mi in ((0, b6, kT[prev], mbp), (1, b7, kT[cur], 0)):
                    S.op("pe", lambda e, bank=bank, mi=mi: e.matmul(bank.t[:, :], lhsT=ident_bf.t[:], rhs=mb.t[:, mi, :], start=True, stop=False),
                         reads=[ident_bf, mb], writes=[bank], inc=False)
                    S.op("pe", lambda e, bank=bank, kbuf=kbuf, g=g: e.matmul(
                        bank.t[:, :], lhsT=kbuf.t[:, g, :], rhs=qT.t[:, :],
                        start=False, stop=True), reads=[kbuf, qT], writes=[bank])
                    S.op("act", lambda e, bank=bank, g=g, which=which: e.activation(out=PT[g].t[:, which, :], in_=bank.t[:, :], func=AF.Exp, scale=0.125),
                         reads=[bank], writes=[PT[g]])
            S.capture = sec["swa2"]
            for g in range(2):
                ob = (b3, b4)[g]
                for j in range(4):
                    S.op("pe", lambda e, g=g, j=j, ob=ob: e.matmul(ob.t[:, j * 65:(j + 1) * 65], lhsT=PT[g].t[:, 0, j * 128:(j + 1) * 128],
                                                                   rhs=vprev.t[:, g, :], start=True, stop=False),
                         reads=[PT[g], vprev], writes=[ob], inc=False)
                    S.op("pe", lambda e, g=g, j=j, ob=ob: e.matmul(ob.t[:, j * 65:(j + 1) * 65], lhsT=PT[g].t[:, 1, j * 128:(j + 1) * 128],
                                                                   rhs=vcur.t[:, g, :], start=False, stop=True),
                         reads=[PT[g], vcur], writes=[ob], inc=(j == 3))
            for g in range(2):
                ob = (b3, b4)[g]
                obv = ob.t[:, 0:260].rearrange("p (j f) -> p j f", j=4)
                S.op("dve", lambda e, g=g, obv=obv: e.tensor_tensor(out=den.t[:, 4 * g:4 * g + 4], in0=obv[:, :, 64], in1=esink.t[:, 4 * g:4 * g + 4], op=ALU.add),
                     reads=[ob, esink], writes=[den])
            S.op("dve", lambda e: e.reciprocal(out=rden.t[:], in_=den.t[:]), reads=[den], writes=[rden])
            for g in range(2):
                ob = (b3, b4)[g]
                obv = ob.t[:, 0:260].rearrange("p (j f) -> p j f", j=4)
                S.op("dve", lambda e, g=g, obv=obv: e.tensor_tensor(
                    out=tmp_s.t[:, 256 * g:256 * (g + 1)].rearrange("p (j f) -> p j f", j=4), in0=obv[:, :, 0:64],
                    in1=rden.t[:, 4 * g:4 * g + 4].unsqueeze(2).broadcast_to([128, 4, 64]), op=ALU.mult),
                    reads=[ob, rden], writes=[tmp_s])
            S.op("pool", lambda e: e.tensor_tensor(out=mix.t[:, 512:1024], in0=tmp_s.t[:], in1=gsb.t[:], op=ALU.mult),
                 reads=[tmp_s, gsb], writes=[mix])
            S.capture = sec["out"]
            tm = b3.t[:, :].bitcast(BF16).rearrange("p (j t) -> p j t", j=8)
            for k in range(KC):
                S.op("pe", lambda e, k=k: e.transpose(tm[:, k, :], mix.t[:, k * 128:(k + 1) * 128], ident_bf.t[:]),
                     reads=[mix, ident_bf], writes=[b3], inc=(k == KC - 1))
            S.op("act", lambda e: e.activation(out=mixT.t[:], in_=tm, func=AF.Copy), reads=[b3], writes=[mixT])
            for n in range(2):
                bank = (b0, b1)[n]
                for k in range(KC):
                    S.op("pe", lambda e, k=k, n=n, bank=bank: e.matmul(bank.t[:, :], lhsT=mixT.t[:, k, :], rhs=wout.t[:, k, n * 512:(n + 1) * 512],
                                                                       start=(k == 0), stop=(k == KC - 1)),
                         reads=[mixT, wout], writes=[bank], inc=(k % 4 == 3))
                S.op("dve", lambda e, n=n, bank=bank: e.tensor_tensor(out=xnew.t[:, n * 512:(n + 1) * 512], in0=bank.t[:, :],
                                                                      in1=xbuf.t[:, n * 512:(n + 1) * 512], op=ALU.add),
                     reads=[bank, xbuf], writes=[xnew])
            yo = yout[t % 2]
            S.op("act", lambda e: e.activation(out=yo.t[:], in_=xnew.t[:], func=AF.Square, accum_out=ssq2.t[:, 0:1]),
                 reads=[xnew], writes=[yo, ssq2])
            S.op("act", lambda e: e.activation(out=lnvC.t[:, 0:1], in_=ssq2.t[:], func=AF.Ln, bias=eps_d.t[:, 0:1]), reads=[ssq2, eps_d], writes=[lnvC])
            S.op("act", lambda e: e.activation(out=rstd2.t[:], in_=lnvC.t[:, 0:1], func=AF.Exp, scale=-0.5), reads=[lnvC], writes=[rstd2])
            S.op("dve", lambda e: e.scalar_tensor_tensor(out=yo.t[:], in0=xnew.t[:], scalar=rstd2.t[:, 0:1], in1=gfin_bc.t[:],
                                                          op0=ALU.mult, op1=ALU.mult), reads=[xnew, rstd2, gfin_bc], writes=[yo])
            S.dma("sp", f"o{t % 2}", out[t * 128:(t + 1) * 128, :], yo.t[:], reads=[yo])
            S.capture = None
            return sec

        S.off = stop < 5
        tiles = [("p", t) for t in range(NP)] + [("m", t) for t in range(NT)]
        L = list(L_setup)
        mkA = lambda i: (A_pre(tiles[i][1], i) if tiles[i][0] == "p" else A_main(tiles[i][1], i))
        secA = mkA(0) if tiles else None
        if tiles:
            for nm in ("a0", "a1", "a2", "a3"):
                L += secA[nm]
        for i, (kind, t) in enumerate(tiles):
            secB = B_pre(t, i) if kind == "p" else B_main(t, i)
            for nm in (("b0", "b1") if kind == "p" else ("gla1", "gla2", "swa1", "swa2")):
                L += secB[nm]
            if i + 1 < len(tiles):
                secA = mkA(i + 1)
                for nm in ("a0", "a1", "a2", "a3"):
                    L += secA[nm]
            if kind == "m":
                L += secB["out"]
        if not S.off:
            S.schedule(L)
        S.off = False
        S.final_wait("sp", ["o0", "o1", "ld0", "stg0", "stg1", "stg2", "stg3"])

        S.ops = {e: [o for o in S.ops[e] if o is not None] for e in S.ENG}
        keys = S.sem_keys()
        sems = {k: es.enter_context(nc.semaphore(f"s_{k}")) for k in keys}
        block = es.enter_context(nc.Block())

        def emit(eng_name):
            def body(e):
                for (waits, fn, inc) in S.ops[eng_name]:
                    for (s, v) in waits:
                        e.wait_ge(sems[s], v)
                    if fn is not None:
                        ins = fn(e)
                        if inc is not None:
                            ins.then_inc(sems[inc[0]], inc[1])
            return body

        block.tensor(emit("pe"))
        block.scalar(emit("act"))
        block.vector(emit("dve"))
        block.gpsimd(emit("pool"))
        block.sync(emit("sp"))
    return nc


def _consts():
    j = np.arange(128)[:, None]
    i = np.arange(128)[None, :]
    c = {}
    c["c_ident"] = np.eye(128, dtype=np.float32)
    c["c_tricum"] = np.where(j <= i, -1.0 / 16.0, 0.0).astype(np.float32)
    c["c_trisuf"] = np.where(j > i, -1.0 / 16.0, 0.0).astype(np.float32)
    c["c_mgla"] = np.where(j <= i, 1.0, 0.0).astype(np.float32)
    cur = np.where(j <= i, 0.0, NEG).astype(np.float32)
    prev = np.where(j > i, 0.0, NEG).astype(np.float32)
    c["c_mbcur"] = np.tile(cur, (1, 4))
    c["c_mbprev"] = np.tile(prev, (1, 4))
    inv_freq = (1.0 / (10000.0 ** (np.arange(0, 64, 2, dtype=np.float64) / 64.0))).astype(np.float32)
    c["invf"] = np.tile(inv_freq[None, :], (128, 1)).astype(np.float32)
    return c


_NC_CACHE = {}


def _col(v):
    return np.ascontiguousarray(v.reshape(-1, 128).T).astype(np.float32)


def make_in_maps(x, c, positions, w_ada, b_ada, g_norm, w_in, w_decay, b_decay, g_gla_head, sinks, w_out, g_final,
                 cfgs, NT, NP):
    cs = _consts()
    w_in_p = np.ascontiguousarray(w_in[0][:, PERM])
    wdec = np.concatenate([w_decay[0], b_decay[0][None, :]], axis=0).astype(np.float32)
    gmix = np.concatenate([_col(g_gla_head[0]), np.ones((128, 4), np.float32)], axis=1)
    maps = []
    for (b, s0, hasp) in cfgs:
        m = dict(cs)
        m["xm"] = np.ascontiguousarray(x[b, s0:s0 + NT * 128])
        npre = max(NP, 1) * 128
        if hasp:
            m["xp"] = np.ascontiguousarray(x[b, s0 - NP * 128:s0]) if NP > 0 else np.zeros((128, D), np.float32)
        else:
            m["xp"] = np.ascontiguousarray(x[b, 0:npre])
        pm = np.zeros((128, NT + 1), np.int32)
        pm[:, :NT] = positions[b, s0:s0 + NT * 128].reshape(NT, 128).T
        if hasp and NP > 0:
            pm[:, NT] = positions[b, s0 - 128:s0]
        m["posm"] = pm
        m["c_col"] = _col(c[b])
        m["w_ada"] = w_ada[0]
        m["b_ada"] = b_ada[0][None, :]
        m["gnorm_col"] = _col(g_norm[0])
        m["w_in"] = w_in_p
        m["wdec"] = wdec
        m["gmix_col"] = gmix
        m["sinks"] = sinks[0][None, :]
        m["w_out"] = w_out[0]
        m["g_final"] = g_final[None, :]
        m["flag_col"] = np.full((128, 1), 1.0 if hasp else 0.0, np.float32)
        m["c_mbprev0"] = cs["c_mbprev"] if hasp else np.full((128, 512), NEG, np.float32)
        maps.append({k: np.ascontiguousarray(v) for k, v in m.items()})
    return maps


def kernel(x, c, positions, w_ada, b_ada, g_norm, w_in, w_decay, b_decay, g_gla_head, sinks, w_out, g_final):
    args = [np.asarray(a) for a in (x, c, positions, w_ada, b_ada, g_norm, w_in, w_decay, b_decay, g_gla_head, sinks, w_out, g_final)]
    x = args[0]
    B, SEQ, _ = x.shape
    NT = NP = SEQ // 2 // 128
    cfgs = [(b, h * (SEQ // 2), h == 1) for b in range(B) for h in range(2)]
    key = (NT, NP)
    if key not in _NC_CACHE:
        _NC_CACHE[key] = build(NT, NP)
    nc = _NC_CACHE[key]
    maps = make_in_maps(*args, cfgs=cfgs, NT=NT, NP=NP)
    res = run_bass_kernel_spmd(nc, maps, core_ids=list(range(len(cfgs))))
    outp = np.empty((B, SEQ, D), np.float32)
    for i, (b, s0, _) in enumerate(cfgs):
        outp[b, s0:s0 + NT * 128] = res.results[i]["out"]
    return outp
```

```python
import math
import os as _os
from contextlib import ExitStack

import numpy as np
import concourse.bass as bass
import concourse.mybir as mybir
from concourse.bass_utils import run_bass_kernel_spmd

F32 = mybir.dt.float32
BF16 = mybir.dt.bfloat16
I32 = mybir.dt.int32
AF = mybir.ActivationFunctionType
ALU = mybir.AluOpType

D = 1024
KC = 8
DIN = 2832
EPS = 1e-6
NEG = -30000.0
SIN_S = 0.999999

O_GQ, O_GK, O_GV, O_GA, O_SK, O_SV, O_GZ, O_SQ, O_SZ = 0, 256, 512, 1024, 1040, 1168, 1296, 1808, 2320
PERM = np.concatenate([
    np.arange(0, 256), np.arange(256, 512), np.arange(512, 1024), np.arange(1024, 1040),
    np.arange(2064, 2192), np.arange(2192, 2320), np.arange(1040, 1552), np.arange(1552, 2064),
    np.arange(2320, 2832)])


class Buf:
    __slots__ = ("name", "t", "w", "r", "excl", "wread")

    def __init__(self, name, t, excl=False):
        self.name = name
        self.t = t
        self.w = None
        self.r = []
        self.excl = excl
        self.wread = False

    def __getitem__(self, k):
        return self.t[k]


class _Probe:
    def __init__(self):
        self.name = None
        self.args = ()
        self.kw = {}

    def __getattr__(self, name):
        def f(*args, **kw):
            self.name, self.args, self.kw = name, args, kw
            return self
        return f


def _est_cost(eng, fn):
    try:
        p = _Probe()
        fn(p)
        out = p.kw.get("out", p.args[0] if p.args else None)
        shp = list(out.shape)
        n = 1
        for d in shp[1:]:
            n *= int(d)
        if eng == "pe":
            if p.name == "transpose":
                return 0.08
            lhsT = p.kw.get("lhsT", p.args[1] if len(p.args) > 1 else None)
            f32 = lhsT is not None and lhsT.dtype == F32
            c = 0.012 + n / 1950.0
            return c * (4.5 if f32 else 1.0)
        if eng == "act":
            return 0.22 + n / 1150.0
        if eng == "dve":
            c = 0.12 + n / 950.0
            if p.name == "reciprocal":
                c = 0.12 + n / 160.0
            if p.name == "scalar_tensor_tensor":
                c = 0.15 + n / 850.0
            return c
        if eng == "pool":
            return 0.2 + n / 550.0
    except Exception:
        pass
    return None


class Sched:
    ENG = ("pe", "act", "dve", "pool", "sp")

    def __init__(self, self_sync=True):
        self.ops = {e: [] for e in self.ENG}
        self.cnt = {}
        self.seen = {e: {} for e in self.ENG}
        self.self_sync = self_sync
        self.off = False
        self.capture = None

    def _tickets(self, eng, reads, writes, xreads=()):
        tk = []
        for b in xreads:
            if b.w is not None and not (b.wread and b.w[0] == eng):
                tk.append(b.w)
            tk.extend(b.r)
        for b in reads:
            if b.excl:
                continue
            if b.w is not None:
                tk.append(b.w)
        for b in writes:
            if b.w is not None:
                tk.append(b.w)
            tk.extend(b.r)
        need = {}
        for (s, v) in tk:
            if s == eng and (eng == "pe" or not self.self_sync):
                continue
            if self.seen[eng].get(s, 0) < v:
                need[s] = max(need.get(s, 0), v)
        for s, v in need.items():
            self.seen[eng][s] = v
        return list(need.items())

    def replay(self, lst):
        cap, self.capture = self.capture, None
        for a in lst:
            if a[0] == "dma":
                self.dma(*a[1], **a[2])
            else:
                self.op(*a[:5])
        self.capture = cap

    def op(self, eng, fn, reads=(), writes=(), inc=True, cost=None):
        if self.off:
            return None
        if self.capture is not None:
            self.capture.append((eng, fn, list(reads), list(writes), inc, cost))
            return None
        xr = [b for b in reads if b.excl]
        waits = self._tickets(eng, reads, list(writes), xr)
        ticket = (eng, self.cnt.get(eng, 0) + 1)
        if inc:
            self.cnt[eng] = ticket[1]
        self.ops[eng].append((waits, fn, (eng, 1) if inc else None))
        for b in reads:
            b.r.append(ticket)
        for b in xr:
            b.w = ticket
            b.r = []
            b.wread = True
        for b in writes:
            b.w = ticket
            b.r = []
            b.wread = False
        return ticket

    def dma(self, q, semkey, out_ap, in_ap, reads=(), writes=(), **kw):
        if self.off:
            return None
        if self.capture is not None:
            self.capture.append(("dma", (q, semkey, out_ap, in_ap, list(reads), list(writes)), kw))
            return None
        waits = self._tickets(q, reads, writes)
        self.cnt[semkey] = self.cnt.get(semkey, 0) + 16
        ticket = (semkey, self.cnt[semkey])
        self.ops[q].append((waits, lambda e: e.dma_start(out=out_ap, in_=in_ap, **kw), (semkey, 16)))
        for b in reads:
            b.r.append(ticket)
        for b in writes:
            b.w = ticket
            b.r = []
        return ticket

    COST = {"pe": 0.2, "act": 0.6, "dve": 0.55, "pool": 0.9, "sp": 0.1}

    def schedule(self, L):
        units = []
        curu = None
        for a in L:
            if a[0] == "dma":
                q, semkey, out_ap, in_ap, reads, writes = a[1]
                nb = 4
                for d in out_ap.shape:
                    nb *= int(d)
                units.append({"eng": q, "ops": [a], "reads": list(reads), "writes": list(writes), "cost": 0.1, "lat": 2.5 if nb < 600000 else 2.0 + nb / 180e3})
                continue
            eng, fn, reads, writes, inc = a[:5]
            cost = a[5] if len(a) > 5 and a[5] is not None else None
            if cost is None:
                cost = _est_cost(eng, fn)
            if cost is None:
                cost = self.COST[eng]
            if eng == "pe":
                if curu is None:
                    curu = {"eng": "pe", "ops": [], "reads": [], "writes": [], "cost": 0.0, "lat": 0.3}
                curu["ops"].append(a)
                curu["reads"] += list(reads)
                curu["writes"] += list(writes)
                curu["cost"] += cost
                if inc:
                    units.append(curu)
                    curu = None
            else:
                units.append({"eng": eng, "ops": [a], "reads": list(reads), "writes": list(writes), "cost": cost, "lat": 0.15})
        assert curu is None
        n = len(units)
        lastw, readers = {}, {}
        deps = [set() for _ in range(n)]
        for j, u in enumerate(units):
            rs = [b for b in u["reads"] if not b.excl]
            ws = list(u["writes"]) + [b for b in u["reads"] if b.excl]
            for b in rs:
                if id(b) in lastw:
                    deps[j].add(lastw[id(b)])
            for b in ws:
                if id(b) in lastw:
                    deps[j].add(lastw[id(b)])
                for r in readers.get(id(b), ()):
                    deps[j].add(r)
            for b in rs:
                readers.setdefault(id(b), []).append(j)
            for b in ws:
                lastw[id(b)] = j
                readers[id(b)] = []
            deps[j].discard(j)
        succ = [[] for _ in range(n)]
        ndep = [len(d) for d in deps]
        for j, d in enumerate(deps):
            for i in d:
                succ[i].append(j)
        ready = {e: [] for e in self.ENG}
        depready = [0.0] * n
        done = [0.0] * n
        efree = {e: 0.0 for e in self.ENG}
        for j in range(n):
            if ndep[j] == 0:
                ready[units[j]["eng"]].append(j)
        order = []
        WIN = 1000
        nsched = 0
        scheduled = [False] * n
        oldest = 0
        while nsched < n:
            while oldest < n and scheduled[oldest]:
                oldest += 1
            best = None
            for e in self.ENG:
                cand = None
                for j in ready[e]:
                    if j > oldest + WIN:
                        continue
                    st = max(depready[j], efree[e])
                    key = (st, j)
                    if cand is None or key < cand[0]:
                        cand = (key, j)
                if cand is not None and (best is None or cand[0] < best[0]):
                    best = cand
            if best is None:
                j = min(j for e in self.ENG for j in ready[e])
                best = ((max(depready[j], efree[units[j]["eng"]]), j), j)
            (st, _), j = best
            u = units[j]
            ready[u["eng"]].remove(j)
            scheduled[j] = True
            nsched += 1
            efree[u["eng"]] = st + u["cost"]
            done[j] = st + u["cost"] + u["lat"]
            order.append(j)
            for k in succ[j]:
                ndep[k] -= 1
                depready[k] = max(depready[k], done[j])
                if ndep[k] == 0:
                    ready[units[k]["eng"]].append(k)
        cap, self.capture = self.capture, None
        for j in order:
            for a in units[j]["ops"]:
                if a[0] == "dma":
                    self.dma(*a[1], **a[2])
                else:
                    self.op(*a[:5])
        self.capture = cap
        return max(done) if done else 0.0

    def final_wait(self, eng, keys):
        waits = [(k, self.cnt[k]) for k in keys if self.cnt.get(k, 0) > 0]
        self.ops[eng].append((waits, None, None))

    def sem_keys(self):
        keys = set(self.cnt.keys())
        for e in self.ENG:
            keys.add(e)
        return sorted(keys)


def build(NT=32, NP=32, self_sync=True, stop=99):
    nc = bass.Bass("TRN2", target_bir_lowering=False)
    S = Sched(self_sync=self_sync)

    def din(name, shape, dt=F32):
        return nc.dram_tensor(name, list(shape), dt, kind="ExternalInput").ap()

    xm = din("xm", [NT * 128, D])
    xp = din("xp", [max(NP, 1) * 128, D])
    posm = din("posm", [128, NT + 1], I32)
    c_col = din("c_col", [128, 8])
    w_ada = din("w_ada", [D, 3 * D])
    b_ada = din("b_ada", [1, 3 * D])
    gnorm_col = din("gnorm_col", [128, 8])
    w_in = din("w_in", [D, DIN])
    wdec = din("wdec", [17, 256])
    gmix_col = din("gmix_col", [128, 8])
    sinks = din("sinks", [1, 8])
    w_out = din("w_out", [D, D])
    g_final = din("g_final", [1, D])
    flag_col = din("flag_col", [128, 1])
    invf = din("invf", [128, 32])
    c_ident = din("c_ident", [128, 128])
    c_tricum = din("c_tricum", [128, 128])
    c_trisuf = din("c_trisuf", [128, 128])
    c_mgla = din("c_mgla", [128, 128])
    c_mbcur = din("c_mbcur", [128, 512])
    c_mbprev = din("c_mbprev", [128, 512])
    c_mbprev0 = din("c_mbprev0", [128, 512])
    out = nc.dram_tensor("out", [NT * 128, D], F32, kind="ExternalOutput").ap()

    with ExitStack() as es:
        def sb(name, shape, dt=F32):
            return Buf(name, es.enter_context(nc.sbuf_tensor(name, list(shape), dt)))

        def ps(name, shape, dt=F32):
            return Buf(name, es.enter_context(nc.psum_tensor(name, list(shape), dt)), excl=True)

        bk = [ps(f"bk{i}", [128, 512]) for i in range(8) if i != 2]
        b0, b1, b3, b4, b5, b6, b7 = bk
        ptb = ps("ptb", [128, 8, 128], BF16)

        NSTG = 3
        stage = [sb(f"stage{i}", [128, 2 * D]) for i in range(NSTG)]
        win = sb("win", [128, KC, DIN], BF16)
        wout = sb("wout", [128, KC, D], BF16)
        ones_f = sb("ones_f", [1, 128])
        negcol = sb("negcol", [128, 1])
        ident_f = sb("ident_f", [128, 128])
        ident_bf = sb("ident_bf", [128, 128], BF16)
        tricum = sb("tricum", [128, 128])
        trisuf = sb("trisuf", [128, 128])
        mgla = sb("mgla", [128, 128])
        mb = sb("mb", [128, 3, 512], BF16)
        posi = sb("posi", [128, NT + 1], I32)
        posf = sb("posf", [128, NT + 1])
        invf_sb = sb("invf_sb", [128, 32])
        ang = sb("ang", [128, NT + 1, 32])
        angm = sb("angm", [128, NT + 1, 32])
        cosT = sb("cosT", [128, NT + 1, 32])
        sinS = sb("sinS", [128, NT + 1, 2, 32])
        sink_bc = sb("sink_bc", [128, 8])
        esink = sb("esink", [128, 8])
        gfin_bc = sb("gfin_bc", [128, D])
        wdec_sb = sb("wdec_sb", [17, 256])
        flag_sb = sb("flag_sb", [128, 1])
        ccol = sb("ccol", [128, 8])
        ecol = sb("ecol", [128, 8])
        siluc = sb("siluc", [128, 8])
        gncol = sb("gncol", [128, 8])
        gmcol = sb("gmcol", [128, 8])
        gscol = sb("gscol", [128, 8])
        shcol = sb("shcol", [128, 8])
        modrow = sb("modrow", [1, 3 * D])
        St = sb("St", [128, 2, 128])
        St_bf = sb("St_bf", [128, 2, 128], BF16)
        decay = sb("decay", [128, 2])
        gaT = sb("gaT", [17, 128], BF16)
        wdec_bf = sb("wdec_bf", [17, 256], BF16)

        xin = [sb(f"xin{i}", [128, D]) for i in range(3)]
        ssq = sb("ssq", [128, 1])
        rstd = sb("rstd", [128, 1])
        hb = sb("hb", [128, D], BF16)
        hTk = [sb(f"hT{k}", [128, 128], BF16) for k in range(KC)]
        QK = [sb(f"qk_sb{i}", [128, 512], BF16) for i in range(2)]
        VV = [sb(f"v_sb{i}", [128, 512], BF16) for i in range(2)]
        GA = [sb(f"ga_sb{i}", [128, 16], BF16) for i in range(2)]
        GG = [sb(f"gate_g{i}", [128, 512], BF16) for i in range(2)]
        GS = [sb(f"gate_s{i}", [128, 512], BF16) for i in range(2)]
        e_t = sb("e_t", [128, 512])
        lnvA = sb("lnvA", [128, 1])
        lnvB = sb("lnvB", [128, 4])
        lnvC = sb("lnvC", [128, 1])
        tmpc = sb("tmpc", [128, 512])
        tmps = sb("tmps", [128, 512])
        QR = [sb(f"q_r{i}", [128, 512], BF16) for i in range(2)]
        tmpck = sb("tmpck", [128, 128])
        tmpsk = sb("tmpsk", [128, 128])
        KR = [sb(f"k_r{i}", [128, 128], BF16) for i in range(2)]
        kT = [sb(f"kT{i}", [128, 2, 128], BF16) for i in range(2)]
        vaug = [sb(f"vaug{i}", [128, 2, 65], BF16) for i in range(3)]
        qT = sb("qT", [128, 512], BF16)
        e_z = sb("e_z", [128, 256])
        Lz = sb("Lz", [128, 256])
        Eplus = sb("Eplus", [128, 2, 128])
        Eminus = sb("Eminus", [128, 2, 128])
        Esuf = sb("Esuf", [128, 256])
        qkT_sb = sb("qkT_sb", [128, 4, 128], BF16)
        nln8 = sb("nln8", [128, 1])
        q_dT = sb("q_dT", [128, 4, 128], BF16)
        k_dT = sb("k_dT", [128, 2, 128], BF16)
        k_tail = sb("k_tail", [128, 256], BF16)
        AT_sb = sb("AT_sb", [128, 512], BF16)
        junk2 = sb("junk2", [128, 128], BF16)
        ssq_g = sb("ssq_g", [128, 4])
        rstd_g = sb("rstd_g", [128, 4])
        tmp_g = sb("tmp_g", [128, 512])
        mix = sb("mix", [128, D], BF16)
        mixT = sb("mixT", [128, KC, 128], BF16)
        PT = [sb(f"PT{g}", [128, 2, 512], BF16) for g in range(2)]
        den = sb("den", [128, 8])
        rden = sb("rden", [128, 8])
        tmp_s = tmp_g
        xnew = sb("xnew", [128, D])
        gate_bc = xnew
        ssq2 = sb("ssq2", [128, 1])
        rstd2 = sb("rstd2", [128, 1])
        yout = [sb(f"yout{i}", [128, D]) for i in range(2)]

        def ld(dst, src, key="ld0", q="sp", **kw):
            S.dma(q, key, dst.t[:] if not isinstance(dst, tuple) else dst[1], src, writes=[dst if not isinstance(dst, tuple) else dst[0]], **kw)

        small = [
            (ccol, c_col), (gncol, gnorm_col), (gmcol, gmix_col), (flag_sb, flag_col), (invf_sb, invf),
            (ident_f, c_ident), (tricum, c_tricum), (trisuf, c_trisuf), (mgla, c_mgla), (posi, posm),
            (wdec_sb, wdec), (modrow, b_ada),
        ]
        for dst, src in small:
            S.dma("sp", "ld0", dst.t[:], src, writes=[dst])
        mbstage = stage[0]
        mbv = stage[0].t[:, 0:1536].rearrange("p (a n) -> p a n", a=3)
        S.dma("sp", "ld0", mbv[:, 0, :], c_mbcur, writes=[mbstage])
        S.dma("sp", "ld0", mbv[:, 1, :], c_mbprev, writes=[mbstage])
        S.dma("sp", "ld0", mbv[:, 2, :], c_mbprev0, writes=[mbstage])
        S.dma("sp", "ld0", sink_bc.t[:], sinks.partition_broadcast(128), writes=[sink_bc])
        S.dma("sp", "ld0", gfin_bc.t[:], g_final.partition_broadcast(128), writes=[gfin_bc])
        fin = ("ld0", S.cnt["ld0"])
        for b in [d for d, _ in small] + [mbstage, sink_bc, gfin_bc]:
            b.w = fin

        eps_d = sb("eps_d", [128, 1])
        eps_h = sb("eps_h", [128, 1])
        S.op("pool", lambda e: e.memset(nln8.t[:], -math.log(8.0)), writes=[nln8])
        S.op("pool", lambda e: e.memset(eps_d.t[:], D * EPS), writes=[eps_d])
        S.op("pool", lambda e: e.memset(eps_h.t[:], 128.0 * EPS), writes=[eps_h])
        S.op("pool", lambda e: e.memset(ones_f.t[:], 1.0), writes=[ones_f])
        S.op("pool", lambda e: e.memset(negcol.t[:], -1.0 / 16.0), writes=[negcol])
        S.op("pool", lambda e: e.memset(gaT.t[:], 1.0), writes=[gaT])
        S.op("pool", lambda e: e.memset(St.t[:], 0.0), writes=[St])
        S.op("pool", lambda e: e.memset(St_bf.t[:], 0.0), writes=[St_bf])
        S.op("pool", lambda e: e.memset(q_dT.t[:], 0.0), writes=[q_dT])
        for i in range(3):
            S.op("pool", lambda e, i=i: e.memset(vaug[i].t[:], 1.0), writes=[vaug[i]])
        for i in range(2):
            S.op("pool", lambda e, i=i: e.memset(kT[i].t[:], 0.0), writes=[kT[i]])
        S.op("dve", lambda e: e.tensor_copy(out=ident_bf.t[:], in_=ident_f.t[:]), reads=[ident_f], writes=[ident_bf])
        S.op("dve", lambda e: e.tensor_copy(out=wdec_bf.t[:], in_=wdec_sb.t[:]), reads=[wdec_sb], writes=[wdec_bf])
        S.op("dve", lambda e: e.tensor_copy(out=mb.t[:], in_=mbv), reads=[mbstage], writes=[mb])
        S.op("dve", lambda e: e.tensor_single_scalar(out=gfin_bc.t[:], in_=gfin_bc.t[:], scalar=32.0, op=ALU.mult),
             reads=[gfin_bc], writes=[gfin_bc])
        S.op("dve", lambda e: e.tensor_single_scalar(out=gmcol.t[:, 0:4], in_=gmcol.t[:, 0:4], scalar=math.sqrt(128.0), op=ALU.mult),
             reads=[gmcol], writes=[gmcol])

        S.off = stop < 1
        L_setup = []
        S.capture = L_setup
        S.op("dve", lambda e: e.tensor_copy(out=posf.t[:], in_=posi.t[:]), reads=[posi], writes=[posf])
        S.op("dve", lambda e: e.tensor_tensor(
            out=ang.t[:], in0=posf.t[:].unsqueeze(2).broadcast_to([128, NT + 1, 32]),
            in1=invf_sb.t[:].unsqueeze(1).broadcast_to([128, NT + 1, 32]), op=ALU.mult),
            reads=[posf, invf_sb], writes=[ang])
        C1 = 6.28125
        C2 = 2.0 * math.pi - 6.28125
        angi = sb("angi", [128, NT + 1, 32], I32)
        S.op("dve", lambda e: e.tensor_single_scalar(out=angm.t[:], in_=ang.t[:], scalar=1.0 / (2.0 * math.pi), op=ALU.mult),
             reads=[ang], writes=[angm])
        S.op("dve", lambda e: e.tensor_copy(out=angi.t[:], in_=angm.t[:]), reads=[angm], writes=[angi])
        S.op("dve", lambda e: e.tensor_copy(out=angm.t[:], in_=angi.t[:]), reads=[angi], writes=[angm])
        S.op("dve", lambda e: e.scalar_tensor_tensor(out=ang.t[:], in0=angm.t[:], scalar=-C1, in1=ang.t[:], op0=ALU.mult, op1=ALU.add),
             reads=[angm, ang], writes=[ang])
        S.op("dve", lambda e: e.scalar_tensor_tensor(out=ang.t[:], in0=angm.t[:], scalar=-C2, in1=ang.t[:], op0=ALU.mult, op1=ALU.add),
             reads=[angm, ang], writes=[ang])
        SC = [-1.0 / 6, 1.0 / 120, -1.0 / 5040, 1.0 / 362880, -1.0 / 39916800]
        CC = [-1.0 / 2, 1.0 / 24, -1.0 / 720, 1.0 / 40320, -1.0 / 3628800, 1.0 / 479001600]
        s_v = sinS.t[:, :, 1, :]
        t_v = sinS.t[:, :, 0, :]
        S.op("dve", lambda e: e.tensor_single_scalar(out=ang.t[:], in_=ang.t[:], scalar=0.5, op=ALU.mult), reads=[ang], writes=[ang])
        S.op("dve", lambda e: e.tensor_tensor(out=angm.t[:], in0=ang.t[:], in1=ang.t[:], op=ALU.mult), reads=[ang], writes=[angm])

        def horner(dst, coefs, dbuf):
            S.op("dve", lambda e: e.tensor_single_scalar(out=dst, in_=angm.t[:], scalar=coefs[-1], op=ALU.mult), reads=[angm], writes=[dbuf])
            for a in reversed(coefs[:-1]):
                S.op("dve", lambda e, a=a: e.scalar_tensor_tensor(out=dst, in0=dst, scalar=a, in1=angm.t[:], op0=ALU.add, op1=ALU.mult),
                     reads=[dbuf, angm], writes=[dbuf])

        horner(cosT.t[:], SC, cosT)
        S.op("dve", lambda e: e.scalar_tensor_tensor(out=s_v, in0=cosT.t[:], scalar=1.0, in1=ang.t[:], op0=ALU.add, op1=ALU.mult),
             reads=[cosT, ang], writes=[sinS])
        horner(cosT.t[:], CC, cosT)
        S.op("dve", lambda e: e.tensor_single_scalar(out=cosT.t[:], in_=cosT.t[:], scalar=1.0, op=ALU.add), reads=[cosT], writes=[cosT])
        S.op("dve", lambda e: e.scalar_tensor_tensor(out=ang.t[:], in0=s_v, scalar=2.0, in1=cosT.t[:], op0=ALU.mult, op1=ALU.mult),
             reads=[sinS, cosT], writes=[ang])
        S.op("dve", lambda e: e.tensor_tensor(out=angm.t[:], in0=s_v, in1=s_v, op=ALU.mult), reads=[sinS], writes=[angm])
        S.op("dve", lambda e: e.tensor_scalar(out=cosT.t[:], in0=angm.t[:], scalar1=-2.0, scalar2=1.0, op0=ALU.mult, op1=ALU.add),
             reads=[angm], writes=[cosT])
        S.op("dve", lambda e: e.tensor_copy(out=s_v, in_=ang.t[:]), reads=[ang], writes=[sinS])
        S.op("dve", lambda e: e.tensor_single_scalar(out=t_v, in_=ang.t[:], scalar=-1.0, op=ALU.mult), reads=[ang], writes=[sinS])
        S.op("act", lambda e: e.activation(out=esink.t[:], in_=sink_bc.t[:], func=AF.Exp), reads=[sink_bc], writes=[esink])
        S.op("act", lambda e: e.activation(out=ecol.t[:], in_=ccol.t[:], func=AF.Exp, scale=-1.0), reads=[ccol], writes=[ecol])
        S.op("dve", lambda e: e.tensor_single_scalar(out=ecol.t[:], in_=ecol.t[:], scalar=1.0, op=ALU.add), reads=[ecol], writes=[ecol])
        S.op("dve", lambda e: e.reciprocal(out=ecol.t[:], in_=ecol.t[:]), reads=[ecol], writes=[ecol])
        S.op("dve", lambda e: e.tensor_tensor(out=siluc.t[:], in0=ccol.t[:], in1=ecol.t[:], op=ALU.mult),
             reads=[ccol, ecol], writes=[siluc])

        S.off = stop < 2
        WQ = "sp"
        modbanks = [b0, b1, b3, b4, b5, b6]
        stg_i = [0]

        def mod_phase(c0, ngrp, gbase):
            for k in range(KC):
                st = stage[stg_i[0] % NSTG]
                S.dma(WQ, f"stg{stg_i[0] % NSTG}", st.t[:, 0:ngrp * 512], w_ada[k * 128:(k + 1) * 128, c0:c0 + ngrp * 512], writes=[st])
                stg_i[0] += 1
                for g in range(ngrp):
                    S.op("pe", lambda e, k=k, g=g, st=st: e.matmul(
                        modbanks[gbase + g].t[0:1, :], lhsT=siluc.t[:, k:k + 1], rhs=st.t[:, g * 512:(g + 1) * 512],
                        start=(k == 0), stop=(k == KC - 1)), reads=[siluc, st], writes=[modbanks[gbase + g]], inc=(g == ngrp - 1))
            for g in range(ngrp):
                gg = gbase + g
                S.op("dve", lambda e, g=g, gg=gg: e.tensor_tensor(out=modrow.t[0:1, gg * 512:(gg + 1) * 512], in0=modbanks[gg].t[0:1, :],
                                                                  in1=modrow.t[0:1, gg * 512:(gg + 1) * 512], op=ALU.add),
                     reads=[modbanks[gg], modrow], writes=[modrow])

        mod_phase(0, 4, 0)
        for j in range(16):
            src = (0 if j < 8 else D) + (j % 8) * 128
            S.op("pe", lambda e, j=j, src=src: e.matmul(b7.t[:, j:j + 1], lhsT=modrow.t[0:1, src:src + 128], rhs=ones_f.t[0:1, 0:1],
                                                        start=True, stop=True), reads=[modrow, ones_f], writes=[b7], inc=(j == 15))
        S.op("dve", lambda e: e.tensor_copy(out=shcol.t[:], in_=b7.t[:, 0:8]), reads=[b7], writes=[shcol])
        S.op("dve", lambda e: e.scalar_tensor_tensor(out=gscol.t[:], in0=b7.t[:, 8:16], scalar=1.0, in1=gncol.t[:],
                                                     op0=ALU.add, op1=ALU.mult), reads=[b7, gncol], writes=[gscol])
        S.op("dve", lambda e: e.tensor_single_scalar(out=gscol.t[:], in_=gscol.t[:], scalar=32.0, op=ALU.mult),
             reads=[gscol], writes=[gscol])

        S.off = stop < 3
        hlf = DIN // 2
        for k in range(KC):
            for hh in range(2):
                st = stage[stg_i[0] % NSTG]
                S.dma(WQ, f"stg{stg_i[0] % NSTG}", st.t[:, 0:hlf], w_in[k * 128:(k + 1) * 128, hh * hlf:(hh + 1) * hlf], writes=[st])
                stg_i[0] += 1
                if hh == 0:
                    S.op("act", lambda e, k=k, st=st: e.activation(out=win.t[:, k, 0:hlf], in_=st.t[:, 0:hlf], func=AF.Copy),
                         reads=[st], writes=[win])
                else:
                    S.op("dve", lambda e, k=k, st=st: e.tensor_copy(out=win.t[:, k, hlf:DIN], in_=st.t[:, 0:hlf]), reads=[st], writes=[win])
        S.off = stop < 4
        mod_phase(2 * D, 2, 4)
        for g in range(2):
            bb = (b3, b4)[g]
            S.op("pe", lambda e, g=g, bb=bb: e.matmul(bb.t[:, :], lhsT=ones_f.t[0:1, :], rhs=modrow.t[0:1, 2 * D + g * 512:2 * D + (g + 1) * 512],
                                                      start=True, stop=True), reads=[ones_f, modrow], writes=[bb])
            S.op("act", lambda e, g=g, bb=bb: e.activation(out=gate_bc.t[:, g * 512:(g + 1) * 512], in_=bb.t[:, :], func=AF.Copy),
                 reads=[bb], writes=[gate_bc])
        for k in range(KC):
            st = stage[stg_i[0] % NSTG]
            S.dma(WQ, f"stg{stg_i[0] % NSTG}", st.t[:, 0:D], w_out[k * 128:(k + 1) * 128, :], writes=[st])
            stg_i[0] += 1
            eng = "dve"
            S.op(eng, lambda e, k=k, st=st: e.scalar_tensor_tensor(out=wout.t[:, k, :], in0=st.t[:, 0:D], scalar=gmcol.t[:, k:k + 1],
                                                                   in1=gate_bc.t[:], op0=ALU.mult, op1=ALU.mult),
                 reads=[st, gmcol, gate_bc], writes=[wout])

        S.capture = None
        H = lambda i: i % 2
        HX = lambda i: i % 3
        ACT_EVAC = 0

        def front(x_ap, slot, xbuf):
            S.dma("sp", f"x{slot}", xbuf.t[:], x_ap, writes=[xbuf])
            S.op("act", lambda e: e.activation(out=hb.t[:], in_=xbuf.t[:], func=AF.Square, accum_out=ssq.t[:, 0:1]),
                 reads=[xbuf], writes=[hb, ssq])
            S.op("act", lambda e: e.activation(out=lnvA.t[:, 0:1], in_=ssq.t[:], func=AF.Ln, bias=eps_d.t[:, 0:1]), reads=[ssq, eps_d], writes=[lnvA])
            S.op("act", lambda e: e.activation(out=rstd.t[:], in_=lnvA.t[:, 0:1], func=AF.Exp, scale=-0.5), reads=[lnvA], writes=[rstd])
            S.op("dve", lambda e: e.tensor_scalar(out=hb.t[:], in0=xbuf.t[:], scalar1=rstd.t[:, 0:1], scalar2=None, op0=ALU.mult),
                 reads=[xbuf, rstd], writes=[hb])
            for k in range(KC):
                S.op("pe", lambda e, k=k: e.transpose(ptb.t[:, k, :], hb.t[:, k * 128:(k + 1) * 128], ident_bf.t[:]),
                     reads=[hb, ident_bf], writes=[ptb], inc=(k == KC - 1))
            for k in range(KC):
                if k < ACT_EVAC:
                    S.op("act", lambda e, k=k: e.activation(out=hTk[k].t[:], in_=ptb.t[:, k, :], func=AF.Identity,
                                                            scale=gscol.t[:, k:k + 1], bias=shcol.t[:, k:k + 1]),
                         reads=[ptb, gscol, shcol], writes=[hTk[k]])
                else:
                    S.op("dve", lambda e, k=k: e.tensor_scalar(out=hTk[k].t[:], in0=ptb.t[:, k, :], scalar1=gscol.t[:, k:k + 1],
                                                               scalar2=shcol.t[:, k:k + 1], op0=ALU.mult, op1=ALU.add),
                         reads=[ptb, gscol, shcol], writes=[hTk[k]])

        def proj(bank, o, w):
            for k in range(KC):
                S.op("pe", lambda e, k=k: e.matmul(bank.t[:, 0:w], lhsT=hTk[k].t[:], rhs=win.t[:, k, o:o + w], start=(k == 0), stop=(k == KC - 1)),
                     reads=[hTk[k], win], writes=[bank], inc=(k % 3 == 2 or k == KC - 1))

        def rope_k(src_bank, col0, tcol, krb, vab):
            skv = src_bank.t[:, col0:col0 + 128].rearrange("p (g a f) -> p g a f", g=2, a=2)
            cosb = cosT.t[:, tcol, :].unsqueeze(1).unsqueeze(1).broadcast_to([128, 2, 2, 32])
            S.op("dve", lambda e: e.tensor_tensor(out=tmpck.t[:].rearrange("p (g a f) -> p g a f", g=2, a=2), in0=skv, in1=cosb, op=ALU.mult),
                 reads=[src_bank, cosT], writes=[tmpck])
            for a in range(2):
                S.op("dve", lambda e, a=a: e.tensor_tensor(
                    out=tmpsk.t[:].rearrange("p (g a f) -> p g a f", g=2, a=2)[:, :, a, :], in0=skv[:, :, 1 - a, :],
                    in1=sinS.t[:, tcol, a, :].unsqueeze(1).broadcast_to([128, 2, 32]), op=ALU.mult),
                    reads=[src_bank, sinS], writes=[tmpsk])
            S.op("pool", lambda e: e.tensor_tensor(out=krb.t[:], in0=tmpck.t[:], in1=tmpsk.t[:], op=ALU.add),
                 reads=[tmpck, tmpsk], writes=[krb])
            S.op("dve", lambda e: e.tensor_copy(out=vab.t[:, :, 0:64],
                                                in_=src_bank.t[:, col0 + 128:col0 + 256].rearrange("p (g f) -> p g f", g=2)),
                 reads=[src_bank], writes=[vab])

        def gate(bank, gbuf):
            S.op("act", lambda e: e.activation(out=e_t.t[:], in_=bank.t[:, :], func=AF.Exp, scale=-1.0), reads=[bank], writes=[e_t])
            S.op("act", lambda e: e.activation(out=e_t.t[:], in_=e_t.t[:], func=AF.Ln, bias=1.0), reads=[e_t], writes=[e_t])
            S.op("act", lambda e: e.activation(out=e_t.t[:], in_=e_t.t[:], func=AF.Exp, scale=-1.0), reads=[e_t], writes=[e_t])
            S.op("dve", lambda e: e.tensor_tensor(out=gbuf.t[:], in0=bank.t[:, :], in1=e_t.t[:], op=ALU.mult),
                 reads=[bank, e_t], writes=[gbuf])

        def gla_decay_common(gab):
            gT_ps = b4.t[:, 256:384].bitcast(BF16)[0:16, 0:128]
            S.op("pe", lambda e: e.transpose(gT_ps, gab.t[:, 0:16], ident_bf.t[:]), reads=[gab, ident_bf], writes=[b4])
            S.op("dve", lambda e: e.tensor_copy(out=gaT.t[0:16, :], in_=gT_ps), reads=[b4], writes=[gaT])
            S.op("pe", lambda e: e.matmul(b3.t[:, 0:256], lhsT=gaT.t[0:17, :], rhs=wdec_bf.t[0:17, :], start=True, stop=True),
                 reads=[gaT, wdec_bf], writes=[b3])
            S.op("act", lambda e: e.activation(out=e_z.t[:], in_=b3.t[:, 0:256], func=AF.Exp, scale=-1.0), reads=[b3], writes=[e_z])
            S.op("act", lambda e: e.activation(out=Lz.t[:], in_=e_z.t[:], func=AF.Ln, bias=1.0), reads=[e_z], writes=[Lz])

        def state_update(ub):
            for p in range(2):
                for hp in range(2):
                    r0 = 64 * hp
                    h = 2 * p + hp
                    S.op("dve", lambda e, p=p, r0=r0, h=h: e.scalar_tensor_tensor(
                        out=St.t[r0:r0 + 64, p, :], in0=St.t[r0:r0 + 64, p, :], scalar=decay.t[r0:r0 + 64, p:p + 1],
                        in1=ub.t[r0:r0 + 64, h * 128:(h + 1) * 128], op0=ALU.mult, op1=ALU.add),
                        reads=[St, decay, ub], writes=[St])

        def u_matmuls(ub, vb):
            for h in range(4):
                p = h // 2
                S.op("pe", lambda e, h=h, p=p: e.matmul(ub.t[:, h * 128:(h + 1) * 128], lhsT=k_tail.t[:, p * 128:(p + 1) * 128],
                                                        rhs=vb.t[:, h * 128:(h + 1) * 128], start=True, stop=True),
                     reads=[k_tail, vb], writes=[ub], inc=(h == 3))

        def sections(names):
            return {k: [] for k in names}

        def A_pre(t, i):
            sec = sections(("a0", "a1", "a2", "a3"))
            xbuf = xin[HX(i)]
            qk, vb, gab = QK[H(i)], VV[H(i)], GA[H(i)]
            S.capture = sec["a0"]
            front(xp[t * 128:(t + 1) * 128, :], HX(i), xbuf)
            S.capture = sec["a1"]
            proj(b0, O_GK, 512)
            S.op("act", lambda e: e.activation(out=qk.t[:, 256:512], in_=b0.t[:, 0:256], func=AF.Copy), reads=[b0], writes=[qk])
            S.op("act", lambda e: e.activation(out=vb.t[:, 0:256], in_=b0.t[:, 256:512], func=AF.Copy), reads=[b0], writes=[vb])
            S.capture = sec["a2"]
            proj(b1, O_GV + 256, 272)
            S.op("act", lambda e: e.activation(out=vb.t[:, 256:512], in_=b1.t[:, 0:256], func=AF.Copy), reads=[b1], writes=[vb])
            S.op("act", lambda e: e.activation(out=gab.t[:], in_=b1.t[:, 256:272], func=AF.Copy), reads=[b1], writes=[gab])
            if t == NP - 1:
                proj(b0, O_SK, 256)
                rope_k(b0, 0, NT, KR[H(i)], vaug[2])
            S.capture = None
            return sec

        def B_pre(t, i):
            sec = sections(("b0", "b1"))
            qk, vb, gab = QK[H(i)], VV[H(i)], GA[H(i)]
            S.capture = sec["b0"]
            if t == NP - 1:
                krb = KR[H(i)]
                S.op("pe", lambda e: e.transpose(ptb.t[:, 0, :], krb.t[:], ident_bf.t[:]), reads=[krb, ident_bf], writes=[ptb])
                for g in range(2):
                    S.op("act", lambda e, g=g: e.activation(out=kT[1].t[64 * g:64 * g + 64, g, :], in_=ptb.t[64 * g:64 * g + 64, 0, :], func=AF.Copy),
                         reads=[ptb], writes=[kT[1]])
            gla_decay_common(gab)
            S.op("pe", lambda e: e.matmul(b3.t[:, 256:512], lhsT=trisuf.t[:], rhs=Lz.t[:], start=True, stop=True),
                 reads=[trisuf, Lz], writes=[b3])
            for p in range(2):
                S.op("pe", lambda e, p=p: e.matmul(b4.t[:, p:p + 1], lhsT=Lz.t[:, p * 128:(p + 1) * 128], rhs=negcol.t[:, 0:1], start=True, stop=True),
                     reads=[Lz, negcol], writes=[b4], inc=(p == 1))
            S.op("act", lambda e: e.activation(out=Esuf.t[:], in_=b3.t[:, 256:512], func=AF.Exp), reads=[b3], writes=[Esuf])
            S.op("act", lambda e: e.activation(out=decay.t[:], in_=b4.t[:, 0:2], func=AF.Exp), reads=[b4], writes=[decay])
            S.op("pool", lambda e: e.tensor_tensor(out=k_tail.t[:], in0=qk.t[:, 256:512], in1=Esuf.t[:], op=ALU.mult),
                 reads=[qk, Esuf], writes=[k_tail])
            S.capture = sec["b1"]
            u_matmuls(b5, vb)
            state_update(b5)
            if t == NP - 1:
                S.op("dve", lambda e: e.tensor_scalar(out=St.t[:], in0=St.t[:], scalar1=flag_sb.t[:, 0:1], scalar2=None, op0=ALU.mult),
                     reads=[St, flag_sb], writes=[St])
                S.op("act", lambda e: e.activation(out=St_bf.t[:], in_=St.t[:], func=AF.Copy), reads=[St], writes=[St_bf])
            S.capture = None
            return sec

        def A_main(t, i):
            sec = sections(("a0", "a1", "a2", "a3"))
            xbuf = xin[HX(i)]
            qk, vb, gab, krb, qrb, ggb, gsb = QK[H(i)], VV[H(i)], GA[H(i)], KR[H(i)], QR[H(i)], GG[H(i)], GS[H(i)]
            S.capture = sec["a0"]
            front(xm[t * 128:(t + 1) * 128, :], HX(i), xbuf)
            S.capture = sec["a1"]
            proj(b0, O_GQ, 512)
            S.op("act", lambda e: e.activation(out=qk.t[:], in_=b0.t[:, :], func=AF.Copy), reads=[b0], writes=[qk])
            proj(b1, O_GV, 512)
            S.op("dve", lambda e: e.tensor_copy(out=vb.t[:], in_=b1.t[:, :]), reads=[b1], writes=[vb])
            S.capture = sec["a2"]
            proj(b0, O_GA, 272)
            S.op("act", lambda e: e.activation(out=gab.t[:], in_=b0.t[:, 0:16], func=AF.Copy), reads=[b0], writes=[gab])
            rope_k(b0, 16, t, krb, vaug[t % 3])
            proj(b1, O_GZ, 512)
            gate(b1, ggb)
            S.capture = sec["a3"]
            proj(b0, O_SQ, 512)
            sqv = b0.t[:, :].rearrange("p (h a f) -> p h a f", h=8, a=2)
            S.op("dve", lambda e: e.tensor_tensor(
                out=tmpc.t[:].rearrange("p (h a f) -> p h a f", h=8, a=2), in0=sqv,
                in1=cosT.t[:, t, :].unsqueeze(1).unsqueeze(1).broadcast_to([128, 8, 2, 32]), op=ALU.mult),
                reads=[b0, cosT], writes=[tmpc])
            for a in range(2):
                S.op("dve", lambda e, a=a: e.tensor_tensor(
                    out=tmps.t[:].rearrange("p (h a f) -> p h a f", h=8, a=2)[:, :, a, :], in0=sqv[:, :, 1 - a, :],
                    in1=sinS.t[:, t, a, :].unsqueeze(1).broadcast_to([128, 8, 32]), op=ALU.mult),
                    reads=[b0, sinS], writes=[tmps])
            S.op("pool", lambda e: e.tensor_tensor(out=qrb.t[:].rearrange("p (j g f) -> p g j f", j=4, g=2),
                                                   in0=tmpc.t[:].rearrange("p (g j f) -> p g j f", g=2, j=4),
                                                   in1=tmps.t[:].rearrange("p (g j f) -> p g j f", g=2, j=4), op=ALU.add),
                 reads=[tmpc, tmps], writes=[qrb])
            proj(b1, O_SZ, 512)
            gate(b1, gsb)
            S.capture = None
            return sec

        def B_main(t, i):
            sec = sections(("gla1", "swa1", "gla2", "swa2", "out"))
            xbuf = xin[HX(i)]
            qk, vb, gab, krb, qrb, ggb, gsb = QK[H(i)], VV[H(i)], GA[H(i)], KR[H(i)], QR[H(i)], GG[H(i)], GS[H(i)]
            cur, prev = t % 2, (t + 1) % 2
            vcur, vprev = vaug[t % 3], vaug[(t - 1) % 3]
            S.capture = sec["gla1"]
            gla_decay_common(gab)
            for p in range(2):
                S.op("pe", lambda e, p=p: e.matmul(b4.t[:, p * 128:(p + 1) * 128], lhsT=Lz.t[:, p * 128:(p + 1) * 128], rhs=tricum.t[:],
                                                   start=True, stop=True), reads=[Lz, tricum], writes=[b4], inc=(p == 1))
            S.op("pe", lambda e: e.matmul(b3.t[:, 256:512], lhsT=trisuf.t[:], rhs=Lz.t[:], start=True, stop=True),
                 reads=[trisuf, Lz], writes=[b3])
            b4v = b4.t[:, 0:256].rearrange("p (a t) -> p a t", a=2)
            S.op("act", lambda e: e.activation(out=Eplus.t[:], in_=b4v, func=AF.Exp, bias=nln8.t[:, 0:1]), reads=[b4, nln8], writes=[Eplus])
            S.op("act", lambda e: e.activation(out=Eminus.t[:], in_=b4v, func=AF.Exp, scale=-1.0), reads=[b4], writes=[Eminus])
            S.op("act", lambda e: e.activation(out=Esuf.t[:], in_=b3.t[:, 256:512], func=AF.Exp), reads=[b3], writes=[Esuf])
            S.op("dve", lambda e: e.reciprocal(out=decay.t[:], in_=Eminus.t[:, :, 127]), reads=[Eminus], writes=[decay])
            S.capture = sec["gla2"]
            t5 = b5.t[:, 0:256].bitcast(BF16).rearrange("p (j t) -> p j t", j=4)
            for j in range(4):
                S.op("pe", lambda e, j=j: e.transpose(t5[:, j, :], qk.t[:, j * 128:(j + 1) * 128], ident_bf.t[:]),
                     reads=[qk, ident_bf], writes=[b5], inc=(j == 3))
            S.op("act", lambda e: e.activation(out=qkT_sb.t[:], in_=t5, func=AF.Copy), reads=[b5], writes=[qkT_sb])
            for hp in range(2):
                r0 = 64 * hp
                S.op("dve", lambda e, hp=hp, r0=r0: e.tensor_tensor(
                    out=q_dT.t[r0:r0 + 64, :, :].rearrange("r (p two) t -> r p two t", two=2)[:, :, hp, :],
                    in0=qkT_sb.t[r0:r0 + 64, 0:2, :], in1=Eplus.t[r0:r0 + 64, :, :], op=ALU.mult),
                    reads=[qkT_sb, Eplus], writes=[q_dT])
            S.op("pool", lambda e: e.tensor_tensor(out=k_dT.t[:], in0=qkT_sb.t[:, 2:4, :], in1=Eminus.t[:], op=ALU.mult),
                 reads=[qkT_sb, Eminus], writes=[k_dT])
            S.op("pool", lambda e: e.tensor_tensor(out=k_tail.t[:], in0=qk.t[:, 256:512], in1=Esuf.t[:], op=ALU.mult),
                 reads=[qk, Esuf], writes=[k_tail])
            for h in range(4):
                p = h // 2
                S.op("pe", lambda e, h=h, p=p: e.matmul(b5.t[:, h * 128:(h + 1) * 128], lhsT=k_dT.t[:, p, :],
                                                        rhs=q_dT.t[:, h, :], start=True, stop=True),
                     reads=[k_dT, q_dT], writes=[b5], inc=(h == 3))
            S.op("dve", lambda e: e.tensor_tensor(out=AT_sb.t[:].rearrange("p (h i) -> p h i", h=4),
                                                  in0=b5.t[:, :].rearrange("p (h i) -> p h i", h=4),
                                                  in1=mgla.t[:].unsqueeze(1).broadcast_to([128, 4, 128]), op=ALU.mult),
                 reads=[b5, mgla], writes=[AT_sb])
            for h in range(4):
                p = h // 2
                S.op("pe", lambda e, h=h: e.matmul(b5.t[:, h * 128:(h + 1) * 128], lhsT=AT_sb.t[:, h * 128:(h + 1) * 128],
                                                   rhs=vb.t[:, h * 128:(h + 1) * 128], start=True, stop=False),
                     reads=[AT_sb, vb], writes=[b5], inc=False)
                S.op("pe", lambda e, h=h, p=p: e.matmul(b5.t[:, h * 128:(h + 1) * 128], lhsT=q_dT.t[:, h, :],
                                                        rhs=St_bf.t[:, p, :], start=False, stop=True),
                     reads=[q_dT, St_bf], writes=[b5], inc=(h == 3))
            for h in range(4):
                S.op("act", lambda e, h=h: e.activation(out=junk2.t[:], in_=b5.t[:, h * 128:(h + 1) * 128], func=AF.Square,
                                                        accum_out=ssq_g.t[:, h:h + 1]), reads=[b5], writes=[junk2, ssq_g])
            S.op("act", lambda e: e.activation(out=lnvB.t[:, 0:4], in_=ssq_g.t[:], func=AF.Ln, bias=eps_h.t[:, 0:1]), reads=[ssq_g, eps_h], writes=[lnvB])
            S.op("act", lambda e: e.activation(out=rstd_g.t[:], in_=lnvB.t[:, 0:4], func=AF.Exp, scale=-0.5), reads=[lnvB], writes=[rstd_g])
            S.op("dve", lambda e: e.tensor_tensor(out=tmp_g.t[:].rearrange("p (h v) -> p h v", h=4),
                                                  in0=b5.t[:, :].rearrange("p (h v) -> p h v", h=4),
                                                  in1=rstd_g.t[:].unsqueeze(2).broadcast_to([128, 4, 128]), op=ALU.mult),
                 reads=[b5, rstd_g], writes=[tmp_g])
            S.op("pool", lambda e: e.tensor_tensor(out=mix.t[:, 0:512], in0=tmp_g.t[:], in1=ggb.t[:], op=ALU.mult),
                 reads=[tmp_g, ggb], writes=[mix])
            u_matmuls(b5, vb)
            state_update(b5)
            S.op("act", lambda e: e.activation(out=St_bf.t[:], in_=St.t[:], func=AF.Copy), reads=[St], writes=[St_bf])
            S.capture = sec["swa1"]
            t7 = b7.t[:, :].bitcast(BF16).rearrange("p (j t) -> p j t", j=8)
            for j in range(4):
                S.op("pe", lambda e, j=j: e.transpose(t7[:, j, :], qrb.t[:, j * 128:(j + 1) * 128], ident_bf.t[:]),
                     reads=[qrb, ident_bf], writes=[b7], inc=False)
            S.op("pe", lambda e: e.transpose(t7[:, 4, :], krb.t[:], ident_bf.t[:]), reads=[krb, ident_bf], writes=[b7])
            S.op("act", lambda e: e.activation(out=qT.t[:].rearrange("p (j q) -> p j q", j=4), in_=t7[:, 0:4, :], func=AF.Copy), reads=[b7], writes=[qT])
            for g in range(2):
                S.op("dve", lambda e, g=g: e.tensor_copy(out=kT[cur].t[64 * g:64 * g + 64, g, :], in_=t7[64 * g:64 * g + 64, 4, :]),
                     reads=[b7], writes=[kT[cur]])
            mbp = 2 if t == 0 else 1
            for g in range(2):
                for which, bank, kbuf, mi in ((0, b6, kT[prev], mbp), (1, b7, kT[cur], 0)):
                    S.op("pe", lambda e, bank=bank, mi=mi: e.matmul(bank.t[:, :], lhsT=ident_bf.t[:], rhs=mb.t[:, mi, :], start=True, stop=False),
                         reads=[ident_bf, mb], writes=[bank], inc=False)
                    S.op("pe", lambda e, bank=bank, kbuf=kbuf, g=g: e.matmul(
                        bank.t[:, :], lhsT=kbuf.t[:, g, :], rhs=qT.t[:, :],
                        start=False, stop=True), reads=[kbuf, qT], writes=[bank])
                    S.op("act", lambda e, bank=bank, g=g, which=which: e.activation(out=PT[g].t[:, which, :], in_=bank.t[:, :], func=AF.Exp, scale=0.125),
                         reads=[bank], writes=[PT[g]])
            S.capture = sec["swa2"]
            for g in range(2):
                ob = (b3, b4)[g]
                for j in range(4):
                    S.op("pe", lambda e, g=g, j=j, ob=ob: e.matmul(ob.t[:, j * 65:(j + 1) * 65], lhsT=PT[g].t[:, 0, j * 128:(j + 1) * 128],
                                                                   rhs=vprev.t[:, g, :], start=True, stop=False),
                         reads=[PT[g], vprev], writes=[ob], inc=False)
                    S.op("pe", lambda e, g=g, j=j, ob=ob: e.matmul(ob.t[:, j * 65:(j + 1) * 65], lhsT=PT[g].t[:, 1, j * 128:(j + 1) * 128],
                                                                   rhs=vcur.t[:, g, :], start=False, stop=True),
                         reads=[PT[g], vcur], writes=[ob], inc=(j == 3))
            for g in range(2):
                ob = (b3, b4)[g]
                obv = ob.t[:, 0:260].rearrange("p (j f) -> p j f", j=4)
                S.op("dve", lambda e, g=g, obv=obv: e.tensor_tensor(out=den.t[:, 4 * g:4 * g + 4], in0=obv[:, :, 64], in1=esink.t[:, 4 * g:4 * g + 4], op=ALU.add),
                     reads=[ob, esink], writes=[den])
            S.op("dve", lambda e: e.reciprocal(out=rden.t[:], in_=den.t[:]), reads=[den], writes=[rden])
            for g in range(2):
                ob = (b3, b4)[g]
                obv = ob.t[:, 0:260].rearrange("p (j f) -> p j f", j=4)
                S.op("dve", lambda e, g=g, obv=obv: e.tensor_tensor(
                    out=tmp_s.t[:, 256 * g:256 * (g + 1)].rearrange("p (j f) -> p j f", j=4), in0=obv[:, :, 0:64],
                    in1=rden.t[:, 4 * g:4 * g + 4].unsqueeze(2).broadcast_to([128, 4, 64]), op=ALU.mult),
                    reads=[ob, rden], writes=[tmp_s])
            S.op("pool", lambda e: e.tensor_tensor(out=mix.t[:, 512:1024], in0=tmp_s.t[:], in1=gsb.t[:], op=ALU.mult),
                 reads=[tmp_s, gsb], writes=[mix])
            S.capture = sec["out"]
            tm = b3.t[:, :].bitcast(BF16).rearrange("p (j t) -> p j t", j=8)
            for k in range(KC):
                S.op("pe", lambda e, k=k: e.transpose(tm[:, k, :], mix.t[:, k * 128:(k + 1) * 128], ident_bf.t[:]),
                     reads=[mix, ident_bf], writes=[b3], inc=(k == KC - 1))
            S.op("dve", lambda e: e.tensor_copy(out=mixT.t[:], in_=tm), reads=[b3], writes=[mixT])
            for n in range(2):
                bank = (b0, b1)[n]
                for k in range(KC):
                    S.op("pe", lambda e, k=k, n=n, bank=bank: e.matmul(bank.t[:, :], lhsT=mixT.t[:, k, :], rhs=wout.t[:, k, n * 512:(n + 1) * 512],
                                                                       start=(k == 0), stop=(k == KC - 1)),
                         reads=[mixT, wout], writes=[bank], inc=(k % 4 == 3))
                S.op("dve", lambda e, n=n, bank=bank: e.tensor_tensor(out=xnew.t[:, n * 512:(n + 1) * 512], in0=bank.t[:, :],
                                                                      in1=xbuf.t[:, n * 512:(n + 1) * 512], op=ALU.add),
                     reads=[bank, xbuf], writes=[xnew])
            yo = yout[t % 2]
            S.op("act", lambda e: e.activation(out=yo.t[:], in_=xnew.t[:], func=AF.Square, accum_out=ssq2.t[:, 0:1]),
                 reads=[xnew], writes=[yo, ssq2])
            S.op("act", lambda e: e.activation(out=lnvC.t[:, 0:1], in_=ssq2.t[:], func=AF.Ln, bias=eps_d.t[:, 0:1]), reads=[ssq2, eps_d], writes=[lnvC])
            S.op("act", lambda e: e.activation(out=rstd2.t[:], in_=lnvC.t[:, 0:1], func=AF.Exp, scale=-0.5), reads=[lnvC], writes=[rstd2])
            S.op("dve", lambda e: e.scalar_tensor_tensor(out=yo.t[:], in0=xnew.t[:], scalar=rstd2.t[:, 0:1], in1=gfin_bc.t[:],
                                                          op0=ALU.mult, op1=ALU.mult), reads=[xnew, rstd2, gfin_bc], writes=[yo])
            S.dma("sp", f"o{t % 2}", out[t * 128:(t + 1) * 128, :], yo.t[:], reads=[yo])
            S.capture = None
            return sec

        S.off = stop < 5
        tiles = [("p", t) for t in range(NP)] + [("m", t) for t in range(NT)]
        L = list(L_setup)
        mkA = lambda i: (A_pre(tiles[i][1], i) if tiles[i][0] == "p" else A_main(tiles[i][1], i))
        secA = mkA(0) if tiles else None
        if tiles:
            for nm in ("a0", "a1", "a2", "a3"):
                L += secA[nm]
        for i, (kind, t) in enumerate(tiles):
            secB = B_pre(t, i) if kind == "p" else B_main(t, i)
            for nm in (("b0", "b1") if kind == "p" else ("gla1", "gla2", "swa1", "swa2")):
                L += secB[nm]
            if i + 1 < len(tiles):
                secA = mkA(i + 1)
                for nm in ("a0", "a1", "a2", "a3"):
                    L += secA[nm]
            if kind == "m":
                L += secB["out"]
        if not S.off:
            S.schedule(L)
        S.off = False
        S.final_wait("sp", ["o0", "o1", "ld0", "stg0", "stg1", "stg2", "stg3"])

        S.ops = {e: [o for o in S.ops[e] if o is not None] for e in S.ENG}
        keys = S.sem_keys()
        sems = {k: es.enter_context(nc.semaphore(f"s_{k}")) for k in keys}
        block = es.enter_context(nc.Block())

        def emit(eng_name):
            def body(e):
                for (waits, fn, inc) in S.ops[eng_name]:
                    for (s, v) in waits:
                        e.wait_ge(sems[s], v)
                    if fn is not None:
                        ins = fn(e)
                        if inc is not None:
                            ins.then_inc(sems[inc[0]], inc[1])
            return body

        block.tensor(emit("pe"))
        block.scalar(emit("act"))
        block.vector(emit("dve"))
        block.gpsimd(emit("pool"))
        block.sync(emit("sp"))
    return nc


def _consts():
    j = np.arange(128)[:, None]
    i = np.arange(128)[None, :]
    c = {}
    c["c_ident"] = np.eye(128, dtype=np.float32)
    c["c_tricum"] = np.where(j <= i, -1.0 / 16.0, 0.0).astype(np.float32)
    c["c_trisuf"] = np.where(j > i, -1.0 / 16.0, 0.0).astype(np.float32)
    c["c_mgla"] = np.where(j <= i, 1.0, 0.0).astype(np.float32)
    cur = np.where(j <= i, 0.0, NEG).astype(np.float32)
    prev = np.where(j > i, 0.0, NEG).astype(np.float32)
    c["c_mbcur"] = np.tile(cur, (1, 4))
    c["c_mbprev"] = np.tile(prev, (1, 4))
    inv_freq = (1.0 / (10000.0 ** (np.arange(0, 64, 2, dtype=np.float64) / 64.0))).astype(np.float32)
    c["invf"] = np.tile(inv_freq[None, :], (128, 1)).astype(np.float32)
    return c


_NC_CACHE = {}


def _col(v):
    return np.ascontiguousarray(v.reshape(-1, 128).T).astype(np.float32)


def make_in_maps(x, c, positions, w_ada, b_ada, g_norm, w_in, w_decay, b_decay, g_gla_head, sinks, w_out, g_final,
                 cfgs, NT, NP):
    cs = _consts()
    w_in_p = np.ascontiguousarray(w_in[0][:, PERM])
    wdec = np.concatenate([w_decay[0], b_decay[0][None, :]], axis=0).astype(np.float32)
    gmix = np.concatenate([_col(g_gla_head[0]), np.ones((128, 4), np.float32)], axis=1)
    maps = []
    for (b, s0, hasp) in cfgs:
        m = dict(cs)
        m["xm"] = np.ascontiguousarray(x[b, s0:s0 + NT * 128])
        npre = max(NP, 1) * 128
        if hasp:
            m["xp"] = np.ascontiguousarray(x[b, s0 - NP * 128:s0]) if NP > 0 else np.zeros((128, D), np.float32)
        else:
            m["xp"] = np.ascontiguousarray(x[b, 0:npre])
        pm = np.zeros((128, NT + 1), np.int32)
        pm[:, :NT] = positions[b, s0:s0 + NT * 128].reshape(NT, 128).T
        if hasp and NP > 0:
            pm[:, NT] = positions[b, s0 - 128:s0]
        m["posm"] = pm
        m["c_col"] = _col(c[b])
        m["w_ada"] = w_ada[0]
        m["b_ada"] = b_ada[0][None, :]
        m["gnorm_col"] = _col(g_norm[0])
        m["w_in"] = w_in_p
        m["wdec"] = wdec
        m["gmix_col"] = gmix
        m["sinks"] = sinks[0][None, :]
        m["w_out"] = w_out[0]
        m["g_final"] = g_final[None, :]
        m["flag_col"] = np.full((128, 1), 1.0 if hasp else 0.0, np.float32)
        m["c_mbprev0"] = cs["c_mbprev"] if hasp else np.full((128, 512), NEG, np.float32)
        maps.append({k: np.ascontiguousarray(v) for k, v in m.items()})
    return maps


def kernel(x, c, positions, w_ada, b_ada, g_norm, w_in, w_decay, b_decay, g_gla_head, sinks, w_out, g_final):
    args = [np.asarray(a) for a in (x, c, positions, w_ada, b_ada, g_norm, w_in, w_decay, b_decay, g_gla_head, sinks, w_out, g_final)]
    x = args[0]
    B, SEQ, _ = x.shape
    NT = NP = SEQ // 2 // 128
    cfgs = [(b, h * (SEQ // 2), h == 1) for b in range(B) for h in range(2)]
    key = (NT, NP)
    if key not in _NC_CACHE:
        _NC_CACHE[key] = build(NT, NP)
    nc = _NC_CACHE[key]
    maps = make_in_maps(*args, cfgs=cfgs, NT=NT, NP=NP)
    res = run_bass_kernel_spmd(nc, maps, core_ids=list(range(len(cfgs))))
    outp = np.empty((B, SEQ, D), np.float32)
    for i, (b, s0, _) in enumerate(cfgs):
        outp[b, s0:s0 + NT * 128] = res.results[i]["out"]
    return outp
```
